# Optimizing a Trainium2 kernel written in Bass

```python
import math
import jax
import jax.numpy as jnp
from jax import lax
import numpy as np


D_MODEL = 1024
BATCH = 2
SEQ = 8192
DEPTH = 4

N_META = 16
CHUNK = 64
PAD = CHUNK - N_META
HGRN_CHUNK = 16
EPS = 1e-6

HG_HEADS = 4
HG_DK = 64
HG_DV = 128
RET_HEADS = 4
RET_DK = 64
RET_DV = 128
ROPE_BASE = 10000.0
SSM_WIDTH = 512
SSM_GROUP = 16
SSM_GROUPS = SSM_WIDTH // SSM_GROUP
SSM_STATE = 64
GDN_HEADS = 4
GDN_DK = 128
GDN_DV = 128
CONV_K = 4
GDN_QKV = 2 * GDN_HEADS * GDN_DK + GDN_HEADS * GDN_DV
BRANCH_WIDTH = 512
N_BRANCH = 4
D_FF = -(-8 * D_MODEL // (3 * 256)) * 256

IN_SPLITS = (
    HG_HEADS * HG_DK, HG_HEADS * HG_DK, HG_HEADS * HG_DV, HG_HEADS * HG_DV,
    RET_HEADS * RET_DK, RET_HEADS * RET_DK, RET_HEADS * RET_DV, RET_HEADS * RET_DV,
    SSM_WIDTH,
    GDN_QKV, GDN_HEADS, GDN_HEADS, GDN_HEADS * GDN_DV,
    N_BRANCH * D_MODEL,
)
IN_COLS = sum(IN_SPLITS)

kernel_name = "hybrid_gated_parallel_mixers"


def _split(z, sizes):
    offsets = [int(o) for o in np.cumsum(sizes)[:-1]]
    return jnp.split(z, offsets, axis=-1)


def rmsnorm(x, g):
    xf = x.astype(jnp.float32)
    y = xf * lax.rsqrt(jnp.mean(xf * xf, axis=-1, keepdims=True) + EPS)
    return (y * g.astype(jnp.float32)).astype(x.dtype)


def head_rmsnorm(o, g):
    y = o * lax.rsqrt(jnp.mean(o * o, axis=-1, keepdims=True) + EPS)
    return y.reshape(o.shape[0], o.shape[1], -1) * g.astype(jnp.float32)


def head_groupnorm(o, g):
    mu = jnp.mean(o, axis=-1, keepdims=True)
    oc = o - mu
    y = oc * lax.rsqrt(jnp.mean(oc * oc, axis=-1, keepdims=True) + EPS)
    return y.reshape(o.shape[0], o.shape[1], -1) * g.astype(jnp.float32)


def l2norm(t):
    return t * lax.rsqrt(jnp.sum(t * t, axis=-1, keepdims=True) + EPS)


def to_chunks(t, c):
    b, l, h, d = t.shape
    return t.reshape(b, l // c, c, h, d).transpose(1, 0, 3, 2, 4)


def from_chunks(t):
    n, b, h, c, d = t.shape
    return t.transpose(1, 0, 3, 2, 4).reshape(b, n * c, h, d)


def rotary(t, pos):
    half = t.shape[-1] // 2
    inv = ROPE_BASE ** (-jnp.arange(half, dtype=jnp.float32) / half)
    ang = pos[:, None] * inv[None, :]
    cos = jnp.cos(ang)[:, None, :]
    sin = jnp.sin(ang)[:, None, :]
    t1, t2 = t[..., :half], t[..., half:]
    return jnp.concatenate([t1 * cos - t2 * sin, t1 * sin + t2 * cos], axis=-1)


def gla_chunked(q, k, v, log_f, c):
    qc, kc, vc, gc = to_chunks(q, c), to_chunks(k, c), to_chunks(v, c), to_chunks(log_f, c)
    b = jnp.cumsum(gc, axis=3)
    b_last = b[..., -1:, :]
    q_t = qc * jnp.exp(b)
    k_t = kc * jnp.exp(-b)
    causal = jnp.tril(jnp.ones((c, c), dtype=bool))
    att = jnp.where(causal, jnp.einsum('nbhtd,nbhsd->nbhts', q_t, k_t), 0.0)
    o_intra = jnp.einsum('nbhts,nbhse->nbhte', att, vc)
    k_end = kc * jnp.exp(b_last - b)
    decay_end = jnp.exp(b_last[..., 0, :])

    def step(S, xs):
        q_n, k_n, v_n, d_n = xs
        o = jnp.einsum('bhtd,bhde->bhte', q_n, S)
        S = d_n[..., None] * S + jnp.einsum('bhsd,bhse->bhde', k_n, v_n)
        return S, o

    S0 = jnp.zeros((q.shape[0], q.shape[2], q.shape[3], v.shape[3]), jnp.float32)
    _, o_inter = lax.scan(step, S0, (q_t, k_end, vc, decay_end))
    return from_chunks(o_intra + o_inter)


def retention_chunked(q, k, v, c):
    qc, kc, vc = to_chunks(q, c), to_chunks(k, c), to_chunks(v, c)
    log_gamma = jnp.log1p(-jnp.exp2(-5.0 - jnp.arange(RET_HEADS, dtype=jnp.float32)))
    idx = jnp.arange(c, dtype=jnp.float32)
    diff = idx[:, None] - idx[None, :]
    decay = jnp.where(diff >= 0, jnp.exp(log_gamma[:, None, None] * jnp.maximum(diff, 0.0)), 0.0)
    att = jnp.einsum('nbhtd,nbhsd->nbhts', qc, kc) * decay
    o_intra = jnp.einsum('nbhts,nbhse->nbhte', att, vc)
    xi = jnp.exp(log_gamma[:, None] * (idx + 1.0))[:, :, None]
    zeta = jnp.exp(log_gamma[:, None] * (c - 1.0 - idx))[:, :, None]
    gamma_c = jnp.exp(log_gamma * c)[:, None, None]

    def step(R, xs):
        q_n, k_n, v_n = xs
        o = jnp.einsum('bhtd,bhde->bhte', q_n * xi, R)
        R = gamma_c * R + jnp.einsum('bhsd,bhse->bhde', k_n * zeta, v_n)
        return R, o

    R0 = jnp.zeros((q.shape[0], RET_HEADS, RET_DK, RET_DV), jnp.float32)
    _, o_inter = lax.scan(step, R0, (qc, kc, vc))
    return from_chunks(o_intra + o_inter)


def gated_delta_chunked(q, k, v, beta, g, c):
    qc, kc, vc = to_chunks(q, c), to_chunks(k, c), to_chunks(v, c)
    bc = to_chunks(beta[..., None], c)[..., 0]
    cum = jnp.cumsum(to_chunks(g[..., None], c)[..., 0], axis=-1)
    lower = jnp.tril(jnp.ones((c, c), dtype=bool))
    strict = jnp.tril(jnp.ones((c, c), dtype=bool), -1)
    gamma = jnp.exp(jnp.where(lower, cum[..., :, None] - cum[..., None, :], -jnp.inf))
    kb = kc * bc[..., None]
    a0 = jnp.where(strict, jnp.einsum('nbhtd,nbhsd->nbhts', kb, kc) * gamma, 0.0)
    eye = jnp.eye(c, dtype=jnp.float32)
    rhs = jnp.concatenate([vc * bc[..., None], kb * jnp.exp(cum)[..., None]], axis=-1)
    uw = lax.linalg.triangular_solve(a0 + eye, rhs, left_side=True, lower=True, unit_diagonal=True)
    u, w = uw[..., :GDN_DV], uw[..., GDN_DV:]
    att = jnp.einsum('nbhtd,nbhsd->nbhts', qc, kc) * gamma
    qd = qc * jnp.exp(cum)[..., None]
    kd = kc * jnp.exp(cum[..., -1:] - cum)[..., None]
    dl = jnp.exp(cum[..., -1])

    def step(S, xs):
        qd_n, att_n, u_n, w_n, kd_n, dl_n = xs
        v_new = u_n - jnp.einsum('bhtd,bhde->bhte', w_n, S)
        o = jnp.einsum('bhtd,bhde->bhte', qd_n, S) + jnp.einsum('bhts,bhse->bhte', att_n, v_new)
        S = dl_n[..., None, None] * S + jnp.einsum('bhsd,bhse->bhde', kd_n, v_new)
        return S, o

    S0 = jnp.zeros((q.shape[0], GDN_HEADS, GDN_DK, GDN_DV), jnp.float32)
    _, o = lax.scan(step, S0, (qd, att, u, w, kd, dl))
    return from_chunks(o)


def _complex_affine_combine(e1, e2):
    a1r, a1i, b1r, b1i = e1
    a2r, a2i, b2r, b2i = e2
    ar = a2r * a1r - a2i * a1i
    ai = a2r * a1i + a2i * a1r
    br = a2r * b1r - a2i * b1i + b2r
    bi = a2r * b1i + a2i * b1r + b2i
    return ar, ai, br, bi


def hgrn2_branch(q_raw, f_raw, i_raw, g_raw, lb, norm_g):
    bsz, L, _ = q_raw.shape
    z = f_raw.astype(jnp.float32).reshape(bsz, L, HG_HEADS, HG_DK)
    lb = lb.reshape(HG_HEADS, HG_DK)
    log_f = jnp.logaddexp(jnp.log(lb), jnp.log1p(-lb) + jax.nn.log_sigmoid(z))
    k = (1.0 - lb) * jax.nn.sigmoid(-z)
    q = jax.nn.silu(q_raw.astype(jnp.float32)).reshape(bsz, L, HG_HEADS, HG_DK) * HG_DK ** -0.5
    v = i_raw.astype(jnp.float32).reshape(bsz, L, HG_HEADS, HG_DV)
    o = gla_chunked(q, k, v, log_f, HGRN_CHUNK)
    o = head_rmsnorm(o, norm_g) * jax.nn.silu(g_raw.astype(jnp.float32))
    return o.astype(q_raw.dtype)


def retention_branch(q_raw, k_raw, v_raw, g_raw, pos, norm_g):
    bsz, L, _ = q_raw.shape
    q = rotary(q_raw.astype(jnp.float32).reshape(bsz, L, RET_HEADS, RET_DK), pos)
    k = rotary(k_raw.astype(jnp.float32).reshape(bsz, L, RET_HEADS, RET_DK), pos) * RET_DK ** -0.5
    v = v_raw.astype(jnp.float32).reshape(bsz, L, RET_HEADS, RET_DV)
    o = retention_chunked(q, k, v, CHUNK)
    o = head_groupnorm(o, norm_g) * jax.nn.silu(g_raw.astype(jnp.float32))
    return o.astype(q_raw.dtype)


def s5_branch(u, a_re, a_im, log_dt, b_re, b_im, c_re, c_im, d_skip, w_glu):
    bsz, L, _ = u.shape
    f32 = jnp.float32
    uf = u.astype(f32).reshape(bsz, L, SSM_GROUPS, SSM_GROUP)
    a_re, a_im = a_re.astype(f32), a_im.astype(f32)
    b_re, b_im, c_re, c_im = b_re.astype(f32), b_im.astype(f32), c_re.astype(f32), c_im.astype(f32)
    dt = jnp.exp(log_dt.astype(f32))[:, None]
    mag = jnp.exp(dt * a_re)
    abar_re = mag * jnp.cos(dt * a_im)
    abar_im = mag * jnp.sin(dt * a_im)
    den = a_re * a_re + a_im * a_im
    nr, ni = abar_re - 1.0, abar_im
    coef_re = (nr * a_re + ni * a_im) / den
    coef_im = (ni * a_re - nr * a_im) / den
    bbar_re = coef_re[..., None] * b_re - coef_im[..., None] * b_im
    bbar_im = coef_re[..., None] * b_im + coef_im[..., None] * b_re
    bu_re = jnp.einsum('blgp,gnp->lbgn', uf, bbar_re)
    bu_im = jnp.einsum('blgp,gnp->lbgn', uf, bbar_im)
    shape = (L, 1, SSM_GROUPS, SSM_STATE)
    a_r = jnp.broadcast_to(abar_re, shape)
    a_i = jnp.broadcast_to(abar_im, shape)
    _, _, x_re, x_im = lax.associative_scan(_complex_affine_combine, (a_r, a_i, bu_re, bu_im), axis=0)
    y = jnp.einsum('lbgn,gpn->blgp', x_re, c_re) - jnp.einsum('lbgn,gpn->blgp', x_im, c_im)
    y = y.reshape(bsz, L, SSM_WIDTH) + d_skip.astype(f32) * u.astype(f32)
    y = jax.nn.gelu(y)
    y = y * jax.nn.sigmoid(y @ w_glu.astype(f32))
    return y.astype(u.dtype)


def causal_depthwise_conv(x, w):
    return lax.conv_general_dilated(x, w[:, None, :], window_strides=(1,), padding=[(CONV_K - 1, 0)],
                                    dimension_numbers=('NWC', 'WIO', 'NWC'), feature_group_count=x.shape[-1])


def gdn_branch(qkv_raw, b_raw, a_raw, g_raw, conv_w, a_log, dt_bias, norm_g):
    bsz, L, _ = qkv_raw.shape
    f32 = jnp.float32
    qkv = jax.nn.silu(causal_depthwise_conv(qkv_raw, conv_w.astype(qkv_raw.dtype))).astype(f32)
    q, k, v = jnp.split(qkv, [GDN_HEADS * GDN_DK, 2 * GDN_HEADS * GDN_DK], axis=-1)
    q = l2norm(q.reshape(bsz, L, GDN_HEADS, GDN_DK)) * GDN_DK ** -0.5
    k = l2norm(k.reshape(bsz, L, GDN_HEADS, GDN_DK))
    v = v.reshape(bsz, L, GDN_HEADS, GDN_DV)
    beta = jax.nn.sigmoid(b_raw.astype(f32))
    g = -jnp.exp(a_log.astype(f32)) * jax.nn.softplus(a_raw.astype(f32) + dt_bias.astype(f32))
    o = gated_delta_chunked(q, k, v, beta, g, CHUNK)
    o = head_rmsnorm(o, norm_g) * jax.nn.silu(g_raw.astype(f32))
    return o.astype(qkv_raw.dtype)


def hybrid_layer(x, mask, pos, lb, norm_mix, w_in, hg_norm, ret_norm, a_re, a_im, log_dt, b_re, b_im,
                 c_re, c_im, d_skip, w_glu, conv_w, a_log, dt_bias, gdn_norm, w_branch, w_out,
                 norm_ffn, w_gu, w_down):
    h = rmsnorm(x, norm_mix) * mask
    (hg_q, hg_f, hg_i, hg_g, rt_q, rt_k, rt_v, rt_g, ss_u,
     gd_qkv, gd_b, gd_a, gd_g, gate) = _split(h @ w_in, IN_SPLITS)
    branches = (
        hgrn2_branch(hg_q, hg_f, hg_i, hg_g, lb, hg_norm),
        retention_branch(rt_q, rt_k, rt_v, rt_g, pos, ret_norm),
        s5_branch(ss_u, a_re, a_im, log_dt, b_re, b_im, c_re, c_im, d_skip, w_glu),
        gdn_branch(gd_qkv, gd_b, gd_a, gd_g, conv_w, a_log, dt_bias, gdn_norm),
    )
    gates = jnp.split(gate, N_BRANCH, axis=-1)
    merged = jax.nn.sigmoid(gates[0]) * (branches[0] @ w_branch[0])
    for m in range(1, N_BRANCH):
        merged = merged + jax.nn.sigmoid(gates[m]) * (branches[m] @ w_branch[m])
    x = x + merged @ w_out
    h2 = rmsnorm(x, norm_ffn)
    gt, up = jnp.split(h2 @ w_gu, 2, axis=-1)
    return x + (jax.nn.silu(gt) * up) @ w_down


def setup_inputs(seed: int = 0) -> dict:
    key = jax.random.key(seed)
    ks = iter(jax.random.split(key, 40))
    f32 = jnp.float32

    def nrm(shape, scale):
        return jax.random.normal(next(ks), shape, f32) * scale

    def unif(shape, lo, hi):
        return jax.random.uniform(next(ks), shape, f32, lo, hi)

    G, N, P, W = SSM_GROUPS, SSM_STATE, SSM_GROUP, SSM_WIDTH
    x = nrm((BATCH, SEQ, D_MODEL), 1.0)
    meta = nrm((N_META, D_MODEL), 1.0)
    norm_mix = 1.0 + nrm((DEPTH, D_MODEL), 0.02)
    w_in = nrm((DEPTH, D_MODEL, IN_COLS), D_MODEL ** -0.5)
    hg_lb_logits = nrm((DEPTH, HG_HEADS * HG_DK), 0.1)
    hg_norm = 1.0 + nrm((DEPTH, HG_HEADS * HG_DV), 0.02)
    ret_norm = 1.0 + nrm((DEPTH, RET_HEADS * RET_DV), 0.02)
    n_idx = jnp.arange(N, dtype=f32)
    ssm_a_re = -0.5 + nrm((DEPTH, G, N), 0.01)
    ssm_a_im = math.pi * n_idx + nrm((DEPTH, G, N), 0.01)
    ssm_log_dt = unif((DEPTH, G), math.log(0.001), math.log(0.1))
    ssm_b_re = nrm((DEPTH, G, N, P), (2 * P) ** -0.5)
    ssm_b_im = nrm((DEPTH, G, N, P), (2 * P) ** -0.5)
    ssm_c_re = nrm((DEPTH, G, P, N), N ** -0.5)
    ssm_c_im = nrm((DEPTH, G, P, N), N ** -0.5)
    ssm_d = nrm((DEPTH, W), 1.0)
    ssm_w_glu = nrm((DEPTH, W, W), W ** -0.5)
    gdn_conv = nrm((DEPTH, CONV_K, GDN_QKV), CONV_K ** -0.5)
    gdn_a_log = jnp.log(unif((DEPTH, GDN_HEADS), 1.0, 16.0))
    dt0 = jnp.exp(unif((DEPTH, GDN_HEADS), math.log(0.001), math.log(0.1)))
    gdn_dt_bias = dt0 + jnp.log(-jnp.expm1(-dt0))
    gdn_norm = 1.0 + nrm((DEPTH, GDN_HEADS * GDN_DV), 0.02)
    w_branch = nrm((DEPTH, N_BRANCH, BRANCH_WIDTH, D_MODEL), BRANCH_WIDTH ** -0.5)
    w_out = nrm((DEPTH, D_MODEL, D_MODEL), D_MODEL ** -0.5)
    norm_ffn = 1.0 + nrm((DEPTH, D_MODEL), 0.02)
    w_gu = nrm((DEPTH, D_MODEL, 2 * D_FF), D_MODEL ** -0.5)
    w_down = nrm((DEPTH, D_FF, D_MODEL), D_FF ** -0.5)
    norm_final = 1.0 + nrm((D_MODEL,), 0.02)
    return {"x": x, "meta": meta, "norm_mix": norm_mix, "w_in": w_in, "hg_lb_logits": hg_lb_logits,
            "hg_norm": hg_norm, "ret_norm": ret_norm, "ssm_a_re": ssm_a_re, "ssm_a_im": ssm_a_im,
            "ssm_log_dt": ssm_log_dt, "ssm_b_re": ssm_b_re, "ssm_b_im": ssm_b_im, "ssm_c_re": ssm_c_re,
            "ssm_c_im": ssm_c_im, "ssm_d": ssm_d, "ssm_w_glu": ssm_w_glu, "gdn_conv": gdn_conv,
            "gdn_a_log": gdn_a_log, "gdn_dt_bias": gdn_dt_bias, "gdn_norm": gdn_norm, "w_branch": w_branch,
            "w_out": w_out, "norm_ffn": norm_ffn, "w_gu": w_gu, "w_down": w_down, "norm_final": norm_final}


def reference(x, meta, norm_mix, w_in, hg_lb_logits, hg_norm, ret_norm, ssm_a_re, ssm_a_im, ssm_log_dt,
              ssm_b_re, ssm_b_im, ssm_c_re, ssm_c_im, ssm_d, ssm_w_glu, gdn_conv, gdn_a_log, gdn_dt_bias,
              gdn_norm, w_branch, w_out, norm_ffn, w_gu, w_down, norm_final):
    bsz = x.shape[0]
    dt = x.dtype
    h = jnp.concatenate([jnp.zeros((bsz, PAD, D_MODEL), dt),
                         jnp.broadcast_to(meta.astype(dt)[None], (bsz, N_META, D_MODEL)), x], axis=1)
    L = h.shape[1]
    idx = jnp.arange(L)
    mask = (idx >= PAD).astype(dt)[:, None]
    pos = (idx - PAD).astype(jnp.float32)
    lb_all = jnp.cumsum(jax.nn.softmax(hg_lb_logits.astype(jnp.float32), axis=0), axis=0)
    lb_all = lb_all - lb_all[:1]
    for l in range(DEPTH):
        h = hybrid_layer(h, mask, pos, lb_all[l], norm_mix[l], w_in[l], hg_norm[l], ret_norm[l],
                         ssm_a_re[l], ssm_a_im[l], ssm_log_dt[l], ssm_b_re[l], ssm_b_im[l], ssm_c_re[l],
                         ssm_c_im[l], ssm_d[l], ssm_w_glu[l], gdn_conv[l], gdn_a_log[l], gdn_dt_bias[l],
                         gdn_norm[l], w_branch[l], w_out[l], norm_ffn[l], w_gu[l], w_down[l])
    out = rmsnorm(h, norm_final)
    return out[:, PAD + N_META:, :]
```

```python
import contextlib
import numpy as np
import concourse.bass as bass
import concourse.mybir as mybir
from concourse.bass_utils import run_bass_kernel_spmd

F32 = mybir.dt.float32
BF16 = mybir.dt.bfloat16
ALU = mybir.AluOpType
AF = mybir.ActivationFunctionType
AX = mybir.AxisListType


class Sched:
    ENG = ['pe', 'dve', 'act', 'pool', 'sp']

    def __init__(self, nc):
        self.nc = nc
        self.ops = {e: [] for e in self.ENG}
        self.cnt = {}
        self.seen = {e: {} for e in self.ENG}
        self.writers = {}
        self.readers = {}
        self.genwar = {}
        self.stack = contextlib.ExitStack()
        self.nt = 0
        self.dma_idx = {}
        import os; nq = int(os.environ.get('NQ', '48')); self.NQ = {'sp': nq, 'pool': max(1, nq // 2), 'act': 8, 'dve': 4, 'pe': 4}

    def sbuf(self, shape, dtype, name=None):
        self.nt += 1
        name = name or f"t{self.nt}"
        return self.stack.enter_context(self.nc.sbuf_tensor(name, list(shape), dtype))

    def psum(self, shape, dtype, name=None):
        self.nt += 1
        name = name or f"p{self.nt}"
        return self.stack.enter_context(self.nc.psum_tensor(name, list(shape), dtype))

    def op(self, eng, fn, reads=(), writes=(), dma=False, pwrites=()):
        if dma:
            idx = self.dma_idx.get(eng, 0)
            self.dma_idx[eng] = idx + 1
            sem = f"q_{eng}_{idx % self.NQ[eng]}"
        else:
            sem = eng
        deps = []
        if dma and self.cnt.get(sem, 0) > 0:
            deps.append((sem, self.cnt[sem]))
        same = (lambda s: (s == sem and not dma))
        for b in reads:
            deps.extend(self.writers.get(b, {}).items())
        for b in writes:
            for s_, v_ in self.writers.get(b, {}).items():
                if not same(s_):
                    deps.append((s_, v_))
            for r in self.readers.get(b, ()):
                if not same(r[0]):
                    deps.append(r)
        for b in pwrites:
            if self.readers.get(b):
                self.genwar[b] = self.readers[b]
                self.readers[b] = []
                self.writers[b] = {}
            for r in self.genwar.get(b, ()):
                if not same(r[0]):
                    deps.append(r)
        waits = {}
        for (s, v) in deps:
            if s == 'pe' and sem == 'pe':
                continue
            if v > waits.get(s, 0):
                waits[s] = v
        seen = self.seen[eng]
        wl = []
        for s, v in waits.items():
            if v > seen.get(s, 0):
                seen[s] = v
                wl.append((s, v))
        amt = 16 if dma else 1
        self.cnt[sem] = self.cnt.get(sem, 0) + amt
        val = self.cnt[sem]
        for b in writes:
            self.writers[b] = {sem: val}
            self.readers[b] = []
            self.genwar[b] = []
        for b in pwrites:
            self.writers.setdefault(b, {})[sem] = val
        for b in reads:
            self.readers.setdefault(b, []).append((sem, val))
        self.ops[eng].append((fn, wl, sem, amt))

    def dma(self, eng, out, in_, reads=(), writes=(), pwrites=(), **kw):
        self.op(eng, lambda e: e.dma_start(out=out, in_=in_, **kw), reads, writes, dma=True, pwrites=pwrites)

    def emit(self):
        nc = self.nc
        names = sorted(self.cnt.keys())
        sems = {n: self.stack.enter_context(nc.semaphore(n)) for n in names}
        final = dict(self.cnt)
        with nc.Block() as block:
            def mk(engname):
                def body(engine):
                    for (fn, wl, sem, amt) in self.ops[engname]:
                        for (s, v) in wl:
                            engine.wait_ge(sems[s], v)
                        ins = fn(engine)
                        ins.then_inc(sems[sem], amt)
                    if engname == 'sp':
                        for s, v in final.items():
                            engine.wait_ge(sems[s], v)
                return body
            block.tensor(mk('pe'))
            block.vector(mk('dve'))
            block.scalar(mk('act'))
            block.gpsimd(mk('pool'))
            block.sync(mk('sp'))
        self.stack.close()


NT = 2064
TW = 344
NTT = 6
EPS = 1e-6

def load_consts_dense(S, nc):
    ones = S.sbuf([128, 128], F32, "ones")
    S.op('pool', lambda e: e.memset(ones[:], 1.0), writes=['ones'])
    return ones

def stage_rmsnorm(S, x_sb, g_sb, h_sb, ones, ps_pool, tmp, xkey='x', hkey='h', gkey='g'):
    sq, rstd = tmp
    for tt in range(NTT):
        sl = slice(tt * TW, (tt + 1) * TW)
        S.op('act', lambda e, sl=sl: e.activation(out=sq[:, :, :], in_=x_sb[:, :, sl], func=AF.Square), reads=[xkey], writes=['sq'])
        ps, pk = ps_pool()
        for k in range(8):
            S.op('pe', lambda e, k=k, ps=ps: e.matmul(ps[:, 0:TW], lhsT=ones[:, :], rhs=sq[:, k, :], start=(k == 0), stop=(k == 7)),
                 reads=['sq', 'ones'], writes=[pk])
        S.op('act', lambda e, ps=ps: e.activation(out=rstd[:, :], in_=ps[:, 0:TW], func=AF.Sqrt, scale=1.0 / 1024, bias=EPSB[0][:, 0:1]), reads=[pk, 'epsb'], writes=['rstd'])
        S.op('dve', lambda e: e.reciprocal(out=rstd[:, :], in_=rstd[:, :]), reads=['rstd'], writes=['rstd'])
        for k in range(8):
            S.op('dve', lambda e, k=k, sl=sl: e.scalar_tensor_tensor(out=h_sb[:, k, sl], in0=x_sb[:, k, sl], scalar=g_sb[:, k:k + 1], in1=rstd[:, :], op0=ALU.mult, op1=ALU.mult),
                 reads=[xkey, gkey, 'rstd'], writes=[hkey])

EPSB = [None]
def make_epsb(S):
    t = S.sbuf([128, 1], F32, "epsb")
    S.op('pool', lambda e: e.memset(t[:], EPS), writes=['epsb'])
    EPSB[0] = t

def stage_proj(S, h_sb, w_dram, ncols, wbufs, ps_pool, evac, hkey='h', kchunks=8, wname='w'):
    nblk = (ncols + 511) // 512
    for cb in range(nblk):
        c0 = cb * 512
        w = min(512, ncols - c0)
        wt = wbufs[cb % len(wbufs)]
        wk = f'{wname}{cb % len(wbufs)}'
        S.dma('pool', wt[:, 0:kchunks, 0:w], w_dram[:, c0:c0 + w].rearrange("(k p) c -> p k c", p=128), writes=[wk])
        for sub in range(w // 128):
            for tt in range(NTT):
                sl = slice(tt * TW, (tt + 1) * TW)
                ps, pk = ps_pool()
                for k in range(kchunks):
                    S.op('pe', lambda e, k=k, ps=ps, wt=wt, sub=sub, sl=sl: e.matmul(ps[:, 0:TW], lhsT=wt[:, k, sub * 128:(sub + 1) * 128], rhs=h_sb[:, k, sl], start=(k == 0), stop=(k == kchunks - 1)),
                         reads=[wk, hkey], writes=[pk])
                evac(cb * 4 + sub, tt, ps, pk)

class PsPool:
    def __init__(self, S, n, shape=(128, 512), dtype=F32, prefix='ps'):
        self.tiles = [S.psum(list(shape), dtype, f"{prefix}{i}") for i in range(n)]
        self.prefix = prefix
        self.i = 0
    def __call__(self):
        t = self.tiles[self.i % len(self.tiles)]
        k = f"{self.prefix}{self.i % len(self.tiles)}"
        self.i += 1
        return t, k

def build_A(NA):
    nc = bass.Bass("TRN2", target_bir_lowering=False)
    xT = nc.dram_tensor("xT", [1024, NT], F32, kind="ExternalInput").ap()
    gm = nc.dram_tensor("gmix", [128, 8], F32, kind="ExternalInput").ap()
    wA = nc.dram_tensor("wA", [1024, NA], F32, kind="ExternalInput").ap()
    zT = nc.dram_tensor("zT", [NA, NT], F32, kind="ExternalOutput").ap()
    S = Sched(nc)
    x_sb = S.sbuf([128, 8, NT], F32, "x_sb")
    h_sb = S.sbuf([128, 8, NT], BF16, "h_sb")
    g_sb = S.sbuf([128, 8], F32, "g_sb")
    sq = S.sbuf([128, 8, TW], F32, "sq")
    rstd = S.sbuf([128, TW], F32, "rstd")
    wbufs = [S.sbuf([128, 8, 512], BF16, f"wb{i}") for i in range(3)]
    stg = [S.sbuf([128, NT], F32, f"stg{i}") for i in range(2)]
    ones = load_consts_dense(S, nc)
    make_epsb(S)
    pp = PsPool(S, 6)
    S.dma('sp', g_sb[:, :], gm[:, :], writes=['g'])
    for k in range(8):
        S.dma('sp', x_sb[:, k, :], xT[k * 128:(k + 1) * 128, :], pwrites=['x'])
    stage_rmsnorm(S, x_sb, g_sb, h_sb, ones, pp, (sq, rstd))
    cnt = [0]
    def evac(cb, tt, ps, pk):
        st = stg[cb % 2]; sk = f'stg{cb % 2}'
        sl = slice(tt * TW, (tt + 1) * TW)
        eng = 'act' if (cnt[0] % 2 == 0) else 'dve'
        cnt[0] += 1
        if eng == 'act':
            S.op('act', lambda e: e.activation(out=st[:, sl], in_=ps[:, 0:TW], func=AF.Copy), reads=[pk], writes=[sk])
        else:
            S.op('dve', lambda e: e.tensor_copy(out=st[:, sl], in_=ps[:, 0:TW]), reads=[pk], writes=[sk])
        if tt == NTT - 1:
            S.dma('sp', zT[cb * 128:(cb + 1) * 128, :], st[:, :], reads=[sk])
    stage_proj(S, h_sb, wA, NA, wbufs, pp, evac)
    S.emit()
    return nc


T = 8704
NBLK = 17
BW = 512
CH = 64
NCH = 8

def TT(S, eng, out, in0, in1, op, r, w):
    S.op(eng, lambda e: e.tensor_tensor(out=out, in0=in0, in1=in1, op=op), reads=r, writes=w)
def TS(S, eng, out, in0, s1, s2, op0, op1, r, w):
    S.op(eng, lambda e: e.tensor_scalar(out=out, in0=in0, scalar1=s1, scalar2=s2, op0=op0, op1=op1), reads=r, writes=w)
def STT(S, eng, out, in0, sc, in1, op0, op1, r, w):
    S.op(eng, lambda e: e.scalar_tensor_tensor(out=out, in0=in0, scalar=sc, in1=in1, op0=op0, op1=op1), reads=r, writes=w)
def ACTF(S, out, in_, func, r, w, scale=1.0, bias=None):
    if bias is None:
        S.op('act', lambda e: e.activation(out=out, in_=in_, func=func, scale=scale), reads=r, writes=w)
    else:
        S.op('act', lambda e: e.activation(out=out, in_=in_, func=func, scale=scale, bias=bias), reads=r, writes=w)
_PE_MODE = [None]
def _ru(n):
    return 32 if n <= 32 else (64 if n <= 64 else 128)
def _pe_mode(S, ap, tr):
    import os
    if os.environ.get('PE_DRAIN') is None:
        return
    m = (_ru(ap.shape[0]), _ru(ap.shape[1]), str(ap.dtype) == str(F32))
    if _PE_MODE[0] is not None and _PE_MODE[0] != m:
        S.op('pe', lambda e: e.drain(), reads=(), writes=())
    _PE_MODE[0] = m
def MM(S, out, lhsT, rhs, r, w, start=True, stop=True):
    _pe_mode(S, lhsT, False)
    S.op('pe', lambda e: e.matmul(out, lhsT=lhsT, rhs=rhs, start=start, stop=stop), reads=r, writes=w)
def TR(S, out, in_, ident, r, w):
    _pe_mode(S, in_, True)
    S.op('pe', lambda e: e.transpose(out, in_, ident), reads=r, writes=w)
def COPY(S, eng, out, in_, r, w):
    if eng == 'act':
        S.op('act', lambda e: e.activation(out=out, in_=in_, func=AF.Copy), reads=r, writes=w)
    else:
        S.op(eng, lambda e: e.tensor_copy(out=out, in_=in_), reads=r, writes=w)
def RED(S, eng, out, in_, r, w):
    S.op(eng, lambda e: e.tensor_reduce(out=out, in_=in_, axis=AX.X, op=ALU.add), reads=r, writes=w)


class Ctx:
    pass


def norm_out(S, C, pfx, o_ps, ok, half, blk, out_dram, center):
    ost = C.ost[(blk * 2 + half) % 2]
    osk = f'ost{(blk * 2 + half) % 2}'
    o_ps = o_ps[0:64, 0:512]
    o3 = o_ps.rearrange("p (c e) -> p c e", e=128)
    sq3 = C.osq[0:64, :].rearrange("p (c e) -> p c e", e=128)
    ACTF(S, C.osq[0:64, :], o_ps, AF.Square, [ok], ['osq'])
    RED(S, 'dve', C.oss[0:64, 0:4], sq3, ['osq'], ['oss'])
    if center:
        RED(S, 'dve', C.oss[0:64, 4:8], o3, [ok], ['oss'])
        TS(S, 'dve', C.oss[0:64, 4:8], C.oss[0:64, 4:8], 1.0 / 128, None, ALU.mult, ALU.bypass, ['oss'], ['oss'])
        TT(S, 'dve', C.oss[0:64, 8:12], C.oss[0:64, 4:8], C.oss[0:64, 4:8], ALU.mult, ['oss'], ['oss'])
        STT(S, 'dve', C.oss[0:64, 0:4], C.oss[0:64, 0:4], 1.0 / 128, C.oss[0:64, 8:12], ALU.mult, ALU.subtract, ['oss'], ['oss'])
        ACTF(S, C.oss[0:64, 0:4], C.oss[0:64, 0:4], AF.Sqrt, ['oss', 'epsb'], ['oss'], scale=1.0, bias=EPSB[0][0:64, 0:1])
    else:
        ACTF(S, C.oss[0:64, 0:4], C.oss[0:64, 0:4], AF.Sqrt, ['oss', 'epsb'], ['oss'], scale=1.0 / 128, bias=EPSB[0][0:64, 0:1])
    S.op('dve', lambda e: e.reciprocal(out=C.oss[0:64, 0:4], in_=C.oss[0:64, 0:4]), reads=['oss'], writes=['oss'])
    for c in range(4):
        if center:
            TS(S, 'dve', ost[0:64, c, :], o3[:, c, :], C.oss[0:64, 4 + c:5 + c], C.oss[0:64, c:c + 1], ALU.subtract, ALU.mult, [ok, 'oss'], [osk])
        else:
            TS(S, 'dve', ost[0:64, c, :], o3[:, c, :], C.oss[0:64, c:c + 1], None, ALU.mult, ALU.bypass, [ok, 'oss'], [osk])
    t0 = blk * BW + half * 256
    S.dma('sp', out_dram[t0:t0 + 256, :].rearrange("(c p) e -> p c e", p=64), ost[0:64, :, :], reads=[osk])


class Hgrn:
    def __init__(self, S, nc, C, layer):
        self.S, self.C, self.layer = S, C, layer
        d = lambda n, s: nc.dram_tensor(n, s, F32, kind="ExternalInput").ap()
        self.qT = d("hg_qT", [64, T]); self.fT = d("hg_fT", [64, T]); self.v = d("hg_v", [T, 128])
        self.lg = d("hg_lg", [64, 4])
        self.out = nc.dram_tensor("hg_out", [T, 128], F32, kind="ExternalOutput").ap()
        sb = S.sbuf
        self.zq = [sb([64, BW], F32, f"hg_zq{i}") for i in range(2)]
        self.zf = [sb([64, BW], F32, f"hg_zf{i}") for i in range(2)]
        self.vb = [sb([64, NCH, 128], BF16, f"hg_vb{i}") for i in range(2)]
        self.lgs = sb([64, 4], F32, "hg_lgs"); self.lb = sb([64, 4], F32, "hg_lb")
        self.f = sb([64, BW], F32, "hg_f"); self.kT = sb([64, BW], F32, "hg_kT"); self.b = sb([64, BW], F32, "hg_b")
        self.bm = sb([64, BW], F32, "hg_bm"); self.e1 = sb([64, BW], F32, "hg_e1"); self.e2 = sb([64, BW], F32, "hg_e2")
        self.sq = sb([64, BW], F32, "hg_sq")
        self.qt = sb([64, BW], BF16, "hg_qt"); self.kt = sb([64, BW], BF16, "hg_kt"); self.ke = sb([64, BW], F32, "hg_ke")
        self.dec = sb([64, 16], F32, "hg_dec")
        self.kes = [sb([64, 64], BF16, f"hg_kes{i}") for i in range(2)]
        self.att = [sb([64, 64], BF16, f"hg_att{i}") for i in range(2)]
        self.Sst = sb([64, 128], F32, "hg_S"); self.Sb = [sb([64, 128], BF16, f"hg_Sb{i}") for i in range(2)]
        S.op('pool', lambda e: e.memset(self.Sst[:], 0.0), writes=['hg_S'])
        S.dma('sp', self.lgs[:, :], self.lg[:, :], writes=['hg_lgs'])
        S.op('dve', lambda e: e.tensor_reduce(out=self.lb[:, 0:1], in_=self.lgs[:, :], axis=AX.X, op=ALU.max), reads=['hg_lgs'], writes=['hg_lb'])
        TS(S, 'dve', self.lgs[:, :], self.lgs[:, :], self.lb[:, 0:1], None, ALU.subtract, ALU.bypass, ['hg_lgs', 'hg_lb'], ['hg_lgs'])
        ACTF(S, self.lgs[:, :], self.lgs[:, :], AF.Exp, ['hg_lgs'], ['hg_lgs'])
        RED(S, 'dve', self.lb[:, 1:2], self.lgs[:, :], ['hg_lgs'], ['hg_lb'])
        S.op('dve', lambda e: e.reciprocal(out=self.lb[:, 1:2], in_=self.lb[:, 1:2]), reads=['hg_lb'], writes=['hg_lb'])
        if layer == 0:
            S.op('dve', lambda e: e.memset(self.lb[:, 2:3], 0.0), reads=['hg_lb'], writes=['hg_lb'])
        else:
            RED(S, 'dve', self.lb[:, 2:3], self.lgs[:, 1:layer + 1], ['hg_lgs', 'hg_lb'], ['hg_lb'])
            TT(S, 'dve', self.lb[:, 2:3], self.lb[:, 2:3], self.lb[:, 1:2], ALU.mult, ['hg_lb'], ['hg_lb'])
        TS(S, 'dve', self.lb[:, 3:4], self.lb[:, 2:3], -1.0, 1.0, ALU.mult, ALU.add, ['hg_lb'], ['hg_lb'])

    def block(self, blk):
        S, C = self.S, self.C
        par = blk % 2
        zq, zf, vb = self.zq[par], self.zf[par], self.vb[par]
        kq, kf, kv_ = f'hg_zq{par}', f'hg_zf{par}', f'hg_vb{par}'
        sl = slice(blk * BW, (blk + 1) * BW)
        S.dma('sp', zq[:, :], self.qT[:, sl], writes=[kq])
        S.dma('sp', zf[:, :], self.fT[:, sl], writes=[kf])
        S.dma('pool', vb[:, :, :], self.v[sl, :].rearrange("(c p) e -> p c e", p=64), writes=[kv_])
        f, kT, b, bm, e1, e2, sq, qt, kt, ke, dec = self.f, self.kT, self.b, self.bm, self.e1, self.e2, self.sq, self.qt, self.kt, self.ke, self.dec
        ACTF(S, f[:, :], zf[:, :], AF.Sigmoid, [kf], ['hg_f'])
        TS(S, 'dve', f[:, :], f[:, :], self.lb[:, 3:4], self.lb[:, 2:3], ALU.mult, ALU.add, ['hg_f', 'hg_lb'], ['hg_f'])
        ACTF(S, b[:, :], f[:, :], AF.Ln, ['hg_f'], ['hg_b'])
        TS(S, 'pool', kT[:, :], f[:, :], -1.0, 1.0, ALU.mult, ALU.add, ['hg_f'], ['hg_kT'])
        S.op('dve', lambda e: e.tensor_tensor_scan(out=b[:, :], data0=C.rmask[0:64, :], data1=b[:, :], initial=0.0, op0=ALU.mult, op1=ALU.add), reads=['hg_b', 'rmask'], writes=['hg_b'])
        b3 = b[:, :].rearrange("p (c t) -> p c t", t=CH)
        v3 = lambda t: t[:, :].rearrange("p (c t) -> p c t", t=CH)
        TT(S, 'dve', v3(bm), b3, b3[:, :, 31:32].broadcast_to([64, NCH, CH]), ALU.subtract, ['hg_b'], ['hg_bm'])
        ACTF(S, e1[:, :], bm[:, :], AF.Exp, ['hg_bm'], ['hg_e1'])
        ACTF(S, e2[:, :], bm[:, :], AF.Exp, ['hg_bm'], ['hg_e2'], scale=-1.0)
        TT(S, 'dve', v3(bm), b3[:, :, 63:64].broadcast_to([64, NCH, CH]), b3, ALU.subtract, ['hg_b'], ['hg_bm'])
        ACTF(S, ke[:, :], bm[:, :], AF.Exp, ['hg_bm'], ['hg_ke'])
        ACTF(S, dec[:, 0:8], b3[:, :, 63], AF.Exp, ['hg_b'], ['hg_dec'])
        ACTF(S, dec[:, 8:16], b3[:, :, 31], AF.Exp, ['hg_b'], ['hg_dec'])
        ACTF(S, sq[:, :], zq[:, :], AF.Silu, [kq], ['hg_sq'])
        STT(S, 'dve', qt[:, :], sq[:, :], 0.125, e1[:, :], ALU.mult, ALU.mult, ['hg_sq', 'hg_e1'], ['hg_qt'])
        TT(S, 'pool', kt[:, :], kT[:, :], e2[:, :], ALU.mult, ['hg_kT', 'hg_e2'], ['hg_kt'])
        TT(S, 'pool', ke[:, :], kT[:, :], ke[:, :], ALU.mult, ['hg_kT', 'hg_ke'], ['hg_ke'])
        for c in range(NCH):
            cs = slice(c * CH, (c + 1) * CH)
            i2 = c % 2
            if c % 4 == 0:
                o_ps, ok = C.obank()
            pt, ptk = C.pp()
            TR(S, pt[0:64, 0:64], ke[:, cs], C.ident[0:64, 0:64], ['hg_ke', 'ident'], [ptk])
            COPY(S, 'act', self.kes[i2][:, :], pt[0:64, 0:64], [ptk], [f'hg_kes{i2}'])
            pa, pak = C.pp()
            MM(S, pa[0:64, 0:64], kt[:, cs], qt[:, cs], ['hg_kt', 'hg_qt'], [pak])
            TT(S, 'dve', self.att[i2][:, :], pa[0:64, 0:64], C.maskU[0:64, :], ALU.mult, [pak, 'maskU'], [f'hg_att{i2}'])
            pk, pkk = C.pp()
            MM(S, pk[0:64, 0:128], self.kes[i2][:, :], vb[:, c, :], [f'hg_kes{i2}', kv_], [pkk])
            TS(S, 'pool', self.Sb[i2][:, :], self.Sst[:, :], dec[:, 8 + c:9 + c], None, ALU.mult, ALU.bypass, ['hg_S', 'hg_dec'], [f'hg_Sb{i2}'])
            oc = o_ps[0:64, (c % 4) * 128:(c % 4 + 1) * 128]
            MM(S, oc, self.att[i2][:, :], vb[:, c, :], [f'hg_att{i2}', kv_], [ok], start=True, stop=False)
            MM(S, oc, qt[:, cs], self.Sb[i2][:, :], ['hg_qt', f'hg_Sb{i2}'], [ok], start=False, stop=True)
            STT(S, 'dve', self.Sst[:, :], self.Sst[:, :], dec[:, c:c + 1], pk[0:64, 0:128], ALU.mult, ALU.add, ['hg_S', 'hg_dec', pkk], ['hg_S'])
            if c % 4 == 3:
                norm_out(S, C, 'hg_', o_ps, ok, c // 4, blk, self.out, center=False)


class Ret:
    def __init__(self, S, nc, C):
        self.S, self.C = S, C
        d = lambda n, s: nc.dram_tensor(n, s, F32, kind="ExternalInput").ap()
        self.qT = d("rt_qT", [64, T]); self.qpT = d("rt_qpT", [64, T]); self.kT = d("rt_kT", [64, T]); self.kpT = d("rt_kpT", [64, T])
        self.v = d("rt_v", [T, 128]); self.cos = d("rt_cos", [64, T]); self.sin = d("rt_sin", [64, T])
        self.tab = d("rt_tab", [64, 64 + 512 + 512 + 1])
        self.out = nc.dram_tensor("rt_out", [T, 128], F32, kind="ExternalOutput").ap()
        sb = S.sbuf
        self.inb = [[sb([64, BW], F32, f"rt_in{j}_{i}") for j in range(6)] for i in range(2)]
        self.vb = [sb([64, NCH, 128], BF16, f"rt_vb{i}") for i in range(2)]
        self.tabs = sb([64, 64 + 512 + 512 + 1], F32, "rt_tabs")
        self.t1 = sb([64, BW], F32, "rt_t1"); self.t2 = sb([64, BW], F32, "rt_t2")
        self.qr = sb([64, BW], BF16, "rt_qr"); self.qx = sb([64, BW], BF16, "rt_qx"); self.kr = sb([64, BW], BF16, "rt_kr"); self.kz = sb([64, BW], F32, "rt_kz")
        self.kzs = [sb([64, 64], BF16, f"rt_kzs{i}") for i in range(2)]
        self.att = [sb([64, 64], BF16, f"rt_att{i}") for i in range(2)]
        self.R = sb([64, 128], F32, "rt_R"); self.Rb = [sb([64, 128], BF16, f"rt_Rb{i}") for i in range(2)]
        S.op('pool', lambda e: e.memset(self.R[:], 0.0), writes=['rt_R'])
        S.dma('sp', self.tabs[:, :], self.tab[:, :], writes=['rt_tabs'])

    def block(self, blk):
        S, C = self.S, self.C
        par = blk % 2
        sl = slice(blk * BW, (blk + 1) * BW)
        ib = self.inb[par]; vb = self.vb[par]
        ik = [f'rt_in{j}_{par}' for j in range(6)]
        kv_ = f'rt_vb{par}'
        for j, src in enumerate([self.qT, self.qpT, self.kT, self.kpT, self.cos, self.sin]):
            S.dma('sp', ib[j][:, :], src[:, sl], writes=[ik[j]])
        S.dma('pool', vb[:, :, :], self.v[sl, :].rearrange("(c p) e -> p c e", p=64), writes=[kv_])
        decT = self.tabs[:, 0:64]; xi = self.tabs[:, 64:576]; zeta = self.tabs[:, 576:1088]; g64 = self.tabs[:, 1088:1089]
        t1, t2, qr, qx, kr, kz = self.t1, self.t2, self.qr, self.qx, self.kr, self.kz
        TT(S, 'dve', t1[:, :], ib[0][:, :], ib[4][:, :], ALU.mult, [ik[0], ik[4]], ['rt_t1'])
        TT(S, 'pool', t2[:, :], ib[1][:, :], ib[5][:, :], ALU.mult, [ik[1], ik[5]], ['rt_t2'])
        TT(S, 'dve', t1[:, :], t1[:, :], t2[:, :], ALU.add, ['rt_t1', 'rt_t2'], ['rt_t1'])
        COPY(S, 'act', qr[:, :], t1[:, :], ['rt_t1'], ['rt_qr'])
        TT(S, 'dve', qx[:, :], t1[:, :], xi, ALU.mult, ['rt_t1', 'rt_tabs'], ['rt_qx'])
        TT(S, 'dve', t1[:, :], ib[2][:, :], ib[4][:, :], ALU.mult, [ik[2], ik[4], 'rt_qr', 'rt_qx'], ['rt_t1'])
        TT(S, 'pool', t2[:, :], ib[3][:, :], ib[5][:, :], ALU.mult, [ik[3], ik[5]], ['rt_t2'])
        TT(S, 'dve', t1[:, :], t1[:, :], t2[:, :], ALU.add, ['rt_t1', 'rt_t2'], ['rt_t1'])
        ACTF(S, kr[:, :], t1[:, :], AF.Copy, ['rt_t1'], ['rt_kr'], scale=0.125)
        STT(S, 'dve', kz[:, :], t1[:, :], 0.125, zeta, ALU.mult, ALU.mult, ['rt_t1', 'rt_tabs'], ['rt_kz'])
        for c in range(NCH):
            cs = slice(c * CH, (c + 1) * CH)
            i2 = c % 2
            if c % 4 == 0:
                o_ps, ok = C.obank()
            pt, ptk = C.pp()
            TR(S, pt[0:64, 0:64], kz[:, cs], C.ident[0:64, 0:64], ['rt_kz', 'ident'], [ptk])
            COPY(S, 'act', self.kzs[i2][:, :], pt[0:64, 0:64], [ptk], [f'rt_kzs{i2}'])
            pa, pak = C.pp()
            MM(S, pa[0:64, 0:64], kr[:, cs], qr[:, cs], ['rt_kr', 'rt_qr'], [pak])
            TT(S, 'dve', self.att[i2][:, :], pa[0:64, 0:64], decT, ALU.mult, [pak, 'rt_tabs'], [f'rt_att{i2}'])
            pk, pkk = C.pp()
            MM(S, pk[0:64, 0:128], self.kzs[i2][:, :], vb[:, c, :], [f'rt_kzs{i2}', kv_], [pkk])
            COPY(S, 'pool', self.Rb[i2][:, :], self.R[:, :], ['rt_R'], [f'rt_Rb{i2}'])
            oc = o_ps[0:64, (c % 4) * 128:(c % 4 + 1) * 128]
            MM(S, oc, self.att[i2][:, :], vb[:, c, :], [f'rt_att{i2}', kv_], [ok], start=True, stop=False)
            MM(S, oc, qx[:, cs], self.Rb[i2][:, :], ['rt_qx', f'rt_Rb{i2}'], [ok], start=False, stop=True)
            STT(S, 'dve', self.R[:, :], self.R[:, :], g64, pk[0:64, 0:128], ALU.mult, ALU.add, ['rt_R', 'rt_tabs', pkk], ['rt_R'])
            if c % 4 == 3:
                norm_out(S, C, 'rt_', o_ps, ok, c // 4, blk, self.out, center=True)


def make_ctx(S, nc, n_general=6):
    C = Ctx()
    d = lambda n, s: nc.dram_tensor(n, s, F32, kind="ExternalInput").ap()
    C.maskU_d = d("maskU", [64, 64]); C.ident_d = d("ident", [128, 128]); C.rmask_d = d("rmask", [128, 512])
    C.maskU = S.sbuf([64, 64], F32, "maskU_s"); C.ident = S.sbuf([128, 128], F32, "ident_s"); C.rmask = S.sbuf([128, 512], F32, "rmask_s")
    S.dma('sp', C.maskU[:, :], C.maskU_d[:, :], writes=['maskU'])
    S.dma('sp', C.ident[:, :], C.ident_d[:, :], writes=['ident'])
    S.dma('sp', C.rmask[:, :], C.rmask_d[:, :], writes=['rmask'])
    make_epsb(S)
    C.ones128 = S.sbuf([128, 128], F32, "ones128")
    S.op('pool', lambda e: e.memset(C.ones128[:], 1.0), writes=['ones128'])
    C.maskLs = S.sbuf([64, 64], F32, "maskLs")
    TS(S, 'dve', C.maskLs[:, :], C.maskU[:, :], -1.0, 1.0, ALU.mult, ALU.add, ['maskU'], ['maskLs'])
    C.pp = PsPool(S, n_general)
    C.obank = PsPool(S, 2, prefix='po')
    C.ost = [S.sbuf([128, 4, 128], F32, f"ost{i}") for i in range(2)]
    C.osq = S.sbuf([128, 512], F32, "osq")
    C.oss = S.sbuf([128, 16], F32, "oss")
    return C


def build_B(layer, mixers=('hg', 'rt', 's5', 'gd'), nblk=NBLK):
    nc = bass.Bass("TRN2", target_bir_lowering=False)
    S = Sched(nc)
    C = make_ctx(S, nc)
    ms = []
    if 'hg' in mixers: ms.append(Hgrn(S, nc, C, layer))
    if 'rt' in mixers: ms.append(Ret(S, nc, C))
    if 'gd' in mixers: ms.append(Gdn(S, nc, C))
    if 's5' in mixers: ms.append(S5(S, nc, C))
    for blk in range(nblk):
        for m in ms:
            m.block(blk)
    S.emit()
    return nc


class Rot:
    def __init__(self, S, name, shape, dtype, n=2):
        self.t = [S.sbuf(shape, dtype, f"{name}{i}") for i in range(n)]
        self.name = name; self.i = -1
    def next(self):
        self.i += 1
        return self.cur()
    def cur(self):
        j = self.i % len(self.t)
        return self.t[j], f"{self.name}{j}"


import os
PDT = BF16 if os.environ.get('GD_BF') else F32

class Gdn:
    def __init__(self, S, nc, C):
        self.S, self.C = S, C
        d = lambda n, s: nc.dram_tensor(n, s, F32, kind="ExternalInput").ap()
        self.x = d("gd_x", [384, T + 3]); self.cw = d("gd_cw", [128, 12]); self.ba = d("gd_ba", [64, 2, T // 64]); self.par = d("gd_par", [64, 2])
        self.out = nc.dram_tensor("gd_out", [T, 128], F32, kind="ExternalOutput").ap()
        sb = S.sbuf
        self.xin = Rot(S, "gd_xin", [128, 3, BW + 3], F32, 1)
        self.cws = sb([128, 12], F32, "gd_cws"); self.bas = sb([64, 2, T // 64], F32, "gd_bas"); self.pars = sb([64, 4], F32, "gd_pars")
        self.y = sb([128, 3, BW], F32, "gd_y"); self.sq = sb([128, 2, BW], F32, "gd_sq"); self.rs = sb([128, 2, BW], F32, "gd_rs")
        self.qT = sb([128, BW], BF16, "gd_qT"); self.kT = sb([128, BW], BF16, "gd_kT"); self.kf = sb([128, BW], F32, "gd_kf")
        self.sc = sb([128, 64], F32, "gd_sc")
        self.gl = Rot(S, "gd_gl", [64, 64], F32, 4); self.GL = Rot(S, "gd_GL", [64, 64], F32); self.GT = Rot(S, "gd_GT", [64, 64], F32)
        self.P = Rot(S, "gd_P", [64, 64], PDT, 3); self.PT = Rot(S, "gd_PT", [64, 64], PDT, 3); self.TTf = Rot(S, "gd_TT", [64, 64], F32, 3)
        self.TTb = Rot(S, "gd_TTb", [64, 64], BF16); self.att = Rot(S, "gd_att", [64, 64], BF16)
        self.kbe = Rot(S, "gd_kbe", [64, 128], BF16); self.kd = Rot(S, "gd_kd", [64, 128], BF16); self.bv = Rot(S, "gd_bv", [64, 128], BF16)
        self.wT = Rot(S, "gd_wT", [128, 64], BF16); self.vn = Rot(S, "gd_vn", [64, 128], BF16); self.o1 = Rot(S, "gd_o1", [64, 128], F32)
        self.St = sb([128, 128], F32, "gd_S"); self.Sb = Rot(S, "gd_Sb", [128, 128], BF16)
        self.ost = Rot(S, "gd_ost", [64, 4, 128], F32)
        self.ones = sb([64, 128], F32, "gd_ones"); self.ctmp = sb([64, 16], F32, "gd_ctmp")
        S.op('pool', lambda e: e.memset(self.ones[:], 1.0), writes=['gd_ones'])
        S.op('pool', lambda e: e.memset(self.St[:], 0.0), writes=['gd_S'])
        sbt, sbk = self.Sb.next()
        S.op('pool', lambda e: e.memset(sbt[:], 0.0), writes=[sbk])
        S.dma('sp', self.cws[:, :], self.cw[:, :], writes=['gd_cws'])
        S.dma('sp', self.bas[:, :, :], self.ba[:, :, :], writes=['gd_bas'])
        S.dma('sp', self.pars[:, 0:2], self.par[:, :], writes=['gd_pars'])
        ACTF(S, self.pars[:, 2:3], self.pars[:, 0:1], AF.Exp, ['gd_pars'], ['gd_pars'])
        TS(S, 'dve', self.pars[:, 2:3], self.pars[:, 2:3], -1.0, None, ALU.mult, ALU.bypass, ['gd_pars'], ['gd_pars'])

    def block(self, blk):
        S, C = self.S, self.C
        xin, xk = self.xin.next()
        for m in range(3):
            S.dma('sp', xin[:, m, :], self.x[m * 128:(m + 1) * 128, blk * BW: blk * BW + BW + 3], pwrites=[xk])
        y, sq, rs, qT, kT, kf, sc = self.y, self.sq, self.rs, self.qT, self.kT, self.kf, self.sc
        for m in range(3):
            eng = 'dve'
            TS(S, eng, y[:, m, :], xin[:, m, 0:BW], self.cws[:, 4 * m:4 * m + 1], None, ALU.mult, ALU.bypass, [xk, 'gd_cws'], [f'gd_y{m}'])
            for j in range(1, 4):
                STT(S, eng, y[:, m, :], xin[:, m, j:j + BW], self.cws[:, 4 * m + j:4 * m + j + 1], y[:, m, :], ALU.mult, ALU.add, [xk, 'gd_cws', f'gd_y{m}'], [f'gd_y{m}'])
            ACTF(S, y[:, m, :], y[:, m, :], AF.Silu, [f'gd_y{m}'], [f'gd_y{m}'])
        ACTF(S, sq[:, :, :], y[:, 0:2, :], AF.Square, ['gd_y0', 'gd_y1'], ['gd_sq'])
        for m in range(2):
            ps, pk = C.pp()
            MM(S, ps[:, 0:BW], C.ones128[:, :], sq[:, m, :], ['ones128', 'gd_sq'], [pk])
            ACTF(S, rs[:, m, :], ps[:, 0:BW], AF.Sqrt, [pk, 'epsb'], ['gd_rs'], bias=EPSB[0][:, 0:1])
        S.op('dve', lambda e: e.reciprocal(out=rs[:, :, :], in_=rs[:, :, :]), reads=['gd_rs'], writes=['gd_rs'])
        STT(S, 'dve', qT[:, :], y[:, 0, :], 128 ** -0.5, rs[:, 0, :], ALU.mult, ALU.mult, ['gd_y0', 'gd_rs'], ['gd_qT'])
        TT(S, 'dve', kf[:, :], y[:, 1, :], rs[:, 1, :], ALU.mult, ['gd_y1', 'gd_rs'], ['gd_kf'])
        COPY(S, 'act', kT[:, :], kf[:, :], ['gd_kf'], ['gd_kT'])
        c0 = blk * NCH
        ACTF(S, sc[0:64, 0:8], self.bas[:, 0, c0:c0 + 8], AF.Sigmoid, ['gd_bas'], ['gd_sc'])
        ACTF(S, sc[0:64, 8:16], self.bas[:, 1, c0:c0 + 8], AF.Exp, ['gd_bas', 'gd_pars'], ['gd_sc'], bias=self.pars[:, 1:2])
        ACTF(S, sc[0:64, 8:16], sc[0:64, 8:16], AF.Ln, ['gd_sc'], ['gd_sc'], bias=1.0)
        TS(S, 'dve', sc[0:64, 8:16], sc[0:64, 8:16], self.pars[:, 2:3], None, ALU.mult, ALU.bypass, ['gd_sc', 'gd_pars'], ['gd_sc'])
        TS(S, 'dve', sc[0:64, 16:24], sc[0:64, 8:16], -1.0, None, ALU.mult, ALU.bypass, ['gd_sc'], ['gd_sc'])
        TS(S, 'dve', sc[0:64, 56:64], sc[0:64, 0:8], -1.0, None, ALU.mult, ALU.bypass, ['gd_sc'], ['gd_sc'])
        pc, pck = C.pp()
        MM(S, pc[0:64, 0:8], C.maskU[0:64, :], sc[0:64, 8:16], ['maskU', 'gd_sc'], [pck])
        MM(S, pc[:, 8:16], self.ones[:, :], sc[0:64, 8:16], ['gd_ones', 'gd_sc'], [pck])
        ACTF(S, sc[0:64, 24:32], pc[0:64, 0:8], AF.Exp, [pck], ['gd_sc'])
        ACTF(S, sc[:, 48:56], pc[:, 8:16], AF.Exp, [pck], ['gd_sc'])
        COPY(S, 'act', self.ctmp[0:64, 0:16], pc[0:64, 0:16], [pck], ['gd_ctmp'])
        TT(S, 'dve', sc[0:64, 32:40], self.ctmp[0:64, 8:16], self.ctmp[0:64, 0:8], ALU.subtract, ['gd_ctmp'], ['gd_sc'])
        ACTF(S, sc[0:64, 32:40], sc[0:64, 32:40], AF.Exp, ['gd_sc'], ['gd_sc'])
        TT(S, 'dve', sc[0:64, 40:48], sc[0:64, 0:8], sc[0:64, 24:32], ALU.mult, ['gd_sc'], ['gd_sc'])
        import os
        stop = int(os.environ.get('GD_STOP', '99'))
        if stop == 0:
            return
        for c in range(NCH):
            cs = slice(c * CH, (c + 1) * CH)
            if c % 4 == 0:
                ost, osk = self.ost.next()
            pt, ptk = C.pp()
            TR(S, pt[0:64, 0:128], kf[:, cs], C.ident[:, :], ['gd_kf', 'ident'], [ptk])
            kbe, kbek = self.kbe.next(); kd, kdk = self.kd.next(); bv, bvk = self.bv.next()
            TS(S, 'dve', kbe[:, :], pt[0:64, 0:128], sc[0:64, 40 + c:41 + c], None, ALU.mult, ALU.bypass, [ptk, 'gd_sc'], [kbek])
            TS(S, 'dve', kd[:, :], pt[0:64, 0:128], sc[0:64, 32 + c:33 + c], None, ALU.mult, ALU.bypass, [ptk, 'gd_sc'], [kdk])
            pv, pvk = C.pp()
            TR(S, pv[0:64, 0:128], y[:, 2, cs], C.ident[:, :], ['gd_y2', 'ident'], [pvk])
            TS(S, 'dve', bv[:, :], pv[0:64, 0:128], sc[0:64, c:c + 1], None, ALU.mult, ALU.bypass, [pvk, 'gd_sc'], [bvk])
            if stop == 1:
                continue
            gl, glk = self.gl.next()
            TS(S, 'pool', gl[:, :], C.maskU[0:64, :], sc[0:64, 16 + c:17 + c], sc[0:64, 8 + c:9 + c], ALU.mult, ALU.add, ['maskU', 'gd_sc'], [glk])
            pd, pdk = C.pp()
            MM(S, pd[0:64, 0:64], C.maskU[0:64, :], gl[:, :], ['maskU', glk], [pdk])
            MM(S, pd[0:64, 64:128], gl[:, :], C.maskU[0:64, :], ['maskU', glk], [pdk])
            GL, GLk = self.GL.next(); GT, GTk = self.GT.next()
            ACTF(S, GL[:, :], pd[0:64, 0:64], AF.Exp, [pdk], [GLk])
            ACTF(S, GT[:, :], pd[0:64, 64:128], AF.Exp, [pdk], [GTk])
            TT(S, 'pool', GL[:, :], GL[:, :], C.maskLs[0:64, :], ALU.mult, [GLk, 'maskLs'], [GLk])
            TT(S, 'pool', GT[:, :], GT[:, :], C.maskU[0:64, :], ALU.mult, [GTk, 'maskU'], [GTk])
            if stop == 2:
                continue
            pg, pgk = C.pp()
            MM(S, pg[0:64, 0:64], kT[:, cs], kT[:, cs], ['gd_kT'], [pgk])
            MM(S, pg[0:64, 64:128], kT[:, cs], qT[:, cs], ['gd_kT', 'gd_qT'], [pgk])
            if stop == 30: continue
            att, attk = self.att.next()
            TT(S, 'dve', att[:, :], pg[0:64, 64:128], GT[:, :], ALU.mult, [pgk, GTk], [attk])
            if stop == 31: continue
            P, Pk = self.P.next(); PT, PTk = self.PT.next(); TTf, TTk = self.TTf.next()
            STT(S, 'dve', P[:, :], pg[0:64, 0:64], sc[0:64, 56 + c:57 + c], GL[:, :], ALU.mult, ALU.mult, [pgk, 'gd_sc', GLk], [Pk])
            if stop == 32: continue
            pp_, ppk = C.pp()
            TR(S, pp_[0:64, 0:64], P[:, :], C.ident[0:64, 0:64], [Pk, 'ident'], [ppk])
            if stop == 33: continue
            COPY(S, 'act', PT[:, :], pp_[0:64, 0:64], [ppk], [PTk])
            if stop == 34: continue
            TT(S, 'dve', TTf[:, :], PT[:, :], C.ident[0:64, 0:64], ALU.add, [PTk, 'ident'], [TTk])
            for it in range(int(os.environ.get('GD_IT', '5'))):
                pq, pqk = C.pp()
                MM(S, pq[0:64, 0:64], PT[:, :], P[:, :], [PTk, Pk], [pqk])
                P2, P2k = self.P.next()
                COPY(S, 'dve', P2[:, :], pq[0:64, 0:64], [pqk], [P2k])
                if it < 4:
                    pq2, pq2k = C.pp()
                    MM(S, pq2[0:64, 0:64], P[:, :], PT[:, :], [PTk, Pk], [pq2k])
                    PT2, PT2k = self.PT.next()
                    COPY(S, 'dve', PT2[:, :], pq2[0:64, 0:64], [pq2k], [PT2k])
                if os.environ.get('GD_SKIPT'):
                    P, Pk = P2, P2k
                    if it < 4:
                        PT, PTk = PT2, PT2k
                    continue
                if PDT == BF16:
                    TTq, TTqk = self.TTb.next()
                    COPY(S, 'act', TTq[:, :], TTf[:, :], [TTk], [TTqk])
                pu, puk = C.pp()
                MM(S, pu[0:64, 0:64], P2[:, :], (TTq if PDT == BF16 else TTf)[:, :], [P2k, TTk], [puk])
                TT2, TT2k = self.TTf.next()
                TT(S, 'dve', TT2[:, :], pu[0:64, 0:64], TTf[:, :], ALU.add, [puk, TTk], [TT2k])
                P, Pk = P2, P2k
                if it < 4:
                    PT, PTk = PT2, PT2k
                TTf, TTk = TT2, TT2k
            if stop == 3:
                continue
            TTb, TTbk = self.TTb.next()
            COPY(S, 'act', TTb[:, :], TTf[:, :], [TTk], [TTbk])
            pw, pwk = C.pp()
            MM(S, pw[:, 0:64], kbe[:, :], TTb[:, :], [kbek, TTbk], [pwk])
            wT, wTk = self.wT.next()
            ACTF(S, wT[:, :], pw[:, 0:64], AF.Copy, [pwk], [wTk], scale=-1.0)
            Sb, Sbk = self.Sb.cur()
            pn, pnk = C.pp()
            MM(S, pn[0:64, 0:128], TTb[:, :], bv[:, :], [TTbk, bvk], [pnk], start=True, stop=False)
            MM(S, pn[0:64, 0:128], wT[:, :], Sb[:, :], [wTk, Sbk], [pnk], start=False, stop=True)
            vn, vnk = self.vn.next()
            COPY(S, 'act', vn[:, :], pn[0:64, 0:128], [pnk], [vnk])
            po, pok = C.pp()
            MM(S, po[0:64, 0:128], qT[:, cs], Sb[:, :], ['gd_qT', Sbk], [pok])
            o1, o1k = self.o1.next()
            TS(S, 'dve', o1[:, :], po[0:64, 0:128], sc[0:64, 24 + c:25 + c], None, ALU.mult, ALU.bypass, [pok, 'gd_sc'], [o1k])
            po2, po2k = C.pp()
            MM(S, po2[0:64, 0:128], att[:, :], vn[:, :], [attk, vnk], [po2k])
            TT(S, 'dve', ost[:, c % 4, :], po2[0:64, 0:128], o1[:, :], ALU.add, [po2k, o1k], [osk])
            ps_, psk = C.pp()
            MM(S, ps_[:, 0:128], kd[:, :], vn[:, :], [kdk, vnk], [psk])
            STT(S, 'dve', self.St[:, :], self.St[:, :], sc[:, 48 + c:49 + c], ps_[:, 0:128], ALU.mult, ALU.add, ['gd_S', 'gd_sc', psk], ['gd_S'])
            Sb2, Sb2k = self.Sb.next()
            COPY(S, 'act', Sb2[:, :], self.St[:, :], ['gd_S'], [Sb2k])
            if c % 4 == 3:
                norm_out(S, C, 'gd_', ost[:, :, :].rearrange("p c e -> p (c e)"), osk, c // 4, blk, self.out, center=False)


HALF_PI = 1.5707963267948966


def cplx_unit(S, pfx, theta, c, s, tmp, shape_sl, keys_r, piT, keys=None):
    kc, ks, kt = keys if keys is not None else (pfx + 'c', pfx + 's', pfx + 't')
    ACTF(S, s, theta, AF.Sin, keys_r, [ks], scale=1.0 / 16)
    ACTF(S, c, theta, AF.Sin, keys_r + ['halfpi'], [kc], scale=1.0 / 16, bias=piT)
    for _ in range(4):
        TT(S, 'dve', tmp, c, s, ALU.mult, [kc, ks], [kt])
        TT(S, 'dve', c, c, c, ALU.mult, [kc], [kc])
        TT(S, 'dve', s, s, s, ALU.mult, [ks], [ks])
        TT(S, 'dve', c, c, s, ALU.subtract, [kc, ks], [kc])
        TS(S, 'dve', s, tmp, 2.0, None, ALU.mult, ALU.bypass, [kt], [ks])


class S5:
    def __init__(self, S, nc, C):
        self.S, self.C = S, C
        d = lambda n, s: nc.dram_tensor(n, s, F32, kind="ExternalInput").ap()
        self.uT = d("s5_uT", [128, T])
        self.pP = d("s5_pP", [128, 3, 4])
        self.pF = d("s5_pF", [128, 3, 512])
        self.bF = d("s5_bF", [128, 2, 512])
        self.cP = d("s5_cP", [128, 2, 512])
        self.out = nc.dram_tensor("s5_out", [128, T], F32, kind="ExternalOutput").ap()
        sb = S.sbuf
        self.halfpi = sb([128, 1], F32, "s5_halfpi")
        S.op('pool', lambda e: e.memset(self.halfpi[:], HALF_PI), writes=['halfpi'])
        pP = sb([128, 3, 4], F32, "s5_pPs"); pF = sb([128, 3, 512], F32, "s5_pFs"); bF = sb([128, 2, 512], F32, "s5_bFs"); cP = sb([128, 2, 512], F32, "s5_cPs")
        S.dma('sp', pP[:, :, :], self.pP[:, :, :], writes=['s5_pP']); S.dma('sp', pF[:, :, :], self.pF[:, :, :], writes=['s5_pF'])
        S.dma('sp', bF[:, :, :], self.bF[:, :, :], writes=['s5_bF']); S.dma('sp', cP[:, :, :], self.cP[:, :, :], writes=['s5_cP'])
        W = 512
        self.w = [sb([128, BW], F32, f"s5_w{i}") for i in range(4)]
        self.z = [sb([128, BW], F32, f"s5_z{i}") for i in range(2)]
        self.xf = [sb([128, BW], F32, f"s5_xf{i}") for i in range(2)]
        dtF, magF, thF, cF, sF, tF, t2F = self.w[0], self.w[1], self.w[2], self.w[3], self.z[0], self.z[1], self.xf[0]
        ACTF(S, dtF[:, :], pF[:, 2, :], AF.Exp, ['s5_pF'], ['s5_w0'])
        TT(S, 'dve', thF[:, :], dtF[:, :], pF[:, 1, :], ALU.mult, ['s5_w0', 's5_pF'], ['s5_w2'])
        TT(S, 'dve', magF[:, :], dtF[:, :], pF[:, 0, :], ALU.mult, ['s5_w0', 's5_pF'], ['s5_w1'])
        ACTF(S, magF[:, :], magF[:, :], AF.Exp, ['s5_w1'], ['s5_w1'])
        cplx_unit(S, 's5F_', thF[:, :], cF[:, :], sF[:, :], tF[:, :], None, ['s5_w2'], self.halfpi[:, 0:1], keys=('s5_w3', 's5_z0', 's5_z1'))
        kc, ks = 's5_w3', 's5_z0'
        TT(S, 'dve', cF[:, :], cF[:, :], magF[:, :], ALU.mult, [kc, 's5_w1'], [kc])
        TS(S, 'dve', cF[:, :], cF[:, :], -1.0, None, ALU.add, ALU.bypass, [kc], [kc])
        TT(S, 'dve', sF[:, :], sF[:, :], magF[:, :], ALU.mult, [ks, 's5_w1'], [ks])
        TT(S, 'dve', dtF[:, :], pF[:, 0, :], pF[:, 0, :], ALU.mult, ['s5_pF'], ['s5_w0'])
        TT(S, 'dve', tF[:, :], pF[:, 1, :], pF[:, 1, :], ALU.mult, ['s5_pF'], ['s5_z1'])
        TT(S, 'dve', dtF[:, :], dtF[:, :], tF[:, :], ALU.add, ['s5_w0', 's5_z1'], ['s5_w0'])
        S.op('dve', lambda e: e.reciprocal(out=dtF[:, :], in_=dtF[:, :]), reads=['s5_w0'], writes=['s5_w0'])
        TT(S, 'dve', tF[:, :], cF[:, :], pF[:, 0, :], ALU.mult, [kc, 's5_pF'], ['s5_z1'])
        TT(S, 'dve', t2F[:, :], sF[:, :], pF[:, 1, :], ALU.mult, [ks, 's5_pF'], ['s5_xf0'])
        TT(S, 'dve', thF[:, :], tF[:, :], t2F[:, :], ALU.add, ['s5_z1', 's5_xf0'], ['s5_w2'])
        TT(S, 'dve', thF[:, :], thF[:, :], dtF[:, :], ALU.mult, ['s5_w2', 's5_w0'], ['s5_w2'])
        TT(S, 'dve', tF[:, :], sF[:, :], pF[:, 0, :], ALU.mult, [ks, 's5_pF'], ['s5_z1'])
        TT(S, 'dve', t2F[:, :], cF[:, :], pF[:, 1, :], ALU.mult, [kc, 's5_pF'], ['s5_xf0'])
        TT(S, 'dve', magF[:, :], tF[:, :], t2F[:, :], ALU.subtract, ['s5_z1', 's5_xf0'], ['s5_w1'])
        TT(S, 'dve', magF[:, :], magF[:, :], dtF[:, :], ALU.mult, ['s5_w1', 's5_w0'], ['s5_w1'])
        self.Bre = sb([128, 512], BF16, "s5_Bre"); self.Bim = sb([128, 512], BF16, "s5_Bim")
        TT(S, 'dve', tF[:, :], thF[:, :], bF[:, 0, :], ALU.mult, ['s5_w2', 's5_bF'], ['s5_z1'])
        TT(S, 'dve', t2F[:, :], magF[:, :], bF[:, 1, :], ALU.mult, ['s5_w1', 's5_bF'], ['s5_xf0'])
        TT(S, 'dve', self.Bre[:, :], tF[:, :], t2F[:, :], ALU.subtract, ['s5_z1', 's5_xf0'], ['s5_Bre'])
        TT(S, 'dve', tF[:, :], thF[:, :], bF[:, 1, :], ALU.mult, ['s5_w2', 's5_bF'], ['s5_z1'])
        TT(S, 'dve', t2F[:, :], magF[:, :], bF[:, 0, :], ALU.mult, ['s5_w1', 's5_bF'], ['s5_xf0'])
        TT(S, 'dve', self.Bim[:, :], tF[:, :], t2F[:, :], ALU.add, ['s5_z1', 's5_xf0'], ['s5_Bim'])
        self.Cre = sb([128, 512], BF16, "s5_Cre"); self.Cim = sb([128, 512], BF16, "s5_Cim")
        COPY(S, 'act', self.Cre[:, :], cP[:, 0, :], ['s5_cP'], ['s5_Cre'])
        ACTF(S, self.Cim[:, :], cP[:, 1, :], AF.Copy, ['s5_cP'], ['s5_Cim'], scale=-1.0)
        dtP = sb([128, 4], F32, "s5_dtP"); self.r = sb([128, 4], F32, "s5_r"); thP = sb([128, 4], F32, "s5_thP")
        self.E = sb([128, 3, 4], F32, "s5_E")
        self.c1 = sb([128, 2, 4], F32, "s5_c1")
        tP = sb([128, 4], F32, "s5_tP")
        ACTF(S, dtP[:, :], pP[:, 2, :], AF.Exp, ['s5_pP'], ['s5_dtP'])
        TT(S, 'dve', thP[:, :], dtP[:, :], pP[:, 1, :], ALU.mult, ['s5_dtP', 's5_pP'], ['s5_thP'])
        TT(S, 'dve', self.r[:, :], dtP[:, :], pP[:, 0, :], ALU.mult, ['s5_dtP', 's5_pP'], ['s5_r'])
        ACTF(S, self.r[:, :], self.r[:, :], AF.Exp, ['s5_r'], ['s5_r'])
        cplx_unit(S, 's5P_', thP[:, :], self.E[:, 0, :], self.E[:, 1, :], tP[:, :], None, ['s5_thP'], self.halfpi[:, 0:1])
        COPY(S, 'dve', self.c1[:, 0, :], self.E[:, 0, :], ['s5P_c'], ['s5_c1'])
        COPY(S, 'dve', self.c1[:, 1, :], self.E[:, 1, :], ['s5P_s'], ['s5_c1'])
        self.cr = sb([128, 4, 512], F32, "s5_cr"); self.ci = sb([128, 4, 512], F32, "s5_ci"); self.rt = sb([128, 4, 512], F32, "s5_rt")
        tt = sb([128, 256], F32, "s5_tt")
        S.op('pool', lambda e: e.memset(self.cr[:, :, 0:1], 1.0), writes=['s5_cr'])
        S.op('pool', lambda e: e.memset(self.ci[:, :, 0:1], 0.0), writes=['s5_ci'])
        S.op('pool', lambda e: e.memset(self.rt[:, :, :], 1.0), writes=['s5_rt'])
        for q in range(4):
            TS(S, 'pool', self.rt[:, q, :], self.rt[:, q, :], self.r[:, q:q + 1], None, ALU.mult, ALU.bypass, ['s5_rt', 's5_r'], ['s5_rt'])
        for k in range(9):
            w = 1 << k
            TS(S, 'dve', self.E[:, 2, :], self.E[:, 1, :], -1.0, None, ALU.mult, ALU.bypass, ['s5P_s'], ['s5_En'])
            for q in range(4):
                Ec, Es, En = self.E[:, 0, q:q + 1], self.E[:, 1, q:q + 1], self.E[:, 2, q:q + 1]
                TS(S, 'dve', tt[:, 0:w], self.cr[:, q, 0:w], Ec, None, ALU.mult, ALU.bypass, ['s5_cr', 's5P_c'], ['s5_tt'])
                STT(S, 'dve', self.cr[:, q, w:2 * w], self.ci[:, q, 0:w], En, tt[:, 0:w], ALU.mult, ALU.add, ['s5_ci', 's5_En', 's5_tt', 's5_cr'], ['s5_cr'])
                TS(S, 'dve', tt[:, 0:w], self.ci[:, q, 0:w], Ec, None, ALU.mult, ALU.bypass, ['s5_ci', 's5P_c'], ['s5_tt'])
                STT(S, 'dve', self.ci[:, q, w:2 * w], self.cr[:, q, 0:w], Es, tt[:, 0:w], ALU.mult, ALU.add, ['s5_cr', 's5P_s', 's5_tt', 's5_ci'], ['s5_ci'])
            TT(S, 'dve', tP[:, :], self.E[:, 0, :], self.E[:, 1, :], ALU.mult, ['s5P_c', 's5P_s'], ['s5P_t'])
            TT(S, 'dve', self.E[:, 0, :], self.E[:, 0, :], self.E[:, 0, :], ALU.mult, ['s5P_c'], ['s5P_c'])
            TT(S, 'dve', self.E[:, 1, :], self.E[:, 1, :], self.E[:, 1, :], ALU.mult, ['s5P_s'], ['s5P_s'])
            TT(S, 'dve', self.E[:, 0, :], self.E[:, 0, :], self.E[:, 1, :], ALU.subtract, ['s5P_c', 's5P_s'], ['s5P_c'])
            TS(S, 'dve', self.E[:, 1, :], tP[:, :], 2.0, None, ALU.mult, ALU.bypass, ['s5P_t'], ['s5P_s'])
        self.zi = sb([128, 2, 4], F32, "s5_zi")
        S.op('pool', lambda e: e.memset(self.zi[:, :, :], 0.0), writes=['s5_zi'])
        self.ub = Rot(S, "s5_ub", [128, BW], BF16)
        self.bre = Rot(S, "s5_bre", [128, BW], F32); self.bim = Rot(S, "s5_bim", [128, BW], F32)
        self.xb = [Rot(S, f"s5_xb{i}", [128, BW], BF16) for i in range(2)]
        self.yst = Rot(S, "s5_yst", [128, BW], F32)
        self.xe = sb([128, 4], F32, "s5_xe")

    def block(self, blk):
        S, C = self.S, self.C
        sl = slice(blk * BW, (blk + 1) * BW)
        ub, ubk = self.ub.next()
        S.dma('pool', ub[:, :], self.uT[:, sl], writes=[ubk])
        y_ps, yk = C.obank()
        w, z, xf = self.w, self.z, self.xf
        for q in range(4):
            qs = slice(q * 128, (q + 1) * 128)
            p1, p1k = C.pp(); p2, p2k = C.pp()
            MM(S, p1[:, 0:BW], self.Bre[:, qs], ub[:, :], ['s5_Bre', ubk], [p1k])
            MM(S, p2[:, 0:BW], self.Bim[:, qs], ub[:, :], ['s5_Bim', ubk], [p2k])
            bre, brek = self.bre.next(); bim, bimk = self.bim.next()
            COPY(S, 'act', bre[:, :], p1[:, 0:BW], [p1k], [brek])
            COPY(S, 'act', bim[:, :], p2[:, 0:BW], [p2k], [bimk])
            cr, ci = self.cr[:, q, :], self.ci[:, q, :]
            TT(S, 'pool', w[0][:, :], bre[:, :], cr, ALU.mult, [brek, 's5_cr'], ['s5_w0'])
            TT(S, 'pool', w[1][:, :], bim[:, :], ci, ALU.mult, [bimk, 's5_ci'], ['s5_w1'])
            TT(S, 'pool', w[0][:, :], w[0][:, :], w[1][:, :], ALU.add, ['s5_w0', 's5_w1'], ['s5_w0'])
            TT(S, 'pool', w[2][:, :], bim[:, :], cr, ALU.mult, [bimk, 's5_cr'], ['s5_w2'])
            TT(S, 'pool', w[3][:, :], bre[:, :], ci, ALU.mult, [brek, 's5_ci'], ['s5_w3'])
            TT(S, 'pool', w[2][:, :], w[2][:, :], w[3][:, :], ALU.subtract, ['s5_w2', 's5_w3'], ['s5_w2'])
            S.op('dve', lambda e, q=q: e.tensor_tensor_scan(out=z[0][:, :], data0=self.rt[:, q, :], data1=w[0][:, :], initial=self.zi[:, 0, q:q + 1], op0=ALU.mult, op1=ALU.add),
                 reads=['s5_rt', 's5_w0', 's5_zi'], writes=['s5_z0'])
            S.op('dve', lambda e, q=q: e.tensor_tensor_scan(out=z[1][:, :], data0=self.rt[:, q, :], data1=w[2][:, :], initial=self.zi[:, 1, q:q + 1], op0=ALU.mult, op1=ALU.add),
                 reads=['s5_rt', 's5_w2', 's5_zi'], writes=['s5_z1'])
            TT(S, 'pool', w[1][:, :], z[0][:, :], cr, ALU.mult, ['s5_z0', 's5_cr'], ['s5_w1'])
            TT(S, 'pool', w[3][:, :], z[1][:, :], ci, ALU.mult, ['s5_z1', 's5_ci'], ['s5_w3'])
            TT(S, 'pool', xf[0][:, :], w[1][:, :], w[3][:, :], ALU.subtract, ['s5_w1', 's5_w3'], ['s5_xf0'])
            TT(S, 'pool', w[1][:, :], z[1][:, :], cr, ALU.mult, ['s5_z1', 's5_cr'], ['s5_w1'])
            TT(S, 'pool', w[3][:, :], z[0][:, :], ci, ALU.mult, ['s5_z0', 's5_ci'], ['s5_w3'])
            TT(S, 'pool', xf[1][:, :], w[1][:, :], w[3][:, :], ALU.add, ['s5_w1', 's5_w3'], ['s5_xf1'])
            xb0, xb0k = self.xb[0].next(); xb1, xb1k = self.xb[1].next()
            COPY(S, 'act', xb0[:, :], xf[0][:, :], ['s5_xf0'], [xb0k])
            COPY(S, 'act', xb1[:, :], xf[1][:, :], ['s5_xf1'], [xb1k])
            c1, s1 = self.c1[:, 0, q:q + 1], self.c1[:, 1, q:q + 1]
            xe = self.xe
            TS(S, 'dve', xe[:, 0:1], xf[0][:, BW - 1:BW], c1, None, ALU.mult, ALU.bypass, ['s5_xf0', 's5_c1'], ['s5_xe'])
            TS(S, 'dve', xe[:, 1:2], xf[1][:, BW - 1:BW], s1, None, ALU.mult, ALU.bypass, ['s5_xf1', 's5_c1'], ['s5_xe'])
            TT(S, 'dve', self.zi[:, 0, q:q + 1], xe[:, 0:1], xe[:, 1:2], ALU.subtract, ['s5_xe'], ['s5_zi'])
            TS(S, 'dve', xe[:, 2:3], xf[1][:, BW - 1:BW], c1, None, ALU.mult, ALU.bypass, ['s5_xf1', 's5_c1'], ['s5_xe'])
            TS(S, 'dve', xe[:, 3:4], xf[0][:, BW - 1:BW], s1, None, ALU.mult, ALU.bypass, ['s5_xf0', 's5_c1'], ['s5_xe'])
            TT(S, 'dve', self.zi[:, 1, q:q + 1], xe[:, 2:3], xe[:, 3:4], ALU.add, ['s5_xe'], ['s5_zi'])
            MM(S, y_ps[:, 0:BW], self.Cre[:, qs], xb0[:, :], ['s5_Cre', xb0k], [yk], start=(q == 0), stop=False)
            MM(S, y_ps[:, 0:BW], self.Cim[:, qs], xb1[:, :], ['s5_Cim', xb1k], [yk], start=False, stop=(q == 3))
        yst, ystk = self.yst.next()
        COPY(S, 'act', yst[:, :], y_ps[:, 0:BW], [yk], [ystk])
        S.dma('sp', self.out[:, sl], yst[:, :], reads=[ystk])


PW = 344
NPASS = 6
NTILE = 1
DFF = 2816
NHB = 22


def rmsnorm_pass(S, x_sb, g_sb, h_sb, ones, pp, sq, rstd, xkey, hkey, gkey):
    for tt in range(NTILE):
        sl = slice(tt * TW, (tt + 1) * TW)
        S.op('act', lambda e, sl=sl: e.activation(out=sq[:, :, :], in_=x_sb[:, :, sl], func=AF.Square), reads=[xkey], writes=['sq'])
        ps, pk = pp()
        for k in range(8):
            MM(S, ps[:, 0:TW], ones[:, :], sq[:, k, :], ['sq', 'ones'], [pk], start=(k == 0), stop=(k == 7))
        ACTF(S, rstd[:, :], ps[:, 0:TW], AF.Sqrt, [pk, 'epsb'], ['rstd'], scale=1.0 / 1024, bias=EPSB[0][:, 0:1])
        S.op('dve', lambda e: e.reciprocal(out=rstd[:, :], in_=rstd[:, :]), reads=['rstd'], writes=['rstd'])
        for k in range(8):
            STT(S, 'dve', h_sb[:, k, sl], x_sb[:, k, sl], g_sb[:, k:k + 1], rstd[:, :], ALU.mult, ALU.mult, [xkey, gkey, 'rstd'], [hkey])


def proj_pass(S, rhs_sb, rkey, w_dram, ncols, kchunks, wbufs, pp, evac, wname, colw=512):
    nblk = (ncols + colw - 1) // colw
    for cbk in range(nblk):
        c0 = cbk * colw
        w = min(colw, ncols - c0)
        i = proj_pass.cnt % len(wbufs)
        proj_pass.cnt += 1
        wt = wbufs[i][:, 0:kchunks * colw].rearrange("p (k c) -> p k c", c=colw)
        wk = f'wb{i}'
        S.dma('pool', wt[:, :, 0:w], w_dram[:, c0:c0 + w].rearrange("(k p) c -> p k c", p=128), writes=[wk])
        for sub in range(w // 128):
            for tt in range(NTILE):
                sl = slice(tt * TW, (tt + 1) * TW)
                ps, pk = pp()
                for k in range(kchunks):
                    MM(S, ps[:, 0:TW], wt[:, k, sub * 128:(sub + 1) * 128], rhs_sb[:, k, sl], [wk, rkey], [pk], start=(k == 0), stop=(k == kchunks - 1))
                evac(cbk * (colw // 128) + sub, tt, ps, pk)
proj_pass.cnt = 0


def build_C(final=False):
    nc = bass.Bass("TRN2", target_bir_lowering=False)
    din = lambda n, s: nc.dram_tensor(n, s, F32, kind="ExternalInput").ap()
    xT = din("xT", [1024, NT]); gm = din("gmix", [128, 8]); gf = din("gffn", [128, 8]); gfin = din("gfin", [128, 8])
    wG = din("wG", [1024, 1536])
    wGate = din("wGate", [1024, 4096])
    mixT = din("mixT", [4 * 512, NT])
    uT = din("uT", [512, NT])
    gains = din("gains", [128, 4, 4])
    wglu = din("wglu", [512, 512]); wbr = din("wbr", [4, 512, 1024]); wout = din("wout", [1024, 1024])
    wgu = din("wgu", [1024, 2 * DFF])
    wdn = din("wdn", [DFF, 1024])
    xo = nc.dram_tensor("xo", [1024, NT], F32, kind="ExternalOutput").ap()
    S = Sched(nc)
    sb = S.sbuf
    ones = load_consts_dense(S, nc)
    make_epsb(S)
    pp = PsPool(S, 8)
    x_sb = sb([128, 8, PW], F32, "x_sb"); h_sb = sb([128, 8, PW], BF16, "h_sb")
    g1 = sb([128, 8], F32, "g1"); g2 = sb([128, 8], F32, "g2"); g3 = sb([128, 8], F32, "g3"); gn = sb([128, 4, 4], F32, "gn")
    sq = sb([128, 8, TW], F32, "sq"); rstd = sb([128, TW], F32, "rstd")
    wbufs = [sb([128, 8 * 512], BF16, f"wbuf{i}") for i in range(3)]
    br = [sb([128, 4, PW], BF16, f"br{m}") for m in range(4)]
    yg = sb([128, 4, PW], F32, "yg"); ygb = sb([128, 4, PW], BF16, "ygb")
    acc = sb([128, 8, PW], F32, "acc"); mg = sb([128, 8, PW], BF16, "mg")
    hid = sb([128, NHB, PW], BF16, "hid")
    stg = [sb([128, PW], F32, f"stg{i}") for i in range(4)]
    tmp = [sb([128, TW], F32, f"tmp{i}") for i in range(4)]
    wbrs = [sb([128, 4, 1024], BF16, f"wbrs{i}") for i in range(1)]
    S.dma('sp', g1[:, :], gm[:, :], writes=['g1']); S.dma('sp', g2[:, :], gf[:, :], writes=['g2']); S.dma('sp', g3[:, :], gfin[:, :], writes=['g3'])
    S.dma('sp', gn[:, :, :], gains[:, :, :], writes=['gn'])
    cnt = [0]
    def stage(i):
        return stg[i % 4], f'stg{i % 4}'
    for ps_ in range(NPASS):
        t0 = ps_ * PW
        tsl = slice(t0, t0 + PW)
        for k in range(8):
            S.dma('sp', x_sb[:, k, :], xT[k * 128:(k + 1) * 128, tsl], pwrites=['x'])
        rmsnorm_pass(S, x_sb, g1, h_sb, ones, pp, sq, rstd, 'x', 'h', 'g1')
        def evac_g(cb, tt, ps, pk):
            m3, kb = cb // 4, cb % 4
            m = (0, 1, 3)[m3]
            sl = slice(tt * TW, (tt + 1) * TW)
            if tt == 0:
                st, sk = stage(cnt[0]); cnt[0] += 1
                evac_g.cur = (st, sk)
                S.dma('sp', st[:, :], mixT[m * 512 + kb * 128: m * 512 + (kb + 1) * 128, tsl], writes=[sk])
            st, sk = evac_g.cur
            tp, tk = tmp[cnt[0] % 2], f'tmp{cnt[0] % 2}'; cnt[0] += 1
            ACTF(S, tp[:, :], ps[:, 0:TW], AF.Silu, [pk], [tk])
            STT(S, 'dve', br[m][:, kb, sl], st[:, sl], gn[:, m, kb:kb + 1], tp[:, :], ALU.mult, ALU.mult, [sk, 'gn', tk], [f'br{m}'])
        proj_pass(S, h_sb, 'h', wG, 1536, 8, wbufs, pp, evac_g, 'w')
        for kb in range(4):
            st, sk = stage(cnt[0]); cnt[0] += 1
            st2, sk2 = stage(cnt[0]); cnt[0] += 1
            S.dma('sp', st[:, :], mixT[2 * 512 + kb * 128: 2 * 512 + (kb + 1) * 128, tsl], writes=[sk])
            S.dma('sp', st2[:, :], uT[kb * 128:(kb + 1) * 128, tsl], writes=[sk2])
            STT(S, 'dve', st[:, :], st2[:, :], gn[:, 2, kb:kb + 1], st[:, :], ALU.mult, ALU.add, [sk, sk2, 'gn'], [sk])
            TT(S, 'pool', st2[:, :], st[:, :], st[:, :], ALU.mult, [sk], [sk2])
            TS(S, 'pool', st2[:, :], st2[:, :], 0.044715, 1.0, ALU.mult, ALU.add, [sk2], [sk2])
            TT(S, 'pool', st2[:, :], st2[:, :], st[:, :], ALU.mult, [sk, sk2], [sk2])
            ACTF(S, st2[:, :], st2[:, :], AF.Sigmoid, [sk2], [sk2], scale=1.5957691216057308)
            TT(S, 'dve', yg[:, kb, :], st[:, :], st2[:, :], ALU.mult, [sk, sk2], ['yg'])
            COPY(S, 'act', ygb[:, kb, :], yg[:, kb, :], ['yg'], ['ygb'])
        def evac_glu(cb, tt, ps, pk):
            sl = slice(tt * TW, (tt + 1) * TW)
            tp, tk = tmp[cnt[0] % 2], f'tmp{cnt[0] % 2}'; cnt[0] += 1
            ACTF(S, tp[:, :], ps[:, 0:TW], AF.Sigmoid, [pk], [tk])
            TT(S, 'dve', br[2][:, cb, sl], yg[:, cb, sl], tp[:, :], ALU.mult, ['yg', tk], ['br2'])
        proj_pass(S, ygb, 'ygb', wglu, 512, 4, wbufs, pp, evac_glu, 'w')
        for m in range(4):
            wb, wbk = wbrs[0], 'wbrs0'
            S.dma('pool', wb[:, :, :], wbr[m].rearrange("(k p) c -> p k c", p=128), writes=[wbk])
            for half in range(2):
                i = proj_pass.cnt % 3; proj_pass.cnt += 1
                wt = wbufs[i][:, :].rearrange("p (k c) -> p k c", c=512); wk = f'wb{i}'
                c0 = m * 1024 + half * 512
                S.dma('pool', wt[:, :, :], wGate[:, c0:c0 + 512].rearrange("(k p) c -> p k c", p=128), writes=[wk])
                for sub in range(4):
                    ob = half * 4 + sub
                    for tt in range(NTILE):
                        sl = slice(tt * TW, (tt + 1) * TW)
                        pg, pgk = pp()
                        for k in range(8):
                            MM(S, pg[:, 0:TW], wt[:, k, sub * 128:(sub + 1) * 128], h_sb[:, k, sl], [wk, 'h'], [pgk], start=(k == 0), stop=(k == 7))
                        pb, pbk = pp()
                        for k in range(4):
                            MM(S, pb[:, 0:TW], wb[:, k, ob * 128:(ob + 1) * 128], br[m][:, k, sl], [wbk, f'br{m}'], [pbk], start=(k == 0), stop=(k == 3))
                        tp, tk = tmp[cnt[0] % 2], f'tmp{cnt[0] % 2}'; cnt[0] += 1
                        ACTF(S, tp[:, :], pg[:, 0:TW], AF.Sigmoid, [pgk], [tk])
                        if m == 0:
                            TT(S, 'dve', acc[:, ob, sl], pb[:, 0:TW], tp[:, :], ALU.mult, [pbk, tk], ['acc'])
                        else:
                            tp2, tk2 = tmp[2 + cnt[0] % 2], f'tmp{2 + cnt[0] % 2}'
                            TT(S, 'dve', tp2[:, :], pb[:, 0:TW], tp[:, :], ALU.mult, [pbk, tk], [tk2])
                            TT(S, 'pool', acc[:, ob, sl], acc[:, ob, sl], tp2[:, :], ALU.add, ['acc', tk2], ['acc'])
        for ob in range(8):
            COPY(S, 'act', mg[:, ob, :], acc[:, ob, :], ['acc'], ['mg'])
        def evac_res(cb, tt, ps, pk):
            sl = slice(tt * TW, (tt + 1) * TW)
            TT(S, 'dve', x_sb[:, cb, sl], ps[:, 0:TW], x_sb[:, cb, sl], ALU.add, [pk, 'x'], ['x'])
        proj_pass(S, mg, 'mg', wout, 1024, 8, wbufs, pp, evac_res, 'w')
        rmsnorm_pass(S, x_sb, g2, h_sb, ones, pp, sq, rstd, 'x', 'h', 'g2')
        def evac_gu(cb, tt, ps, pk):
            hb, isup = cb // 2, cb % 2
            sl = slice(tt * TW, (tt + 1) * TW)
            tp, tk = tmp[tt], f'tmp{tt}'
            if not isup:
                ACTF(S, tp[:, :], ps[:, 0:TW], AF.Silu, [pk], [tk])
            else:
                TT(S, 'dve', hid[:, hb, sl], ps[:, 0:TW], tp[:, :], ALU.mult, [pk, tk], ['hid'])
        proj_pass(S, h_sb, 'h', wgu, 2 * DFF, 8, wbufs, pp, evac_gu, 'w', colw=256)
        proj_pass(S, hid, 'hid', wdn, 1024, NHB, wbufs, pp, evac_res, 'w', colw=128)
        if final:
            rmsnorm_pass(S, x_sb, g3, acc, ones, pp, sq, rstd, 'x', 'acc', 'g3')
            for k in range(8):
                S.dma('sp', xo[k * 128:(k + 1) * 128, tsl], acc[:, k, :], reads=['acc'])
        else:
            for k in range(8):
                S.dma('sp', xo[k * 128:(k + 1) * 128, tsl], x_sb[:, k, :], reads=['x'])
    S.emit()
    return nc


def pad_T(a):
    out = np.zeros((T, a.shape[1]), np.float32); out[:a.shape[0]] = a; return out
def consts_B():
    s = np.arange(64)
    maskU = (s[:, None] <= s[None, :]).astype(np.float32)
    rmask = np.ones((128, 512), np.float32); rmask[:, ::64] = 0.0
    return {"maskU": maskU, "ident": np.eye(128, dtype=np.float32), "rmask": rmask}
def ret_tables(j):
    lg = np.log1p(-np.exp2(np.float32(-5.0 - j))).astype(np.float32)
    s = np.arange(64, dtype=np.float32)
    diff = s[None, :] - s[:, None]
    decT = np.where(diff >= 0, np.exp(lg * np.maximum(diff, 0)), 0).astype(np.float32)
    tm = (np.arange(512) % 64).astype(np.float32)
    xi = np.exp(lg * (tm + 1)).astype(np.float32); zeta = np.exp(lg * (63 - tm)).astype(np.float32)
    g64 = np.exp(lg * 64).astype(np.float32)
    tab = np.concatenate([decT, np.tile(xi[None], (64, 1)), np.tile(zeta[None], (64, 1)), np.full((64, 1), g64, np.float32)], 1)
    return np.ascontiguousarray(tab.astype(np.float32))
def rope_tables():
    pos = (np.arange(T) - 48).astype(np.float32)
    half = 32
    inv = (np.float32(10000.0) ** (-np.arange(half, dtype=np.float32) / half)).astype(np.float32)
    ang = pos[None, :] * inv[:, None]
    c = np.cos(ang).astype(np.float32); s_ = np.sin(ang).astype(np.float32)
    return np.concatenate([c, c], 0), np.concatenate([-s_, s_], 0)
def inputs_B_hg_rt(zb, j, lb_logits):
    m = {}
    m["hg_qT"] = np.ascontiguousarray(pad_T(zb[:, 64*j:64*j+64]).T)
    m["hg_fT"] = np.ascontiguousarray(pad_T(zb[:, 256+64*j:256+64*j+64]).T)
    m["hg_v"] = pad_T(zb[:, 512+128*j:512+128*j+128])
    m["hg_lg"] = np.ascontiguousarray(lb_logits[:, 64*j:64*j+64].T)
    perm = (np.arange(64) + 32) % 64
    q = pad_T(zb[:, 1536+64*j:1536+64*j+64]); k = pad_T(zb[:, 1792+64*j:1792+64*j+64])
    m["rt_qT"] = np.ascontiguousarray(q.T); m["rt_qpT"] = np.ascontiguousarray(q[:, perm].T)
    m["rt_kT"] = np.ascontiguousarray(k.T); m["rt_kpT"] = np.ascontiguousarray(k[:, perm].T)
    m["rt_v"] = pad_T(zb[:, 2048+128*j:2048+128*j+128])
    c, s_ = rope_tables(); m["rt_cos"] = c; m["rt_sin"] = s_
    m["rt_tab"] = ret_tables(j)
    return m

def inputs_B_gd(zb, j, conv_w, a_log, dt_bias):
    m = {}
    x = np.zeros((384, T + 3), np.float32)
    for mm in range(3):
        c0 = 3584 + 512 * mm + 128 * j
        x[mm*128:(mm+1)*128, 3:3+zb.shape[0]] = zb[:, c0:c0+128].T
    m["gd_x"] = x
    cw = np.zeros((128, 12), np.float32)
    for mm in range(3):
        cw[:, mm*4:(mm+1)*4] = conv_w[:, 512*mm + 128*j: 512*mm + 128*j + 128].T
    m["gd_cw"] = cw
    ba = np.zeros((64, 2, T // 64), np.float32)
    ba[:, 0, :] = pad_T(zb[:, 5120+j:5121+j])[:, 0].reshape(T // 64, 64).T
    ba[:, 1, :] = pad_T(zb[:, 5124+j:5125+j])[:, 0].reshape(T // 64, 64).T
    m["gd_ba"] = ba
    m["gd_par"] = np.tile(np.array([[a_log[j], dt_bias[j]]], np.float32), (64, 1))
    return m

def inputs_B_s5(zb, j, a_re, a_im, log_dt, b_re, b_im, c_re, c_im):
    m = {}
    m["s5_uT"] = np.ascontiguousarray(pad_T(zb[:, 3072 + 128 * j: 3072 + 128 * j + 128]).T)
    gs = np.arange(8 * j, 8 * j + 8)
    A = np.stack([a_re[gs], a_im[gs], np.tile(log_dt[gs][:, None], (1, 64))], 0)
    pP = A.reshape(3, 4, 2, 64).transpose(2, 3, 0, 1).reshape(128, 3, 4)
    pF = np.tile(A.reshape(3, 512)[None], (128, 1, 1))
    bF = np.zeros((128, 2, 8, 64), np.float32); cP = np.zeros((2, 64, 2, 4, 128), np.float32)
    for gl in range(8):
        g = gs[gl]
        bF[16 * gl:16 * gl + 16, 0, gl, :] = b_re[g].T
        bF[16 * gl:16 * gl + 16, 1, gl, :] = b_im[g].T
        q, g2 = gl // 2, gl % 2
        cP[g2, :, 0, q, 16 * gl:16 * gl + 16] = c_re[g].T
        cP[g2, :, 1, q, 16 * gl:16 * gl + 16] = c_im[g].T
    m["s5_pP"] = np.ascontiguousarray(pP.astype(np.float32)); m["s5_pF"] = np.ascontiguousarray(pF.astype(np.float32))
    m["s5_bF"] = np.ascontiguousarray(bF.reshape(128, 2, 512)); m["s5_cP"] = np.ascontiguousarray(cP.reshape(128, 2, 512))
    return m


A_COLS = np.r_[0:1024, 1536:2560, 3072:3584, 3584:5128]
NA_PAD = ((len(A_COLS) + 127) // 128) * 128
_CACHE = {}

def _prog(key, fn):
    if key not in _CACHE:
        _CACHE[key] = fn()
    return _CACHE[key]

def tok_rows(sg):
    return np.r_[48:64, 64 + sg * 2048: 64 + (sg + 1) * 2048]

def run_A(l, xTs, inp):
    w_in = np.asarray(inp["w_in"][l], np.float32)
    wA = np.zeros((1024, NA_PAD), np.float32); wA[:, :len(A_COLS)] = w_in[:, A_COLS]
    gm = np.ascontiguousarray(np.asarray(inp["norm_mix"][l], np.float32).reshape(8, 128).T)
    nc = _prog(('A',), lambda: build_A(NA_PAD))
    res = run_bass_kernel_spmd(nc, [{"xT": xTs[c], "gmix": gm, "wA": wA} for c in range(8)], core_ids=list(range(8)))
    z = np.zeros((2, 8256, 5640), np.float32)
    for c in range(8):
        b, sg = c // 4, c % 4
        zt = res.results[c]["zT"][:len(A_COLS)].T
        if sg == 0:
            z[b][48:64, A_COLS] = zt[:16]
        z[b][64 + sg * 2048: 64 + (sg + 1) * 2048, A_COLS] = zt[16:]
    return z

def run_B(l, z, inp):
    cst = consts_B()
    in_maps = []
    g = lambda k: np.asarray(inp[k][l], np.float32)
    for c in range(8):
        b, j = c // 4, c % 4
        m = inputs_B_hg_rt(z[b], j, np.asarray(inp["hg_lb_logits"], np.float32))
        m.update(inputs_B_gd(z[b], j, g('gdn_conv'), g('gdn_a_log'), g('gdn_dt_bias')))
        m.update(inputs_B_s5(z[b], j, g('ssm_a_re'), g('ssm_a_im'), g('ssm_log_dt'), g('ssm_b_re'), g('ssm_b_im'), g('ssm_c_re'), g('ssm_c_im')))
        m.update(cst)
        in_maps.append(m)
    nc = _prog(('B', l), lambda: build_B(l))
    res = run_bass_kernel_spmd(nc, in_maps, core_ids=list(range(8)))
    mix = np.zeros((2, 4, 8256, 512), np.float32)
    for c in range(8):
        b, j = c // 4, c % 4
        r = res.results[c]
        mix[b, 0][:, 128 * j:128 * j + 128] = r["hg_out"][:8256]
        mix[b, 1][:, 128 * j:128 * j + 128] = r["rt_out"][:8256]
        mix[b, 2][:, 128 * j:128 * j + 128] = r["s5_out"].T[:8256]
        mix[b, 3][:, 128 * j:128 * j + 128] = r["gd_out"][:8256]
    return mix

def run_C(l, xTs, z, mix, inp, final):
    g = lambda k: np.asarray(inp[k][l], np.float32)
    w_in = g("w_in")
    col = lambda v: np.ascontiguousarray(v.reshape(-1, 128).T)
    wG = np.ascontiguousarray(w_in[:, np.r_[1024:1536, 2560:3072, 5128:5640]])
    wGate = np.ascontiguousarray(w_in[:, 5640:9736])
    gains = np.stack([col(g("hg_norm")), col(g("ret_norm")), col(g("ssm_d")), col(g("gdn_norm"))], 1)
    w_gu = g("w_gu")
    order = np.concatenate([np.r_[hb * 128:(hb + 1) * 128, 2816 + hb * 128: 2816 + (hb + 1) * 128] for hb in range(22)])
    shared = {"gmix": col(g("norm_mix")), "gffn": col(g("norm_ffn")), "gfin": col(np.asarray(inp["norm_final"], np.float32)),
              "wG": wG, "wGate": wGate, "gains": np.ascontiguousarray(gains), "wglu": g("ssm_w_glu"), "wbr": g("w_branch"), "wout": g("w_out"),
              "wgu": np.ascontiguousarray(w_gu[:, order]), "wdn": g("w_down")}
    in_maps = []
    for c in range(8):
        b, sg = c // 4, c % 4
        rows = tok_rows(sg)
        m = dict(shared)
        m["xT"] = xTs[c]
        m["mixT"] = np.ascontiguousarray(mix[b][:, rows, :].transpose(0, 2, 1).reshape(2048, 2064))
        m["uT"] = np.ascontiguousarray(z[b][rows, 3072:3584].T)
        in_maps.append(m)
    nc = _prog(('C', final), lambda: build_C(final))
    res = run_bass_kernel_spmd(nc, in_maps, core_ids=list(range(8)))
    return [res.results[c]["xo"] for c in range(8)]

def initial_xT(inp):
    x = np.asarray(inp["x"], np.float32); meta = np.asarray(inp["meta"], np.float32)
    return [np.ascontiguousarray(np.concatenate([meta, x[c // 4, (c % 4) * 2048:(c % 4 + 1) * 2048]], 0).T) for c in range(8)]


def kernel(**inputs):
    inp = {k: np.asarray(v) for k, v in inputs.items()}
    xTs = initial_xT(inp)
    for l in range(4):
        z = run_A(l, xTs, inp)
        mix = run_B(l, z, inp)
        xTs = run_C(l, xTs, z, mix, inp, final=(l == 3))
    out = np.zeros((2, 8192, 1024), np.float32)
    for c in range(8):
        b, sg = c // 4, c % 4
        out[b, sg * 2048:(sg + 1) * 2048, :] = xTs[c][:, 16:].T
    return out
```

```python
import contextlib
import numpy as np
import concourse.bass as bass
import concourse.mybir as mybir
from concourse.bass_utils import run_bass_kernel_spmd

F32 = mybir.dt.float32
BF16 = mybir.dt.bfloat16
ALU = mybir.AluOpType
AF = mybir.ActivationFunctionType
AX = mybir.AxisListType


class Sched:
    ENG = ['pe', 'dve', 'act', 'pool', 'sp']

    def __init__(self, nc):
        self.nc = nc
        self.ops = {e: [] for e in self.ENG}
        self.cnt = {}
        self.seen = {e: {} for e in self.ENG}
        self.writers = {}
        self.readers = {}
        self.genwar = {}
        self.stack = contextlib.ExitStack()
        self.nt = 0
        self.dma_idx = {}
        import os; nq = int(os.environ.get('NQ', '48')); self.NQ = {'sp': nq, 'pool': max(1, nq // 2), 'act': 8, 'dve': 4, 'pe': 4}

    def sbuf(self, shape, dtype, name=None):
        self.nt += 1
        name = name or f"t{self.nt}"
        return self.stack.enter_context(self.nc.sbuf_tensor(name, list(shape), dtype))

    def psum(self, shape, dtype, name=None):
        self.nt += 1
        name = name or f"p{self.nt}"
        return self.stack.enter_context(self.nc.psum_tensor(name, list(shape), dtype))

    def op(self, eng, fn, reads=(), writes=(), dma=False, pwrites=()):
        if dma:
            idx = self.dma_idx.get(eng, 0)
            self.dma_idx[eng] = idx + 1
            sem = f"q_{eng}_{idx % self.NQ[eng]}"
        else:
            sem = eng
        deps = []
        if dma and self.cnt.get(sem, 0) > 0:
            deps.append((sem, self.cnt[sem]))
        same = (lambda s: (s == sem and not dma))
        for b in reads:
            deps.extend(self.writers.get(b, {}).items())
        for b in writes:
            for s_, v_ in self.writers.get(b, {}).items():
                if not same(s_):
                    deps.append((s_, v_))
            for r in self.readers.get(b, ()):
                if not same(r[0]):
                    deps.append(r)
        for b in pwrites:
            if self.readers.get(b):
                self.genwar[b] = self.readers[b]
                self.readers[b] = []
                self.writers[b] = {}
            for r in self.genwar.get(b, ()):
                if not same(r[0]):
                    deps.append(r)
        waits = {}
        for (s, v) in deps:
            if s == 'pe' and sem == 'pe':
                continue
            if v > waits.get(s, 0):
                waits[s] = v
        seen = self.seen[eng]
        wl = []
        for s, v in waits.items():
            if v > seen.get(s, 0):
                seen[s] = v
                wl.append((s, v))
        amt = 16 if dma else 1
        self.cnt[sem] = self.cnt.get(sem, 0) + amt
        val = self.cnt[sem]
        for b in writes:
            self.writers[b] = {sem: val}
            self.readers[b] = []
            self.genwar[b] = []
        for b in pwrites:
            self.writers.setdefault(b, {})[sem] = val
        for b in reads:
            self.readers.setdefault(b, []).append((sem, val))
        self.ops[eng].append((fn, wl, sem, amt))

    def dma(self, eng, out, in_, reads=(), writes=(), pwrites=(), **kw):
        self.op(eng, lambda e: e.dma_start(out=out, in_=in_, **kw), reads, writes, dma=True, pwrites=pwrites)

    def emit(self):
        nc = self.nc
        names = sorted(self.cnt.keys())
        sems = {n: self.stack.enter_context(nc.semaphore(n)) for n in names}
        final = dict(self.cnt)
        with nc.Block() as block:
            def mk(engname):
                def body(engine):
                    for (fn, wl, sem, amt) in self.ops[engname]:
                        for (s, v) in wl:
                            engine.wait_ge(sems[s], v)
                        ins = fn(engine)
                        ins.then_inc(sems[sem], amt)
                    if engname == 'sp':
                        for s, v in final.items():
                            engine.wait_ge(sems[s], v)
                return body
            block.tensor(mk('pe'))
            block.vector(mk('dve'))
            block.scalar(mk('act'))
            block.gpsimd(mk('pool'))
            block.sync(mk('sp'))
        self.stack.close()


NT = 2064
TW = 344
NTT = 6
EPS = 1e-6

def load_consts_dense(S, nc):
    ones = S.sbuf([128, 128], F32, "ones")
    S.op('pool', lambda e: e.memset(ones[:], 1.0), writes=['ones'])
    return ones

def stage_rmsnorm(S, x_sb, g_sb, h_sb, ones, ps_pool, tmp, xkey='x', hkey='h', gkey='g'):
    sq, rstd = tmp
    for tt in range(NTT):
        sl = slice(tt * TW, (tt + 1) * TW)
        S.op('act', lambda e, sl=sl: e.activation(out=sq[:, :, :], in_=x_sb[:, :, sl], func=AF.Square), reads=[xkey], writes=['sq'])
        ps, pk = ps_pool()
        for k in range(8):
            S.op('pe', lambda e, k=k, ps=ps: e.matmul(ps[:, 0:TW], lhsT=ones[:, :], rhs=sq[:, k, :], start=(k == 0), stop=(k == 7)),
                 reads=['sq', 'ones'], writes=[pk])
        S.op('act', lambda e, ps=ps: e.activation(out=rstd[:, :], in_=ps[:, 0:TW], func=AF.Sqrt, scale=1.0 / 1024, bias=EPSB[0][:, 0:1]), reads=[pk, 'epsb'], writes=['rstd'])
        S.op('dve', lambda e: e.reciprocal(out=rstd[:, :], in_=rstd[:, :]), reads=['rstd'], writes=['rstd'])
        for k in range(8):
            S.op('dve', lambda e, k=k, sl=sl: e.scalar_tensor_tensor(out=h_sb[:, k, sl], in0=x_sb[:, k, sl], scalar=g_sb[:, k:k + 1], in1=rstd[:, :], op0=ALU.mult, op1=ALU.mult),
                 reads=[xkey, gkey, 'rstd'], writes=[hkey])

EPSB = [None]
def make_epsb(S):
    t = S.sbuf([128, 1], F32, "epsb")
    S.op('pool', lambda e: e.memset(t[:], EPS), writes=['epsb'])
    EPSB[0] = t

def stage_proj(S, h_sb, w_dram, ncols, wbufs, ps_pool, evac, hkey='h', kchunks=8, wname='w'):
    nblk = (ncols + 511) // 512
    for cb in range(nblk):
        c0 = cb * 512
        w = min(512, ncols - c0)
        wt = wbufs[cb % len(wbufs)]
        wk = f'{wname}{cb % len(wbufs)}'
        S.dma('pool', wt[:, 0:kchunks, 0:w], w_dram[:, c0:c0 + w].rearrange("(k p) c -> p k c", p=128), writes=[wk])
        for sub in range(w // 128):
            for tt in range(NTT):
                sl = slice(tt * TW, (tt + 1) * TW)
                ps, pk = ps_pool()
                for k in range(kchunks):
                    S.op('pe', lambda e, k=k, ps=ps, wt=wt, sub=sub, sl=sl: e.matmul(ps[:, 0:TW], lhsT=wt[:, k, sub * 128:(sub + 1) * 128], rhs=h_sb[:, k, sl], start=(k == 0), stop=(k == kchunks - 1)),
                         reads=[wk, hkey], writes=[pk])
                evac(cb * 4 + sub, tt, ps, pk)

class PsPool:
    def __init__(self, S, n, shape=(128, 512), dtype=F32, prefix='ps'):
        self.tiles = [S.psum(list(shape), dtype, f"{prefix}{i}") for i in range(n)]
        self.prefix = prefix
        self.i = 0
    def __call__(self):
        t = self.tiles[self.i % len(self.tiles)]
        k = f"{self.prefix}{self.i % len(self.tiles)}"
        self.i += 1
        return t, k

def build_A(NA):
    nc = bass.Bass("TRN2", target_bir_lowering=False)
    xT = nc.dram_tensor("xT", [1024, NT], F32, kind="ExternalInput").ap()
    gm = nc.dram_tensor("gmix", [128, 8], F32, kind="ExternalInput").ap()
    wA = nc.dram_tensor("wA", [1024, NA], F32, kind="ExternalInput").ap()
    zT = nc.dram_tensor("zT", [NA, NT], F32, kind="ExternalOutput").ap()
    S = Sched(nc)
    x_sb = S.sbuf([128, 8, NT], F32, "x_sb")
    h_sb = S.sbuf([128, 8, NT], BF16, "h_sb")
    g_sb = S.sbuf([128, 8], F32, "g_sb")
    sq = S.sbuf([128, 8, TW], F32, "sq")
    rstd = S.sbuf([128, TW], F32, "rstd")
    wbufs = [S.sbuf([128, 8, 512], BF16, f"wb{i}") for i in range(3)]
    stg = [S.sbuf([128, NT], F32, f"stg{i}") for i in range(2)]
    ones = load_consts_dense(S, nc)
    make_epsb(S)
    pp = PsPool(S, 6)
    S.dma('sp', g_sb[:, :], gm[:, :], writes=['g'])
    for k in range(8):
        S.dma('sp', x_sb[:, k, :], xT[k * 128:(k + 1) * 128, :], pwrites=['x'])
    stage_rmsnorm(S, x_sb, g_sb, h_sb, ones, pp, (sq, rstd))
    cnt = [0]
    def evac(cb, tt, ps, pk):
        st = stg[cb % 2]; sk = f'stg{cb % 2}'
        sl = slice(tt * TW, (tt + 1) * TW)
        eng = 'act' if (cnt[0] % 2 == 0) else 'dve'
        cnt[0] += 1
        if eng == 'act':
            S.op('act', lambda e: e.activation(out=st[:, sl], in_=ps[:, 0:TW], func=AF.Copy), reads=[pk], writes=[sk])
        else:
            S.op('dve', lambda e: e.tensor_copy(out=st[:, sl], in_=ps[:, 0:TW]), reads=[pk], writes=[sk])
        if tt == NTT - 1:
            S.dma('sp', zT[cb * 128:(cb + 1) * 128, :], st[:, :], reads=[sk])
    stage_proj(S, h_sb, wA, NA, wbufs, pp, evac)
    S.emit()
    return nc


T = 8704
NBLK = 17
BW = 512
CH = 64
NCH = 8

def TT(S, eng, out, in0, in1, op, r, w):
    S.op(eng, lambda e: e.tensor_tensor(out=out, in0=in0, in1=in1, op=op), reads=r, writes=w)
def TS(S, eng, out, in0, s1, s2, op0, op1, r, w):
    S.op(eng, lambda e: e.tensor_scalar(out=out, in0=in0, scalar1=s1, scalar2=s2, op0=op0, op1=op1), reads=r, writes=w)
def STT(S, eng, out, in0, sc, in1, op0, op1, r, w):
    S.op(eng, lambda e: e.scalar_tensor_tensor(out=out, in0=in0, scalar=sc, in1=in1, op0=op0, op1=op1), reads=r, writes=w)
def ACTF(S, out, in_, func, r, w, scale=1.0, bias=None):
    if bias is None:
        S.op('act', lambda e: e.activation(out=out, in_=in_, func=func, scale=scale), reads=r, writes=w)
    else:
        S.op('act', lambda e: e.activation(out=out, in_=in_, func=func, scale=scale, bias=bias), reads=r, writes=w)
_PE_MODE = [None]
def _ru(n):
    return 32 if n <= 32 else (64 if n <= 64 else 128)
def _pe_mode(S, ap, tr):
    import os
    if os.environ.get('PE_DRAIN') is None:
        return
    m = (_ru(ap.shape[0]), _ru(ap.shape[1]), str(ap.dtype) == str(F32))
    if _PE_MODE[0] is not None and _PE_MODE[0] != m:
        S.op('pe', lambda e: e.drain(), reads=(), writes=())
    _PE_MODE[0] = m
def MM(S, out, lhsT, rhs, r, w, start=True, stop=True):
    _pe_mode(S, lhsT, False)
    S.op('pe', lambda e: e.matmul(out, lhsT=lhsT, rhs=rhs, start=start, stop=stop), reads=r, writes=w)
def TR(S, out, in_, ident, r, w):
    _pe_mode(S, in_, True)
    S.op('pe', lambda e: e.transpose(out, in_, ident), reads=r, writes=w)
def COPY(S, eng, out, in_, r, w):
    if eng == 'act':
        S.op('act', lambda e: e.activation(out=out, in_=in_, func=AF.Copy), reads=r, writes=w)
    else:
        S.op(eng, lambda e: e.tensor_copy(out=out, in_=in_), reads=r, writes=w)
def RED(S, eng, out, in_, r, w):
    S.op(eng, lambda e: e.tensor_reduce(out=out, in_=in_, axis=AX.X, op=ALU.add), reads=r, writes=w)


class Ctx:
    pass


def norm_out(S, C, pfx, o_ps, ok, half, blk, out_dram, center):
    ost = C.ost[(blk * 2 + half) % 2]
    osk = f'ost{(blk * 2 + half) % 2}'
    o_ps = o_ps[0:64, 0:512]
    o3 = o_ps.rearrange("p (c e) -> p c e", e=128)
    sq3 = C.osq[0:64, :].rearrange("p (c e) -> p c e", e=128)
    ACTF(S, C.osq[0:64, :], o_ps, AF.Square, [ok], ['osq'])
    RED(S, 'dve', C.oss[0:64, 0:4], sq3, ['osq'], ['oss'])
    if center:
        RED(S, 'dve', C.oss[0:64, 4:8], o3, [ok], ['oss'])
        TS(S, 'dve', C.oss[0:64, 4:8], C.oss[0:64, 4:8], 1.0 / 128, None, ALU.mult, ALU.bypass, ['oss'], ['oss'])
        TT(S, 'dve', C.oss[0:64, 8:12], C.oss[0:64, 4:8], C.oss[0:64, 4:8], ALU.mult, ['oss'], ['oss'])
        STT(S, 'dve', C.oss[0:64, 0:4], C.oss[0:64, 0:4], 1.0 / 128, C.oss[0:64, 8:12], ALU.mult, ALU.subtract, ['oss'], ['oss'])
        ACTF(S, C.oss[0:64, 0:4], C.oss[0:64, 0:4], AF.Sqrt, ['oss', 'epsb'], ['oss'], scale=1.0, bias=EPSB[0][0:64, 0:1])
    else:
        ACTF(S, C.oss[0:64, 0:4], C.oss[0:64, 0:4], AF.Sqrt, ['oss', 'epsb'], ['oss'], scale=1.0 / 128, bias=EPSB[0][0:64, 0:1])
    S.op('dve', lambda e: e.reciprocal(out=C.oss[0:64, 0:4], in_=C.oss[0:64, 0:4]), reads=['oss'], writes=['oss'])
    for c in range(4):
        if center:
            TS(S, 'dve', ost[0:64, c, :], o3[:, c, :], C.oss[0:64, 4 + c:5 + c], C.oss[0:64, c:c + 1], ALU.subtract, ALU.mult, [ok, 'oss'], [osk])
        else:
            TS(S, 'dve', ost[0:64, c, :], o3[:, c, :], C.oss[0:64, c:c + 1], None, ALU.mult, ALU.bypass, [ok, 'oss'], [osk])
    t0 = blk * BW + half * 256
    S.dma('sp', out_dram[t0:t0 + 256, :].rearrange("(c p) e -> p c e", p=64), ost[0:64, :, :], reads=[osk])


class Hgrn:
    def __init__(self, S, nc, C, layer):
        self.S, self.C, self.layer = S, C, layer
        d = lambda n, s: nc.dram_tensor(n, s, F32, kind="ExternalInput").ap()
        self.qT = d("hg_qT", [64, T]); self.fT = d("hg_fT", [64, T]); self.v = d("hg_v", [T, 128])
        self.lg = d("hg_lg", [64, 4])
        self.out = nc.dram_tensor("hg_out", [T, 128], F32, kind="ExternalOutput").ap()
        sb = S.sbuf
        self.zq = [sb([64, BW], F32, f"hg_zq{i}") for i in range(1)]
        self.zf = [sb([64, BW], F32, f"hg_zf{i}") for i in range(1)]
        self.vb = [sb([64, NCH, 128], BF16, f"hg_vb{i}") for i in range(2)]
        self.lgs = sb([64, 4], F32, "hg_lgs"); self.lb = sb([64, 4], F32, "hg_lb")
        self.f = sb([64, BW], F32, "hg_f"); self.kT = sb([64, BW], F32, "hg_kT"); self.b = sb([64, BW], F32, "hg_b")
        self.bm = sb([64, BW], F32, "hg_bm"); self.e1 = sb([64, BW], F32, "hg_e1"); self.e2 = sb([64, BW], F32, "hg_e2")
        self.sq = sb([64, BW], F32, "hg_sq")
        self.qt = sb([64, BW], BF16, "hg_qt"); self.kt = sb([64, BW], BF16, "hg_kt"); self.ke = sb([64, BW], F32, "hg_ke")
        self.dec = sb([64, 16], F32, "hg_dec")
        self.kes = [sb([64, 64], BF16, f"hg_kes{i}") for i in range(2)]
        self.att = [sb([64, 64], BF16, f"hg_att{i}") for i in range(2)]
        self.Sst = sb([64, 128], F32, "hg_S"); self.Sb = [sb([64, 128], BF16, f"hg_Sb{i}") for i in range(2)]
        S.op('pool', lambda e: e.memset(self.Sst[:], 0.0), writes=['hg_S'])
        S.dma('sp', self.lgs[:, :], self.lg[:, :], writes=['hg_lgs'])
        S.op('dve', lambda e: e.tensor_reduce(out=self.lb[:, 0:1], in_=self.lgs[:, :], axis=AX.X, op=ALU.max), reads=['hg_lgs'], writes=['hg_lb'])
        TS(S, 'dve', self.lgs[:, :], self.lgs[:, :], self.lb[:, 0:1], None, ALU.subtract, ALU.bypass, ['hg_lgs', 'hg_lb'], ['hg_lgs'])
        ACTF(S, self.lgs[:, :], self.lgs[:, :], AF.Exp, ['hg_lgs'], ['hg_lgs'])
        RED(S, 'dve', self.lb[:, 1:2], self.lgs[:, :], ['hg_lgs'], ['hg_lb'])
        S.op('dve', lambda e: e.reciprocal(out=self.lb[:, 1:2], in_=self.lb[:, 1:2]), reads=['hg_lb'], writes=['hg_lb'])
        if layer == 0:
            S.op('dve', lambda e: e.memset(self.lb[:, 2:3], 0.0), reads=['hg_lb'], writes=['hg_lb'])
        else:
            RED(S, 'dve', self.lb[:, 2:3], self.lgs[:, 1:layer + 1], ['hg_lgs', 'hg_lb'], ['hg_lb'])
            TT(S, 'dve', self.lb[:, 2:3], self.lb[:, 2:3], self.lb[:, 1:2], ALU.mult, ['hg_lb'], ['hg_lb'])
        TS(S, 'dve', self.lb[:, 3:4], self.lb[:, 2:3], -1.0, 1.0, ALU.mult, ALU.add, ['hg_lb'], ['hg_lb'])

    def block(self, blk):
        S, C = self.S, self.C
        par = blk % 2
        zq, zf, vb = self.zq[0], self.zf[0], self.vb[par]
        kq, kf, kv_ = 'hg_zq0', 'hg_zf0', f'hg_vb{par}'
        sl = slice(blk * BW, (blk + 1) * BW)
        S.dma('sp', zq[:, :], self.qT[:, sl], writes=[kq])
        S.dma('sp', zf[:, :], self.fT[:, sl], writes=[kf])
        S.dma('pool', vb[:, :, :], self.v[sl, :].rearrange("(c p) e -> p c e", p=64), writes=[kv_])
        f, kT, b, bm, e1, e2, sq, qt, kt, ke, dec = self.f, self.kT, self.b, self.bm, self.e1, self.e2, self.sq, self.qt, self.kt, self.ke, self.dec
        ACTF(S, f[:, :], zf[:, :], AF.Sigmoid, [kf], ['hg_f'])
        TS(S, 'dve', f[:, :], f[:, :], self.lb[:, 3:4], self.lb[:, 2:3], ALU.mult, ALU.add, ['hg_f', 'hg_lb'], ['hg_f'])
        ACTF(S, b[:, :], f[:, :], AF.Ln, ['hg_f'], ['hg_b'])
        TS(S, 'pool', kT[:, :], f[:, :], -1.0, 1.0, ALU.mult, ALU.add, ['hg_f'], ['hg_kT'])
        S.op('dve', lambda e: e.tensor_tensor_scan(out=b[:, :], data0=C.rmask[0:64, :], data1=b[:, :], initial=0.0, op0=ALU.mult, op1=ALU.add), reads=['hg_b', 'rmask'], writes=['hg_b'])
        b3 = b[:, :].rearrange("p (c t) -> p c t", t=CH)
        v3 = lambda t: t[:, :].rearrange("p (c t) -> p c t", t=CH)
        TT(S, 'dve', v3(bm), b3, b3[:, :, 31:32].broadcast_to([64, NCH, CH]), ALU.subtract, ['hg_b'], ['hg_bm'])
        ACTF(S, e1[:, :], bm[:, :], AF.Exp, ['hg_bm'], ['hg_e1'])
        ACTF(S, e2[:, :], bm[:, :], AF.Exp, ['hg_bm'], ['hg_e2'], scale=-1.0)
        TT(S, 'dve', v3(bm), b3[:, :, 63:64].broadcast_to([64, NCH, CH]), b3, ALU.subtract, ['hg_b'], ['hg_bm'])
        ACTF(S, ke[:, :], bm[:, :], AF.Exp, ['hg_bm'], ['hg_ke'])
        ACTF(S, dec[:, 0:8], b3[:, :, 63], AF.Exp, ['hg_b'], ['hg_dec'])
        ACTF(S, dec[:, 8:16], b3[:, :, 31], AF.Exp, ['hg_b'], ['hg_dec'])
        ACTF(S, sq[:, :], zq[:, :], AF.Silu, [kq], ['hg_sq'])
        STT(S, 'dve', qt[:, :], sq[:, :], 0.125, e1[:, :], ALU.mult, ALU.mult, ['hg_sq', 'hg_e1'], ['hg_qt'])
        TT(S, 'pool', kt[:, :], kT[:, :], e2[:, :], ALU.mult, ['hg_kT', 'hg_e2'], ['hg_kt'])
        TT(S, 'pool', ke[:, :], kT[:, :], ke[:, :], ALU.mult, ['hg_kT', 'hg_ke'], ['hg_ke'])
        yield
        for c in range(NCH):
            cs = slice(c * CH, (c + 1) * CH)
            i2 = c % 2
            if c % 4 == 0:
                o_ps, ok = C.ob_hg()
            pt, ptk = C.pp()
            TR(S, pt[0:64, 0:64], ke[:, cs], C.ident[0:64, 0:64], ['hg_ke', 'ident'], [ptk])
            COPY(S, 'act', self.kes[i2][:, :], pt[0:64, 0:64], [ptk], [f'hg_kes{i2}'])
            pa, pak = C.pp()
            MM(S, pa[0:64, 0:64], kt[:, cs], qt[:, cs], ['hg_kt', 'hg_qt'], [pak])
            TT(S, 'dve', self.att[i2][:, :], pa[0:64, 0:64], C.maskU[0:64, :], ALU.mult, [pak, 'maskU'], [f'hg_att{i2}'])
            pk, pkk = C.pp()
            MM(S, pk[0:64, 0:128], self.kes[i2][:, :], vb[:, c, :], [f'hg_kes{i2}', kv_], [pkk])
            TS(S, 'pool', self.Sb[i2][:, :], self.Sst[:, :], dec[:, 8 + c:9 + c], None, ALU.mult, ALU.bypass, ['hg_S', 'hg_dec'], [f'hg_Sb{i2}'])
            oc = o_ps[0:64, (c % 4) * 128:(c % 4 + 1) * 128]
            MM(S, oc, self.att[i2][:, :], vb[:, c, :], [f'hg_att{i2}', kv_], [ok], start=True, stop=False)
            MM(S, oc, qt[:, cs], self.Sb[i2][:, :], ['hg_qt', f'hg_Sb{i2}'], [ok], start=False, stop=True)
            STT(S, 'dve', self.Sst[:, :], self.Sst[:, :], dec[:, c:c + 1], pk[0:64, 0:128], ALU.mult, ALU.add, ['hg_S', 'hg_dec', pkk], ['hg_S'])
            if c % 4 == 3:
                norm_out(S, C, 'hg_', o_ps, ok, c // 4, blk, self.out, center=False)
            yield


class Ret:
    def __init__(self, S, nc, C):
        self.S, self.C = S, C
        d = lambda n, s: nc.dram_tensor(n, s, F32, kind="ExternalInput").ap()
        self.qT = d("rt_qT", [64, T]); self.qpT = d("rt_qpT", [64, T]); self.kT = d("rt_kT", [64, T]); self.kpT = d("rt_kpT", [64, T])
        self.v = d("rt_v", [T, 128]); self.cos = d("rt_cos", [64, T]); self.sin = d("rt_sin", [64, T])
        self.tab = d("rt_tab", [64, 64 + 512 + 512 + 1])
        self.out = nc.dram_tensor("rt_out", [T, 128], F32, kind="ExternalOutput").ap()
        sb = S.sbuf
        self.inb = [[sb([64, BW], F32, f"rt_in{j}_{i}") for j in range(6)] for i in range(1)]
        self.vb = [sb([64, NCH, 128], BF16, f"rt_vb{i}") for i in range(2)]
        self.tabs = sb([64, 64 + 512 + 512 + 1], F32, "rt_tabs")
        self.t1 = sb([64, BW], F32, "rt_t1"); self.t2 = sb([64, BW], F32, "rt_t2")
        self.qr = sb([64, BW], BF16, "rt_qr"); self.qx = sb([64, BW], BF16, "rt_qx"); self.kr = sb([64, BW], BF16, "rt_kr"); self.kz = sb([64, BW], F32, "rt_kz")
        self.kzs = [sb([64, 64], BF16, f"rt_kzs{i}") for i in range(2)]
        self.att = [sb([64, 64], BF16, f"rt_att{i}") for i in range(2)]
        self.R = sb([64, 128], F32, "rt_R"); self.Rb = [sb([64, 128], BF16, f"rt_Rb{i}") for i in range(2)]
        S.op('pool', lambda e: e.memset(self.R[:], 0.0), writes=['rt_R'])
        S.dma('sp', self.tabs[:, :], self.tab[:, :], writes=['rt_tabs'])

    def block(self, blk):
        S, C = self.S, self.C
        par = blk % 2
        sl = slice(blk * BW, (blk + 1) * BW)
        ib = self.inb[0]; vb = self.vb[par]
        ik = [f'rt_in{j}_0' for j in range(6)]
        kv_ = f'rt_vb{par}'
        for j, src in enumerate([self.qT, self.qpT, self.kT, self.kpT, self.cos, self.sin]):
            S.dma('sp', ib[j][:, :], src[:, sl], writes=[ik[j]])
        S.dma('pool', vb[:, :, :], self.v[sl, :].rearrange("(c p) e -> p c e", p=64), writes=[kv_])
        decT = self.tabs[:, 0:64]; xi = self.tabs[:, 64:576]; zeta = self.tabs[:, 576:1088]; g64 = self.tabs[:, 1088:1089]
        t1, t2, qr, qx, kr, kz = self.t1, self.t2, self.qr, self.qx, self.kr, self.kz
        TT(S, 'dve', t1[:, :], ib[0][:, :], ib[4][:, :], ALU.mult, [ik[0], ik[4]], ['rt_t1'])
        TT(S, 'pool', t2[:, :], ib[1][:, :], ib[5][:, :], ALU.mult, [ik[1], ik[5]], ['rt_t2'])
        TT(S, 'dve', t1[:, :], t1[:, :], t2[:, :], ALU.add, ['rt_t1', 'rt_t2'], ['rt_t1'])
        COPY(S, 'act', qr[:, :], t1[:, :], ['rt_t1'], ['rt_qr'])
        TT(S, 'dve', qx[:, :], t1[:, :], xi, ALU.mult, ['rt_t1', 'rt_tabs'], ['rt_qx'])
        TT(S, 'dve', t1[:, :], ib[2][:, :], ib[4][:, :], ALU.mult, [ik[2], ik[4], 'rt_qr', 'rt_qx'], ['rt_t1'])
        TT(S, 'pool', t2[:, :], ib[3][:, :], ib[5][:, :], ALU.mult, [ik[3], ik[5]], ['rt_t2'])
        TT(S, 'dve', t1[:, :], t1[:, :], t2[:, :], ALU.add, ['rt_t1', 'rt_t2'], ['rt_t1'])
        ACTF(S, kr[:, :], t1[:, :], AF.Copy, ['rt_t1'], ['rt_kr'], scale=0.125)
        STT(S, 'dve', kz[:, :], t1[:, :], 0.125, zeta, ALU.mult, ALU.mult, ['rt_t1', 'rt_tabs'], ['rt_kz'])
        yield
        for c in range(NCH):
            cs = slice(c * CH, (c + 1) * CH)
            i2 = c % 2
            if c % 4 == 0:
                o_ps, ok = C.ob_rt()
            pt, ptk = C.pp()
            TR(S, pt[0:64, 0:64], kz[:, cs], C.ident[0:64, 0:64], ['rt_kz', 'ident'], [ptk])
            COPY(S, 'act', self.kzs[i2][:, :], pt[0:64, 0:64], [ptk], [f'rt_kzs{i2}'])
            pa, pak = C.pp()
            MM(S, pa[0:64, 0:64], kr[:, cs], qr[:, cs], ['rt_kr', 'rt_qr'], [pak])
            TT(S, 'dve', self.att[i2][:, :], pa[0:64, 0:64], decT, ALU.mult, [pak, 'rt_tabs'], [f'rt_att{i2}'])
            pk, pkk = C.pp()
            MM(S, pk[0:64, 0:128], self.kzs[i2][:, :], vb[:, c, :], [f'rt_kzs{i2}', kv_], [pkk])
            COPY(S, 'pool', self.Rb[i2][:, :], self.R[:, :], ['rt_R'], [f'rt_Rb{i2}'])
            oc = o_ps[0:64, (c % 4) * 128:(c % 4 + 1) * 128]
            MM(S, oc, self.att[i2][:, :], vb[:, c, :], [f'rt_att{i2}', kv_], [ok], start=True, stop=False)
            MM(S, oc, qx[:, cs], self.Rb[i2][:, :], ['rt_qx', f'rt_Rb{i2}'], [ok], start=False, stop=True)
            STT(S, 'dve', self.R[:, :], self.R[:, :], g64, pk[0:64, 0:128], ALU.mult, ALU.add, ['rt_R', 'rt_tabs', pkk], ['rt_R'])
            if c % 4 == 3:
                norm_out(S, C, 'rt_', o_ps, ok, c // 4, blk, self.out, center=True)
            yield


def make_ctx(S, nc, n_general=6):
    C = Ctx()
    d = lambda n, s: nc.dram_tensor(n, s, F32, kind="ExternalInput").ap()
    C.maskU_d = d("maskU", [64, 64]); C.ident_d = d("ident", [128, 128]); C.rmask_d = d("rmask", [128, 512])
    C.maskU = S.sbuf([64, 64], F32, "maskU_s"); C.ident = S.sbuf([128, 128], F32, "ident_s"); C.rmask = S.sbuf([128, 512], F32, "rmask_s")
    S.dma('sp', C.maskU[:, :], C.maskU_d[:, :], writes=['maskU'])
    S.dma('sp', C.ident[:, :], C.ident_d[:, :], writes=['ident'])
    S.dma('sp', C.rmask[:, :], C.rmask_d[:, :], writes=['rmask'])
    make_epsb(S)
    C.ones128 = S.sbuf([128, 128], F32, "ones128")
    S.op('pool', lambda e: e.memset(C.ones128[:], 1.0), writes=['ones128'])
    C.maskLs = S.sbuf([64, 64], F32, "maskLs")
    TS(S, 'dve', C.maskLs[:, :], C.maskU[:, :], -1.0, 1.0, ALU.mult, ALU.add, ['maskU'], ['maskLs'])
    C.pp = PsPool(S, 5)
    C.ob_hg = PsPool(S, 1, prefix='pohg'); C.ob_rt = PsPool(S, 1, prefix='port'); C.ob_s5 = PsPool(S, 1, prefix='pos5')
    C.ost = [S.sbuf([128, 4, 128], F32, f"ost{i}") for i in range(2)]
    C.osq = S.sbuf([128, 512], F32, "osq")
    C.oss = S.sbuf([128, 16], F32, "oss")
    return C


def build_B(layer, mixers=('hg', 'rt', 's5', 'gd'), nblk=NBLK):
    nc = bass.Bass("TRN2", target_bir_lowering=False)
    S = Sched(nc)
    C = make_ctx(S, nc)
    ms = []
    if 'hg' in mixers: ms.append(Hgrn(S, nc, C, layer))
    if 'rt' in mixers: ms.append(Ret(S, nc, C))
    if 'gd' in mixers: ms.append(Gdn(S, nc, C))
    if 's5' in mixers: ms.append(S5(S, nc, C))
    for blk in range(nblk):
        active = [m.block(blk) for m in ms]
        while active:
            for g in list(active):
                try:
                    next(g)
                except StopIteration:
                    active.remove(g)
    S.emit()
    return nc


class Rot:
    def __init__(self, S, name, shape, dtype, n=2):
        self.t = [S.sbuf(shape, dtype, f"{name}{i}") for i in range(n)]
        self.name = name; self.i = -1
    def next(self):
        self.i += 1
        return self.cur()
    def cur(self):
        j = self.i % len(self.t)
        return self.t[j], f"{self.name}{j}"


import os
PDT = BF16 if os.environ.get('GD_BF') else F32

class Gdn:
    def __init__(self, S, nc, C):
        self.S, self.C = S, C
        d = lambda n, s: nc.dram_tensor(n, s, F32, kind="ExternalInput").ap()
        self.x = d("gd_x", [384, T + 3]); self.cw = d("gd_cw", [128, 12]); self.ba = d("gd_ba", [64, 2, T // 64]); self.par = d("gd_par", [64, 2])
        self.out = nc.dram_tensor("gd_out", [T, 128], F32, kind="ExternalOutput").ap()
        sb = S.sbuf
        self.xin = Rot(S, "gd_xin", [128, 3, BW + 3], F32, 1)
        self.cws = sb([128, 12], F32, "gd_cws"); self.bas = sb([64, 2, T // 64], F32, "gd_bas"); self.pars = sb([64, 4], F32, "gd_pars")
        self.y = sb([128, 3, BW], F32, "gd_y"); self.sq = sb([128, 2, BW], F32, "gd_sq"); self.rs = self.sq
        self.qT = sb([128, BW], BF16, "gd_qT"); self.kT = sb([128, BW], BF16, "gd_kT"); self.kf = sb([128, BW], F32, "gd_kf")
        self.sc = sb([128, 64], F32, "gd_sc")
        self.pc = []
        for c in range(NCH):
            X = {}
            for n in ('gl', 'GL', 'GT'):
                X[n] = sb([64, 64], F32, f"gdc{c}_{n}")
            X['P'] = [sb([64, 64], F32, f"gdc{c}_P{i}") for i in range(2)]
            X['PT'] = [sb([64, 64], F32, f"gdc{c}_PT{i}") for i in range(2)]
            X['TT'] = [sb([64, 64], F32, f"gdc{c}_TT{i}") for i in range(2)]
            X['TTb'] = sb([64, 64], BF16, f"gdc{c}_TTb"); X['att'] = sb([64, 64], BF16, f"gdc{c}_att")
            for n in ('kbe', 'kd', 'bv'):
                X[n] = sb([64, 128], BF16, f"gdc{c}_{n}")
            X['WqT'] = sb([128, 64], BF16, f"gdc{c}_WqT"); X['wtok'] = sb([64, 128], BF16, f"gdc{c}_wtok"); X['u'] = sb([64, 128], BF16, f"gdc{c}_u")
            X['MT'] = sb([128, 128], F32, f"gdc{c}_MT"); X['Nn'] = sb([128, 128], F32, f"gdc{c}_Nn")
            self.pc.append(X)
        self.o1 = Rot(S, "gd_o1", [64, 128], F32)
        self.Stt = Rot(S, "gd_S", [128, 128], F32); self.Sb = Rot(S, "gd_Sb", [128, 128], BF16)
        self.ost = Rot(S, "gd_ost", [64, 4, 128], F32)
        self.ones = sb([64, 128], F32, "gd_ones"); self.ctmp = sb([64, 16], F32, "gd_ctmp")
        S.op('pool', lambda e: e.memset(self.ones[:], 1.0), writes=['gd_ones'])
        st0, st0k = self.Stt.next()
        S.op('pool', lambda e: e.memset(st0[:], 0.0), writes=[st0k])
        sbt, sbk = self.Sb.next()
        S.op('pool', lambda e: e.memset(sbt[:], 0.0), writes=[sbk])
        S.dma('sp', self.cws[:, :], self.cw[:, :], writes=['gd_cws'])
        S.dma('sp', self.bas[:, :, :], self.ba[:, :, :], writes=['gd_bas'])
        S.dma('sp', self.pars[:, 0:2], self.par[:, :], writes=['gd_pars'])
        ACTF(S, self.pars[:, 2:3], self.pars[:, 0:1], AF.Exp, ['gd_pars'], ['gd_pars'])
        TS(S, 'dve', self.pars[:, 2:3], self.pars[:, 2:3], -1.0, None, ALU.mult, ALU.bypass, ['gd_pars'], ['gd_pars'])

    def block(self, blk):
        S, C = self.S, self.C
        xin, xk = self.xin.next()
        for m in range(3):
            S.dma('sp', xin[:, m, :], self.x[m * 128:(m + 1) * 128, blk * BW: blk * BW + BW + 3], pwrites=[xk])
        y, sq, rs, qT, kT, kf, sc = self.y, self.sq, self.rs, self.qT, self.kT, self.kf, self.sc
        for m in range(3):
            eng = 'dve'
            TS(S, eng, y[:, m, :], xin[:, m, 0:BW], self.cws[:, 4 * m:4 * m + 1], None, ALU.mult, ALU.bypass, [xk, 'gd_cws'], [f'gd_y{m}'])
            for j in range(1, 4):
                STT(S, eng, y[:, m, :], xin[:, m, j:j + BW], self.cws[:, 4 * m + j:4 * m + j + 1], y[:, m, :], ALU.mult, ALU.add, [xk, 'gd_cws', f'gd_y{m}'], [f'gd_y{m}'])
            ACTF(S, y[:, m, :], y[:, m, :], AF.Silu, [f'gd_y{m}'], [f'gd_y{m}'])
        ACTF(S, sq[:, :, :], y[:, 0:2, :], AF.Square, ['gd_y0', 'gd_y1'], ['gd_sq'])
        for m in range(2):
            ps, pk = C.pp()
            MM(S, ps[:, 0:BW], C.ones128[:, :], sq[:, m, :], ['ones128', 'gd_sq'], [pk])
            ACTF(S, rs[:, m, :], ps[:, 0:BW], AF.Sqrt, [pk, 'epsb', 'gd_sq'], ['gd_sq'], bias=EPSB[0][:, 0:1])
        S.op('dve', lambda e: e.reciprocal(out=rs[:, :, :], in_=rs[:, :, :]), reads=['gd_sq'], writes=['gd_sq'])
        STT(S, 'dve', qT[:, :], y[:, 0, :], 128 ** -0.5, rs[:, 0, :], ALU.mult, ALU.mult, ['gd_y0', 'gd_sq'], ['gd_qT'])
        TT(S, 'dve', kf[:, :], y[:, 1, :], rs[:, 1, :], ALU.mult, ['gd_y1', 'gd_sq'], ['gd_kf'])
        COPY(S, 'act', kT[:, :], kf[:, :], ['gd_kf'], ['gd_kT'])
        c0 = blk * NCH
        ACTF(S, sc[0:64, 0:8], self.bas[:, 0, c0:c0 + 8], AF.Sigmoid, ['gd_bas'], ['gd_sc'])
        ACTF(S, sc[0:64, 8:16], self.bas[:, 1, c0:c0 + 8], AF.Exp, ['gd_bas', 'gd_pars'], ['gd_sc'], bias=self.pars[:, 1:2])
        ACTF(S, sc[0:64, 8:16], sc[0:64, 8:16], AF.Ln, ['gd_sc'], ['gd_sc'], bias=1.0)
        TS(S, 'dve', sc[0:64, 8:16], sc[0:64, 8:16], self.pars[:, 2:3], None, ALU.mult, ALU.bypass, ['gd_sc', 'gd_pars'], ['gd_sc'])
        TS(S, 'dve', sc[0:64, 16:24], sc[0:64, 8:16], -1.0, None, ALU.mult, ALU.bypass, ['gd_sc'], ['gd_sc'])
        TS(S, 'dve', sc[0:64, 56:64], sc[0:64, 0:8], -1.0, None, ALU.mult, ALU.bypass, ['gd_sc'], ['gd_sc'])
        pc, pck = C.pp()
        MM(S, pc[0:64, 0:8], C.maskU[0:64, :], sc[0:64, 8:16], ['maskU', 'gd_sc'], [pck])
        MM(S, pc[:, 8:16], self.ones[:, :], sc[0:64, 8:16], ['gd_ones', 'gd_sc'], [pck])
        ACTF(S, sc[0:64, 24:32], pc[0:64, 0:8], AF.Exp, [pck], ['gd_sc'])
        ACTF(S, sc[:, 48:56], pc[:, 8:16], AF.Exp, [pck], ['gd_sc'])
        COPY(S, 'act', self.ctmp[0:64, 0:16], pc[0:64, 0:16], [pck], ['gd_ctmp'])
        TT(S, 'dve', sc[0:64, 32:40], self.ctmp[0:64, 8:16], self.ctmp[0:64, 0:8], ALU.subtract, ['gd_ctmp'], ['gd_sc'])
        ACTF(S, sc[0:64, 32:40], sc[0:64, 32:40], AF.Exp, ['gd_sc'], ['gd_sc'])
        TT(S, 'dve', sc[0:64, 40:48], sc[0:64, 0:8], sc[0:64, 24:32], ALU.mult, ['gd_sc'], ['gd_sc'])
        import os
        yield
        preps = [self.prep(c) for c in range(NCH)]
        alive = list(preps)
        while alive:
            for g in list(alive):
                try:
                    next(g)
                except StopIteration:
                    alive.remove(g)
            yield
        for c in range(NCH):
            self.recur(blk, c)
            yield

    def prep(self, c):
        S, C = self.S, self.C
        y, qT, kT, kf, sc = self.y, self.qT, self.kT, self.kf, self.sc
        cs = slice(c * CH, (c + 1) * CH)
        X = self.pc[c]
        kx = lambda n: f'gdc{c}_{n}'
        pt, ptk = C.pp()
        TR(S, pt[0:64, 0:128], kf[:, cs], C.ident[:, :], ['gd_kf', 'ident'], [ptk])
        TS(S, 'dve', X['kbe'][:, :], pt[0:64, 0:128], sc[0:64, 40 + c:41 + c], None, ALU.mult, ALU.bypass, [ptk, 'gd_sc'], [kx('kbe')])
        TS(S, 'dve', X['kd'][:, :], pt[0:64, 0:128], sc[0:64, 32 + c:33 + c], None, ALU.mult, ALU.bypass, [ptk, 'gd_sc'], [kx('kd')])
        yield
        pv, pvk = C.pp()
        TR(S, pv[0:64, 0:128], y[:, 2, cs], C.ident[:, :], ['gd_y2', 'ident'], [pvk])
        TS(S, 'dve', X['bv'][:, :], pv[0:64, 0:128], sc[0:64, c:c + 1], None, ALU.mult, ALU.bypass, [pvk, 'gd_sc'], [kx('bv')])
        yield
        gl, GL, GT = X['gl'], X['GL'], X['GT']
        TS(S, 'pool', gl[:, :], C.maskU[0:64, :], sc[0:64, 16 + c:17 + c], sc[0:64, 8 + c:9 + c], ALU.mult, ALU.add, ['maskU', 'gd_sc'], [kx('gl')])
        pd, pdk = C.pp()
        MM(S, pd[0:64, 0:64], C.maskU[0:64, :], gl[:, :], ['maskU', kx('gl')], [pdk])
        MM(S, pd[0:64, 64:128], gl[:, :], C.maskU[0:64, :], ['maskU', kx('gl')], [pdk])
        ACTF(S, GL[:, :], pd[0:64, 0:64], AF.Exp, [pdk], [kx('GL')])
        ACTF(S, GT[:, :], pd[0:64, 64:128], AF.Exp, [pdk], [kx('GT')])
        yield
        TT(S, 'pool', GL[:, :], GL[:, :], C.maskLs[0:64, :], ALU.mult, [kx('GL'), 'maskLs'], [kx('GL')])
        TT(S, 'pool', GT[:, :], GT[:, :], C.maskU[0:64, :], ALU.mult, [kx('GT'), 'maskU'], [kx('GT')])
        pg, pgk = C.pp()
        MM(S, pg[0:64, 0:64], kT[:, cs], kT[:, cs], ['gd_kT'], [pgk])
        MM(S, pg[0:64, 64:128], kT[:, cs], qT[:, cs], ['gd_kT', 'gd_qT'], [pgk])
        TT(S, 'dve', X['att'][:, :], pg[0:64, 64:128], GT[:, :], ALU.mult, [pgk, kx('GT')], [kx('att')])
        P, PT, TTf = X['P'], X['PT'], X['TT']
        i = 0
        STT(S, 'dve', P[i][:, :], pg[0:64, 0:64], sc[0:64, 56 + c:57 + c], GL[:, :], ALU.mult, ALU.mult, [pgk, 'gd_sc', kx('GL')], [kx('P0')])
        yield
        pp_, ppk = C.pp()
        TR(S, pp_[0:64, 0:64], P[i][:, :], C.ident[0:64, 0:64], [kx('P0'), 'ident'], [ppk])
        COPY(S, 'act', PT[i][:, :], pp_[0:64, 0:64], [ppk], [kx('PT0')])
        TT(S, 'dve', TTf[i][:, :], PT[i][:, :], C.ident[0:64, 0:64], ALU.add, [kx('PT0'), 'ident'], [kx('TT0')])
        yield
        for it in range(5):
            j = 1 - i
            pq, pqk = C.pp()
            MM(S, pq[0:64, 0:64], PT[i][:, :], P[i][:, :], [kx(f'PT{i}'), kx(f'P{i}')], [pqk])
            COPY(S, 'act', P[j][:, :], pq[0:64, 0:64], [pqk], [kx(f'P{j}')])
            if it < 4:
                pq2, pq2k = C.pp()
                MM(S, pq2[0:64, 0:64], P[i][:, :], PT[i][:, :], [kx(f'PT{i}'), kx(f'P{i}')], [pq2k])
                COPY(S, 'dve', PT[j][:, :], pq2[0:64, 0:64], [pq2k], [kx(f'PT{j}')])
            pu, puk = C.pp()
            MM(S, pu[0:64, 0:64], P[j][:, :], TTf[i][:, :], [kx(f'P{j}'), kx(f'TT{i}')], [puk])
            TT(S, 'dve', TTf[j][:, :], pu[0:64, 0:64], TTf[i][:, :], ALU.add, [puk, kx(f'TT{i}')], [kx(f'TT{j}')])
            i = j
            yield
        COPY(S, 'act', X['TTb'][:, :], TTf[i][:, :], [kx(f'TT{i}')], [kx('TTb')])
        pw, pwk = C.pp()
        MM(S, pw[0:64, 0:128], X['TTb'][:, :], X['kbe'][:, :], [kx('kbe'), kx('TTb')], [pwk])
        MM(S, pw[0:64, 128:256], X['TTb'][:, :], X['bv'][:, :], [kx('bv'), kx('TTb')], [pwk])
        COPY(S, 'act', X['wtok'][:, :], pw[0:64, 0:128], [pwk], [kx('wtok')])
        COPY(S, 'act', X['u'][:, :], pw[0:64, 128:256], [pwk], [kx('u')])
        yield
        pm, pmk = C.pp()
        MM(S, pm[:, 0:128], X['wtok'][:, :], X['kd'][:, :], [kx('wtok'), kx('kd')], [pmk])
        STT(S, 'dve', X['MT'][:, :], C.ident[:, :], sc[:, 48 + c:49 + c], pm[:, 0:128], ALU.mult, ALU.subtract, ['ident', 'gd_sc', pmk], [kx('MT')])
        pn_, pn_k = C.pp()
        MM(S, pn_[:, 0:128], X['kd'][:, :], X['u'][:, :], [kx('kd'), kx('u')], [pn_k])
        COPY(S, 'act', X['Nn'][:, :], pn_[:, 0:128], [pn_k], [kx('Nn')])
        yield
        pq_, pq_k = C.pp()
        MM(S, pq_[:, 0:64], X['wtok'][:, :], X['att'][:, :], [kx('wtok'), kx('att')], [pq_k])
        ACTF(S, X['WqT'][:, :], pq_[:, 0:64], AF.Copy, [pq_k], [kx('WqT')], scale=-1.0)

    def recur(self, blk, c):
        S, C = self.S, self.C
        qT, sc = self.qT, self.sc
        cs = slice(c * CH, (c + 1) * CH)
        X = self.pc[c]
        kx = lambda n: f'gdc{c}_{n}'
        if c % 4 == 0:
            self.ost_cur = self.ost.next()
        ost, osk = self.ost_cur
        Sb, Sbk = self.Sb.cur()
        St, Stk = self.Stt.cur()
        pS, pSk = C.pp()
        MM(S, pS[:, 0:128], X['MT'][:, :], St[:, :], [kx('MT'), Stk], [pSk])
        St2, St2k = self.Stt.next()
        TT(S, 'dve', St2[:, :], pS[:, 0:128], X['Nn'][:, :], ALU.add, [pSk, kx('Nn')], [St2k])
        po, pok = C.pp()
        MM(S, po[0:64, 0:128], qT[:, cs], Sb[:, :], ['gd_qT', Sbk], [pok])
        o1, o1k = self.o1.next()
        TS(S, 'dve', o1[:, :], po[0:64, 0:128], sc[0:64, 24 + c:25 + c], None, ALU.mult, ALU.bypass, [pok, 'gd_sc'], [o1k])
        po2, po2k = C.pp()
        MM(S, po2[0:64, 0:128], X['WqT'][:, :], Sb[:, :], [kx('WqT'), Sbk], [po2k], start=True, stop=False)
        MM(S, po2[0:64, 0:128], X['att'][:, :], X['u'][:, :], [kx('att'), kx('u')], [po2k], start=False, stop=True)
        TT(S, 'dve', ost[:, c % 4, :], po2[0:64, 0:128], o1[:, :], ALU.add, [po2k, o1k], [osk])
        Sb2, Sb2k = self.Sb.next()
        COPY(S, 'act', Sb2[:, :], St2[:, :], [St2k], [Sb2k])
        if c % 4 == 3:
            norm_out(S, C, 'gd_', ost[:, :, :].rearrange("p c e -> p (c e)"), osk, c // 4, blk, self.out, center=False)

HALF_PI = 1.5707963267948966


def cplx_unit(S, pfx, theta, c, s, tmp, shape_sl, keys_r, piT, keys=None):
    kc, ks, kt = keys if keys is not None else (pfx + 'c', pfx + 's', pfx + 't')
    ACTF(S, s, theta, AF.Sin, keys_r, [ks], scale=1.0 / 16)
    ACTF(S, c, theta, AF.Sin, keys_r + ['halfpi'], [kc], scale=1.0 / 16, bias=piT)
    for _ in range(4):
        TT(S, 'dve', tmp, c, s, ALU.mult, [kc, ks], [kt])
        TT(S, 'dve', c, c, c, ALU.mult, [kc], [kc])
        TT(S, 'dve', s, s, s, ALU.mult, [ks], [ks])
        TT(S, 'dve', c, c, s, ALU.subtract, [kc, ks], [kc])
        TS(S, 'dve', s, tmp, 2.0, None, ALU.mult, ALU.bypass, [kt], [ks])


class S5:
    def __init__(self, S, nc, C):
        self.S, self.C = S, C
        d = lambda n, s: nc.dram_tensor(n, s, F32, kind="ExternalInput").ap()
        self.uT = d("s5_uT", [128, T])
        self.pP = d("s5_pP", [128, 3, 4])
        self.pF = d("s5_pF", [128, 3, 512])
        self.bF = d("s5_bF", [128, 2, 512])
        self.cP = d("s5_cP", [128, 2, 512])
        self.out = nc.dram_tensor("s5_out", [128, T], F32, kind="ExternalOutput").ap()
        sb = S.sbuf
        self.halfpi = sb([128, 1], F32, "s5_halfpi")
        S.op('pool', lambda e: e.memset(self.halfpi[:], HALF_PI), writes=['halfpi'])
        pP = sb([128, 3, 4], F32, "s5_pPs"); pF = sb([128, 3, 512], F32, "s5_pFs"); bF = sb([128, 2, 512], F32, "s5_bFs"); cP = sb([128, 2, 512], F32, "s5_cPs")
        S.dma('sp', pP[:, :, :], self.pP[:, :, :], writes=['s5_pP']); S.dma('sp', pF[:, :, :], self.pF[:, :, :], writes=['s5_pF'])
        S.dma('sp', bF[:, :, :], self.bF[:, :, :], writes=['s5_bF']); S.dma('sp', cP[:, :, :], self.cP[:, :, :], writes=['s5_cP'])
        W = 512
        self.w = [sb([128, BW], F32, f"s5_w{i}") for i in range(4)]
        self.z = [sb([128, BW], F32, f"s5_z{i}") for i in range(2)]
        self.xf = [sb([128, BW], F32, f"s5_xf{i}") for i in range(2)]
        dtF, magF, thF, cF, sF, tF, t2F = self.w[0], self.w[1], self.w[2], self.w[3], self.z[0], self.z[1], self.xf[0]
        ACTF(S, dtF[:, :], pF[:, 2, :], AF.Exp, ['s5_pF'], ['s5_w0'])
        TT(S, 'dve', thF[:, :], dtF[:, :], pF[:, 1, :], ALU.mult, ['s5_w0', 's5_pF'], ['s5_w2'])
        TT(S, 'dve', magF[:, :], dtF[:, :], pF[:, 0, :], ALU.mult, ['s5_w0', 's5_pF'], ['s5_w1'])
        ACTF(S, magF[:, :], magF[:, :], AF.Exp, ['s5_w1'], ['s5_w1'])
        cplx_unit(S, 's5F_', thF[:, :], cF[:, :], sF[:, :], tF[:, :], None, ['s5_w2'], self.halfpi[:, 0:1], keys=('s5_w3', 's5_z0', 's5_z1'))
        kc, ks = 's5_w3', 's5_z0'
        TT(S, 'dve', cF[:, :], cF[:, :], magF[:, :], ALU.mult, [kc, 's5_w1'], [kc])
        TS(S, 'dve', cF[:, :], cF[:, :], -1.0, None, ALU.add, ALU.bypass, [kc], [kc])
        TT(S, 'dve', sF[:, :], sF[:, :], magF[:, :], ALU.mult, [ks, 's5_w1'], [ks])
        TT(S, 'dve', dtF[:, :], pF[:, 0, :], pF[:, 0, :], ALU.mult, ['s5_pF'], ['s5_w0'])
        TT(S, 'dve', tF[:, :], pF[:, 1, :], pF[:, 1, :], ALU.mult, ['s5_pF'], ['s5_z1'])
        TT(S, 'dve', dtF[:, :], dtF[:, :], tF[:, :], ALU.add, ['s5_w0', 's5_z1'], ['s5_w0'])
        S.op('dve', lambda e: e.reciprocal(out=dtF[:, :], in_=dtF[:, :]), reads=['s5_w0'], writes=['s5_w0'])
        TT(S, 'dve', tF[:, :], cF[:, :], pF[:, 0, :], ALU.mult, [kc, 's5_pF'], ['s5_z1'])
        TT(S, 'dve', t2F[:, :], sF[:, :], pF[:, 1, :], ALU.mult, [ks, 's5_pF'], ['s5_xf0'])
        TT(S, 'dve', thF[:, :], tF[:, :], t2F[:, :], ALU.add, ['s5_z1', 's5_xf0'], ['s5_w2'])
        TT(S, 'dve', thF[:, :], thF[:, :], dtF[:, :], ALU.mult, ['s5_w2', 's5_w0'], ['s5_w2'])
        TT(S, 'dve', tF[:, :], sF[:, :], pF[:, 0, :], ALU.mult, [ks, 's5_pF'], ['s5_z1'])
        TT(S, 'dve', t2F[:, :], cF[:, :], pF[:, 1, :], ALU.mult, [kc, 's5_pF'], ['s5_xf0'])
        TT(S, 'dve', magF[:, :], tF[:, :], t2F[:, :], ALU.subtract, ['s5_z1', 's5_xf0'], ['s5_w1'])
        TT(S, 'dve', magF[:, :], magF[:, :], dtF[:, :], ALU.mult, ['s5_w1', 's5_w0'], ['s5_w1'])
        self.Bre = sb([128, 512], BF16, "s5_Bre"); self.Bim = sb([128, 512], BF16, "s5_Bim")
        TT(S, 'dve', tF[:, :], thF[:, :], bF[:, 0, :], ALU.mult, ['s5_w2', 's5_bF'], ['s5_z1'])
        TT(S, 'dve', t2F[:, :], magF[:, :], bF[:, 1, :], ALU.mult, ['s5_w1', 's5_bF'], ['s5_xf0'])
        TT(S, 'dve', self.Bre[:, :], tF[:, :], t2F[:, :], ALU.subtract, ['s5_z1', 's5_xf0'], ['s5_Bre'])
        TT(S, 'dve', tF[:, :], thF[:, :], bF[:, 1, :], ALU.mult, ['s5_w2', 's5_bF'], ['s5_z1'])
        TT(S, 'dve', t2F[:, :], magF[:, :], bF[:, 0, :], ALU.mult, ['s5_w1', 's5_bF'], ['s5_xf0'])
        TT(S, 'dve', self.Bim[:, :], tF[:, :], t2F[:, :], ALU.add, ['s5_z1', 's5_xf0'], ['s5_Bim'])
        self.Cre = sb([128, 512], BF16, "s5_Cre"); self.Cim = sb([128, 512], BF16, "s5_Cim")
        COPY(S, 'act', self.Cre[:, :], cP[:, 0, :], ['s5_cP'], ['s5_Cre'])
        ACTF(S, self.Cim[:, :], cP[:, 1, :], AF.Copy, ['s5_cP'], ['s5_Cim'], scale=-1.0)
        dtP = sb([128, 4], F32, "s5_dtP"); self.r = sb([128, 4], F32, "s5_r"); thP = sb([128, 4], F32, "s5_thP")
        self.E = sb([128, 3, 4], F32, "s5_E")
        self.c1 = sb([128, 2, 4], F32, "s5_c1")
        tP = sb([128, 4], F32, "s5_tP")
        ACTF(S, dtP[:, :], pP[:, 2, :], AF.Exp, ['s5_pP'], ['s5_dtP'])
        TT(S, 'dve', thP[:, :], dtP[:, :], pP[:, 1, :], ALU.mult, ['s5_dtP', 's5_pP'], ['s5_thP'])
        TT(S, 'dve', self.r[:, :], dtP[:, :], pP[:, 0, :], ALU.mult, ['s5_dtP', 's5_pP'], ['s5_r'])
        ACTF(S, self.r[:, :], self.r[:, :], AF.Exp, ['s5_r'], ['s5_r'])
        cplx_unit(S, 's5P_', thP[:, :], self.E[:, 0, :], self.E[:, 1, :], tP[:, :], None, ['s5_thP'], self.halfpi[:, 0:1])
        COPY(S, 'dve', self.c1[:, 0, :], self.E[:, 0, :], ['s5P_c'], ['s5_c1'])
        COPY(S, 'dve', self.c1[:, 1, :], self.E[:, 1, :], ['s5P_s'], ['s5_c1'])
        self.cr = sb([128, 4, 512], F32, "s5_cr"); self.ci = sb([128, 4, 512], F32, "s5_ci"); self.rt = sb([128, 1, 512], F32, "s5_rt")
        tt = sb([128, 256], F32, "s5_tt")
        S.op('pool', lambda e: e.memset(self.cr[:, :, 0:1], 1.0), writes=['s5_cr'])
        S.op('pool', lambda e: e.memset(self.ci[:, :, 0:1], 0.0), writes=['s5_ci'])
        S.op('pool', lambda e: e.memset(self.rt[:, :, :], 1.0), writes=['s5_rt'])
        for k in range(9):
            w = 1 << k
            TS(S, 'dve', self.E[:, 2, :], self.E[:, 1, :], -1.0, None, ALU.mult, ALU.bypass, ['s5P_s'], ['s5_En'])
            for q in range(4):
                Ec, Es, En = self.E[:, 0, q:q + 1], self.E[:, 1, q:q + 1], self.E[:, 2, q:q + 1]
                TS(S, 'dve', tt[:, 0:w], self.cr[:, q, 0:w], Ec, None, ALU.mult, ALU.bypass, ['s5_cr', 's5P_c'], ['s5_tt'])
                STT(S, 'dve', self.cr[:, q, w:2 * w], self.ci[:, q, 0:w], En, tt[:, 0:w], ALU.mult, ALU.add, ['s5_ci', 's5_En', 's5_tt', 's5_cr'], ['s5_cr'])
                TS(S, 'dve', tt[:, 0:w], self.ci[:, q, 0:w], Ec, None, ALU.mult, ALU.bypass, ['s5_ci', 's5P_c'], ['s5_tt'])
                STT(S, 'dve', self.ci[:, q, w:2 * w], self.cr[:, q, 0:w], Es, tt[:, 0:w], ALU.mult, ALU.add, ['s5_cr', 's5P_s', 's5_tt', 's5_ci'], ['s5_ci'])
            TT(S, 'dve', tP[:, :], self.E[:, 0, :], self.E[:, 1, :], ALU.mult, ['s5P_c', 's5P_s'], ['s5P_t'])
            TT(S, 'dve', self.E[:, 0, :], self.E[:, 0, :], self.E[:, 0, :], ALU.mult, ['s5P_c'], ['s5P_c'])
            TT(S, 'dve', self.E[:, 1, :], self.E[:, 1, :], self.E[:, 1, :], ALU.mult, ['s5P_s'], ['s5P_s'])
            TT(S, 'dve', self.E[:, 0, :], self.E[:, 0, :], self.E[:, 1, :], ALU.subtract, ['s5P_c', 's5P_s'], ['s5P_c'])
            TS(S, 'dve', self.E[:, 1, :], tP[:, :], 2.0, None, ALU.mult, ALU.bypass, ['s5P_t'], ['s5P_s'])
        self.zi = sb([128, 2, 4], F32, "s5_zi")
        S.op('pool', lambda e: e.memset(self.zi[:, :, :], 0.0), writes=['s5_zi'])
        self.ub = Rot(S, "s5_ub", [128, BW], BF16)
        self.bre = Rot(S, "s5_bre", [128, BW], F32); self.bim = Rot(S, "s5_bim", [128, BW], F32)
        self.xb = [Rot(S, f"s5_xb{i}", [128, BW], BF16) for i in range(2)]
        self.yst = Rot(S, "s5_yst", [128, BW], F32, 1)
        self.xe = sb([128, 4], F32, "s5_xe")

    def block(self, blk):
        S, C = self.S, self.C
        sl = slice(blk * BW, (blk + 1) * BW)
        ub, ubk = self.ub.next()
        S.dma('pool', ub[:, :], self.uT[:, sl], writes=[ubk])
        y_ps, yk = C.ob_s5()
        w, z, xf = self.w, self.z, self.xf
        for q in range(4):
            qs = slice(q * 128, (q + 1) * 128)
            p1, p1k = C.pp(); p2, p2k = C.pp()
            MM(S, p1[:, 0:BW], self.Bre[:, qs], ub[:, :], ['s5_Bre', ubk], [p1k])
            MM(S, p2[:, 0:BW], self.Bim[:, qs], ub[:, :], ['s5_Bim', ubk], [p2k])
            bre, brek = self.bre.next(); bim, bimk = self.bim.next()
            COPY(S, 'act', bre[:, :], p1[:, 0:BW], [p1k], [brek])
            COPY(S, 'act', bim[:, :], p2[:, 0:BW], [p2k], [bimk])
            cr, ci = self.cr[:, q, :], self.ci[:, q, :]
            TS(S, 'pool', self.rt[:, 0, :], cr, 0.0, self.r[:, q:q + 1], ALU.mult, ALU.add, ['s5_cr', 's5_r'], ['s5_rt'])
            TT(S, 'pool', w[0][:, :], bre[:, :], cr, ALU.mult, [brek, 's5_cr'], ['s5_w0'])
            TT(S, 'pool', w[1][:, :], bim[:, :], ci, ALU.mult, [bimk, 's5_ci'], ['s5_w1'])
            TT(S, 'pool', w[0][:, :], w[0][:, :], w[1][:, :], ALU.add, ['s5_w0', 's5_w1'], ['s5_w0'])
            TT(S, 'pool', w[2][:, :], bim[:, :], cr, ALU.mult, [bimk, 's5_cr'], ['s5_w2'])
            TT(S, 'pool', w[3][:, :], bre[:, :], ci, ALU.mult, [brek, 's5_ci'], ['s5_w3'])
            TT(S, 'pool', w[2][:, :], w[2][:, :], w[3][:, :], ALU.subtract, ['s5_w2', 's5_w3'], ['s5_w2'])
            S.op('dve', lambda e, q=q: e.tensor_tensor_scan(out=z[0][:, :], data0=self.rt[:, 0, :], data1=w[0][:, :], initial=self.zi[:, 0, q:q + 1], op0=ALU.mult, op1=ALU.add),
                 reads=['s5_rt', 's5_w0', 's5_zi'], writes=['s5_z0'])
            S.op('dve', lambda e, q=q: e.tensor_tensor_scan(out=z[1][:, :], data0=self.rt[:, 0, :], data1=w[2][:, :], initial=self.zi[:, 1, q:q + 1], op0=ALU.mult, op1=ALU.add),
                 reads=['s5_rt', 's5_w2', 's5_zi'], writes=['s5_z1'])
            TT(S, 'pool', w[1][:, :], z[0][:, :], cr, ALU.mult, ['s5_z0', 's5_cr'], ['s5_w1'])
            TT(S, 'pool', w[3][:, :], z[1][:, :], ci, ALU.mult, ['s5_z1', 's5_ci'], ['s5_w3'])
            TT(S, 'pool', xf[0][:, :], w[1][:, :], w[3][:, :], ALU.subtract, ['s5_w1', 's5_w3'], ['s5_xf0'])
            TT(S, 'pool', w[1][:, :], z[1][:, :], cr, ALU.mult, ['s5_z1', 's5_cr'], ['s5_w1'])
            TT(S, 'pool', w[3][:, :], z[0][:, :], ci, ALU.mult, ['s5_z0', 's5_ci'], ['s5_w3'])
            TT(S, 'pool', xf[1][:, :], w[1][:, :], w[3][:, :], ALU.add, ['s5_w1', 's5_w3'], ['s5_xf1'])
            xb0, xb0k = self.xb[0].next(); xb1, xb1k = self.xb[1].next()
            COPY(S, 'act', xb0[:, :], xf[0][:, :], ['s5_xf0'], [xb0k])
            COPY(S, 'act', xb1[:, :], xf[1][:, :], ['s5_xf1'], [xb1k])
            c1, s1 = self.c1[:, 0, q:q + 1], self.c1[:, 1, q:q + 1]
            xe = self.xe
            TS(S, 'dve', xe[:, 0:1], xf[0][:, BW - 1:BW], c1, None, ALU.mult, ALU.bypass, ['s5_xf0', 's5_c1'], ['s5_xe'])
            TS(S, 'dve', xe[:, 1:2], xf[1][:, BW - 1:BW], s1, None, ALU.mult, ALU.bypass, ['s5_xf1', 's5_c1'], ['s5_xe'])
            TT(S, 'dve', self.zi[:, 0, q:q + 1], xe[:, 0:1], xe[:, 1:2], ALU.subtract, ['s5_xe'], ['s5_zi'])
            TS(S, 'dve', xe[:, 2:3], xf[1][:, BW - 1:BW], c1, None, ALU.mult, ALU.bypass, ['s5_xf1', 's5_c1'], ['s5_xe'])
            TS(S, 'dve', xe[:, 3:4], xf[0][:, BW - 1:BW], s1, None, ALU.mult, ALU.bypass, ['s5_xf0', 's5_c1'], ['s5_xe'])
            TT(S, 'dve', self.zi[:, 1, q:q + 1], xe[:, 2:3], xe[:, 3:4], ALU.add, ['s5_xe'], ['s5_zi'])
            MM(S, y_ps[:, 0:BW], self.Cre[:, qs], xb0[:, :], ['s5_Cre', xb0k], [yk], start=(q == 0), stop=False)
            MM(S, y_ps[:, 0:BW], self.Cim[:, qs], xb1[:, :], ['s5_Cim', xb1k], [yk], start=False, stop=(q == 3))
            yield
        yst, ystk = self.yst.next()
        COPY(S, 'act', yst[:, :], y_ps[:, 0:BW], [yk], [ystk])
        S.dma('sp', self.out[:, sl], yst[:, :], reads=[ystk])


PW = 688
NPASS = 3
NTILE = 2
DFF = 2816
NHB = 22


def rmsnorm_pass(S, x_sb, g_sb, h_sb, ones, pp, sq, rstd, xkey, hkey, gkey):
    for tt in range(NTILE):
        sl = slice(tt * TW, (tt + 1) * TW)
        S.op('act', lambda e, sl=sl: e.activation(out=sq[:, :, :], in_=x_sb[:, :, sl], func=AF.Square), reads=[xkey], writes=['sq'])
        ps, pk = pp()
        for k in range(8):
            MM(S, ps[:, 0:TW], ones[:, :], sq[:, k, :], ['sq', 'ones'], [pk], start=(k == 0), stop=(k == 7))
        ACTF(S, rstd[:, :], ps[:, 0:TW], AF.Sqrt, [pk, 'epsb'], ['rstd'], scale=1.0 / 1024, bias=EPSB[0][:, 0:1])
        S.op('dve', lambda e: e.reciprocal(out=rstd[:, :], in_=rstd[:, :]), reads=['rstd'], writes=['rstd'])
        for k in range(8):
            STT(S, 'dve', h_sb[:, k, sl], x_sb[:, k, sl], g_sb[:, k:k + 1], rstd[:, :], ALU.mult, ALU.mult, [xkey, gkey, 'rstd'], [hkey])


def proj_pass(S, rhs_sb, rkey, w_dram, ncols, kchunks, wbufs, pp, evac, wname, colw=512):
    nblk = (ncols + colw - 1) // colw
    for cbk in range(nblk):
        c0 = cbk * colw
        w = min(colw, ncols - c0)
        i = proj_pass.cnt % len(wbufs)
        proj_pass.cnt += 1
        wt = wbufs[i][:, 0:kchunks * colw].rearrange("p (k c) -> p k c", c=colw)
        wk = f'wb{i}'
        S.dma('pool', wt[:, :, 0:w], w_dram[:, c0:c0 + w].rearrange("(k p) c -> p k c", p=128), writes=[wk])
        for sub in range(w // 128):
            for tt in range(NTILE):
                sl = slice(tt * TW, (tt + 1) * TW)
                ps, pk = pp()
                for k in range(kchunks):
                    MM(S, ps[:, 0:TW], wt[:, k, sub * 128:(sub + 1) * 128], rhs_sb[:, k, sl], [wk, rkey], [pk], start=(k == 0), stop=(k == kchunks - 1))
                evac(cbk * (colw // 128) + sub, tt, ps, pk)
proj_pass.cnt = 0


def build_C(final=False):
    nc = bass.Bass("TRN2", target_bir_lowering=False)
    din = lambda n, s: nc.dram_tensor(n, s, F32, kind="ExternalInput").ap()
    xT = din("xT", [1024, NT]); gm = din("gmix", [128, 8]); gf = din("gffn", [128, 8]); gfin = din("gfin", [128, 8])
    wG = din("wG", [1024, 1536])
    wGate = din("wGate", [1024, 4096])
    mixT = din("mixT", [4 * 512, NT])
    uT = din("uT", [512, NT])
    gains = din("gains", [128, 4, 4])
    wglu = din("wglu", [512, 512]); wbr = din("wbr", [4, 512, 1024]); wout = din("wout", [1024, 1024])
    wgu = din("wgu", [1024, 2 * DFF])
    wdn = din("wdn", [DFF, 1024])
    xo = nc.dram_tensor("xo", [1024, NT], F32, kind="ExternalOutput").ap()
    S = Sched(nc)
    sb = S.sbuf
    ones = load_consts_dense(S, nc)
    make_epsb(S)
    pp = PsPool(S, 8)
    x_sb = sb([128, 8, PW], F32, "x_sb"); h_sb = sb([128, 8, PW], BF16, "h_sb")
    g1 = sb([128, 8], F32, "g1"); g2 = sb([128, 8], F32, "g2"); g3 = sb([128, 8], F32, "g3"); gn = sb([128, 4, 4], F32, "gn")
    sq = sb([128, 8, TW], F32, "sq"); rstd = sb([128, TW], F32, "rstd")
    wbufs = [sb([128, 8 * 512], BF16, f"wbuf{i}") for i in range(3)]
    br = [sb([128, 4, PW], BF16, f"br{m}") for m in range(4)]
    yg = sb([128, 4, PW], F32, "yg"); ygb = sb([128, 4, PW], BF16, "ygb")
    acc = sb([128, 8, PW], F32, "acc"); mg = sb([128, 8, PW], BF16, "mg")
    hid = sb([128, NHB, PW], BF16, "hid")
    stg = [sb([128, PW], F32, f"stg{i}") for i in range(4)]
    tmp = [sb([128, TW], F32, f"tmp{i}") for i in range(4)]
    wbrs = [sb([128, 4, 1024], BF16, f"wbrs{i}") for i in range(1)]
    S.dma('sp', g1[:, :], gm[:, :], writes=['g1']); S.dma('sp', g2[:, :], gf[:, :], writes=['g2']); S.dma('sp', g3[:, :], gfin[:, :], writes=['g3'])
    S.dma('sp', gn[:, :, :], gains[:, :, :], writes=['gn'])
    cnt = [0]
    def stage(i):
        return stg[i % 4], f'stg{i % 4}'
    for ps_ in range(NPASS):
        t0 = ps_ * PW
        tsl = slice(t0, t0 + PW)
        for k in range(8):
            S.dma('sp', x_sb[:, k, :], xT[k * 128:(k + 1) * 128, tsl], pwrites=['x'])
        rmsnorm_pass(S, x_sb, g1, h_sb, ones, pp, sq, rstd, 'x', 'h', 'g1')
        def evac_g(cb, tt, ps, pk):
            m3, kb = cb // 4, cb % 4
            m = (0, 1, 3)[m3]
            sl = slice(tt * TW, (tt + 1) * TW)
            if tt == 0:
                st, sk = stage(cnt[0]); cnt[0] += 1
                evac_g.cur = (st, sk)
                S.dma('sp', st[:, :], mixT[m * 512 + kb * 128: m * 512 + (kb + 1) * 128, tsl], writes=[sk])
            st, sk = evac_g.cur
            tp, tk = tmp[cnt[0] % 2], f'tmp{cnt[0] % 2}'; cnt[0] += 1
            ACTF(S, tp[:, :], ps[:, 0:TW], AF.Silu, [pk], [tk])
            STT(S, 'dve', br[m][:, kb, sl], st[:, sl], gn[:, m, kb:kb + 1], tp[:, :], ALU.mult, ALU.mult, [sk, 'gn', tk], [f'br{m}'])
        proj_pass(S, h_sb, 'h', wG, 1536, 8, wbufs, pp, evac_g, 'w')
        for kb in range(4):
            st, sk = stage(cnt[0]); cnt[0] += 1
            st2, sk2 = stage(cnt[0]); cnt[0] += 1
            S.dma('sp', st[:, :], mixT[2 * 512 + kb * 128: 2 * 512 + (kb + 1) * 128, tsl], writes=[sk])
            S.dma('sp', st2[:, :], uT[kb * 128:(kb + 1) * 128, tsl], writes=[sk2])
            STT(S, 'dve', st[:, :], st2[:, :], gn[:, 2, kb:kb + 1], st[:, :], ALU.mult, ALU.add, [sk, sk2, 'gn'], [sk])
            TT(S, 'pool', st2[:, :], st[:, :], st[:, :], ALU.mult, [sk], [sk2])
            TS(S, 'pool', st2[:, :], st2[:, :], 0.044715, 1.0, ALU.mult, ALU.add, [sk2], [sk2])
            TT(S, 'pool', st2[:, :], st2[:, :], st[:, :], ALU.mult, [sk, sk2], [sk2])
            ACTF(S, st2[:, :], st2[:, :], AF.Sigmoid, [sk2], [sk2], scale=1.5957691216057308)
            TT(S, 'dve', yg[:, kb, :], st[:, :], st2[:, :], ALU.mult, [sk, sk2], ['yg'])
            COPY(S, 'act', ygb[:, kb, :], yg[:, kb, :], ['yg'], ['ygb'])
        def evac_glu(cb, tt, ps, pk):
            sl = slice(tt * TW, (tt + 1) * TW)
            tp, tk = tmp[cnt[0] % 2], f'tmp{cnt[0] % 2}'; cnt[0] += 1
            ACTF(S, tp[:, :], ps[:, 0:TW], AF.Sigmoid, [pk], [tk])
            TT(S, 'dve', br[2][:, cb, sl], yg[:, cb, sl], tp[:, :], ALU.mult, ['yg', tk], ['br2'])
        proj_pass(S, ygb, 'ygb', wglu, 512, 4, wbufs, pp, evac_glu, 'w')
        for m in range(4):
            wb, wbk = wbrs[0], 'wbrs0'
            S.dma('pool', wb[:, :, :], wbr[m].rearrange("(k p) c -> p k c", p=128), writes=[wbk])
            for half in range(2):
                i = proj_pass.cnt % 3; proj_pass.cnt += 1
                wt = wbufs[i][:, :].rearrange("p (k c) -> p k c", c=512); wk = f'wb{i}'
                c0 = m * 1024 + half * 512
                S.dma('pool', wt[:, :, :], wGate[:, c0:c0 + 512].rearrange("(k p) c -> p k c", p=128), writes=[wk])
                for sub in range(4):
                    ob = half * 4 + sub
                    for tt in range(NTILE):
                        sl = slice(tt * TW, (tt + 1) * TW)
                        pg, pgk = pp()
                        for k in range(8):
                            MM(S, pg[:, 0:TW], wt[:, k, sub * 128:(sub + 1) * 128], h_sb[:, k, sl], [wk, 'h'], [pgk], start=(k == 0), stop=(k == 7))
                        pb, pbk = pp()
                        for k in range(4):
                            MM(S, pb[:, 0:TW], wb[:, k, ob * 128:(ob + 1) * 128], br[m][:, k, sl], [wbk, f'br{m}'], [pbk], start=(k == 0), stop=(k == 3))
                        tp, tk = tmp[cnt[0] % 2], f'tmp{cnt[0] % 2}'; cnt[0] += 1
                        ACTF(S, tp[:, :], pg[:, 0:TW], AF.Sigmoid, [pgk], [tk])
                        if m == 0:
                            TT(S, 'dve', acc[:, ob, sl], pb[:, 0:TW], tp[:, :], ALU.mult, [pbk, tk], ['acc'])
                        else:
                            tp2, tk2 = tmp[2 + cnt[0] % 2], f'tmp{2 + cnt[0] % 2}'
                            TT(S, 'dve', tp2[:, :], pb[:, 0:TW], tp[:, :], ALU.mult, [pbk, tk], [tk2])
                            TT(S, 'pool', acc[:, ob, sl], acc[:, ob, sl], tp2[:, :], ALU.add, ['acc', tk2], ['acc'])
        for ob in range(8):
            COPY(S, 'act', mg[:, ob, :], acc[:, ob, :], ['acc'], ['mg'])
        def evac_res(cb, tt, ps, pk):
            sl = slice(tt * TW, (tt + 1) * TW)
            TT(S, 'dve', x_sb[:, cb, sl], ps[:, 0:TW], x_sb[:, cb, sl], ALU.add, [pk, 'x'], ['x'])
        proj_pass(S, mg, 'mg', wout, 1024, 8, wbufs, pp, evac_res, 'w')
        rmsnorm_pass(S, x_sb, g2, h_sb, ones, pp, sq, rstd, 'x', 'h', 'g2')
        def evac_gu(cb, tt, ps, pk):
            hb, isup = cb // 2, cb % 2
            sl = slice(tt * TW, (tt + 1) * TW)
            tp, tk = tmp[tt], f'tmp{tt}'
            if not isup:
                ACTF(S, tp[:, :], ps[:, 0:TW], AF.Silu, [pk], [tk])
            else:
                TT(S, 'dve', hid[:, hb, sl], ps[:, 0:TW], tp[:, :], ALU.mult, [pk, tk], ['hid'])
        proj_pass(S, h_sb, 'h', wgu, 2 * DFF, 8, wbufs, pp, evac_gu, 'w', colw=256)
        proj_pass(S, hid, 'hid', wdn, 1024, NHB, wbufs, pp, evac_res, 'w', colw=128)
        if final:
            rmsnorm_pass(S, x_sb, g3, acc, ones, pp, sq, rstd, 'x', 'acc', 'g3')
            for k in range(8):
                S.dma('sp', xo[k * 128:(k + 1) * 128, tsl], acc[:, k, :], reads=['acc'])
        else:
            for k in range(8):
                S.dma('sp', xo[k * 128:(k + 1) * 128, tsl], x_sb[:, k, :], reads=['x'])
    S.emit()
    return nc


def pad_T(a):
    out = np.zeros((T, a.shape[1]), np.float32); out[:a.shape[0]] = a; return out
def consts_B():
    s = np.arange(64)
    maskU = (s[:, None] <= s[None, :]).astype(np.float32)
    rmask = np.ones((128, 512), np.float32); rmask[:, ::64] = 0.0
    return {"maskU": maskU, "ident": np.eye(128, dtype=np.float32), "rmask": rmask}
def ret_tables(j):
    lg = np.log1p(-np.exp2(np.float32(-5.0 - j))).astype(np.float32)
    s = np.arange(64, dtype=np.float32)
    diff = s[None, :] - s[:, None]
    decT = np.where(diff >= 0, np.exp(lg * np.maximum(diff, 0)), 0).astype(np.float32)
    tm = (np.arange(512) % 64).astype(np.float32)
    xi = np.exp(lg * (tm + 1)).astype(np.float32); zeta = np.exp(lg * (63 - tm)).astype(np.float32)
    g64 = np.exp(lg * 64).astype(np.float32)
    tab = np.concatenate([decT, np.tile(xi[None], (64, 1)), np.tile(zeta[None], (64, 1)), np.full((64, 1), g64, np.float32)], 1)
    return np.ascontiguousarray(tab.astype(np.float32))
def rope_tables():
    pos = (np.arange(T) - 48).astype(np.float32)
    half = 32
    inv = (np.float32(10000.0) ** (-np.arange(half, dtype=np.float32) / half)).astype(np.float32)
    ang = pos[None, :] * inv[:, None]
    c = np.cos(ang).astype(np.float32); s_ = np.sin(ang).astype(np.float32)
    return np.concatenate([c, c], 0), np.concatenate([-s_, s_], 0)
def inputs_B_hg_rt(zb, j, lb_logits):
    m = {}
    m["hg_qT"] = np.ascontiguousarray(pad_T(zb[:, 64*j:64*j+64]).T)
    m["hg_fT"] = np.ascontiguousarray(pad_T(zb[:, 256+64*j:256+64*j+64]).T)
    m["hg_v"] = pad_T(zb[:, 512+128*j:512+128*j+128])
    m["hg_lg"] = np.ascontiguousarray(lb_logits[:, 64*j:64*j+64].T)
    perm = (np.arange(64) + 32) % 64
    q = pad_T(zb[:, 1536+64*j:1536+64*j+64]); k = pad_T(zb[:, 1792+64*j:1792+64*j+64])
    m["rt_qT"] = np.ascontiguousarray(q.T); m["rt_qpT"] = np.ascontiguousarray(q[:, perm].T)
    m["rt_kT"] = np.ascontiguousarray(k.T); m["rt_kpT"] = np.ascontiguousarray(k[:, perm].T)
    m["rt_v"] = pad_T(zb[:, 2048+128*j:2048+128*j+128])
    c, s_ = rope_tables(); m["rt_cos"] = c; m["rt_sin"] = s_
    m["rt_tab"] = ret_tables(j)
    return m

def inputs_B_gd(zb, j, conv_w, a_log, dt_bias):
    m = {}
    x = np.zeros((384, T + 3), np.float32)
    for mm in range(3):
        c0 = 3584 + 512 * mm + 128 * j
        x[mm*128:(mm+1)*128, 3:3+zb.shape[0]] = zb[:, c0:c0+128].T
    m["gd_x"] = x
    cw = np.zeros((128, 12), np.float32)
    for mm in range(3):
        cw[:, mm*4:(mm+1)*4] = conv_w[:, 512*mm + 128*j: 512*mm + 128*j + 128].T
    m["gd_cw"] = cw
    ba = np.zeros((64, 2, T // 64), np.float32)
    ba[:, 0, :] = pad_T(zb[:, 5120+j:5121+j])[:, 0].reshape(T // 64, 64).T
    ba[:, 1, :] = pad_T(zb[:, 5124+j:5125+j])[:, 0].reshape(T // 64, 64).T
    m["gd_ba"] = ba
    m["gd_par"] = np.tile(np.array([[a_log[j], dt_bias[j]]], np.float32), (64, 1))
    return m

def inputs_B_s5(zb, j, a_re, a_im, log_dt, b_re, b_im, c_re, c_im):
    m = {}
    m["s5_uT"] = np.ascontiguousarray(pad_T(zb[:, 3072 + 128 * j: 3072 + 128 * j + 128]).T)
    gs = np.arange(8 * j, 8 * j + 8)
    A = np.stack([a_re[gs], a_im[gs], np.tile(log_dt[gs][:, None], (1, 64))], 0)
    pP = A.reshape(3, 4, 2, 64).transpose(2, 3, 0, 1).reshape(128, 3, 4)
    pF = np.tile(A.reshape(3, 512)[None], (128, 1, 1))
    bF = np.zeros((128, 2, 8, 64), np.float32); cP = np.zeros((2, 64, 2, 4, 128), np.float32)
    for gl in range(8):
        g = gs[gl]
        bF[16 * gl:16 * gl + 16, 0, gl, :] = b_re[g].T
        bF[16 * gl:16 * gl + 16, 1, gl, :] = b_im[g].T
        q, g2 = gl // 2, gl % 2
        cP[g2, :, 0, q, 16 * gl:16 * gl + 16] = c_re[g].T
        cP[g2, :, 1, q, 16 * gl:16 * gl + 16] = c_im[g].T
    m["s5_pP"] = np.ascontiguousarray(pP.astype(np.float32)); m["s5_pF"] = np.ascontiguousarray(pF.astype(np.float32))
    m["s5_bF"] = np.ascontiguousarray(bF.reshape(128, 2, 512)); m["s5_cP"] = np.ascontiguousarray(cP.reshape(128, 2, 512))
    return m


A_COLS = np.r_[0:1024, 1536:2560, 3072:3584, 3584:5128]
NA_PAD = ((len(A_COLS) + 127) // 128) * 128
_CACHE = {}

def _prog(key, fn):
    if key not in _CACHE:
        _CACHE[key] = fn()
    return _CACHE[key]

def tok_rows(sg):
    return np.r_[48:64, 64 + sg * 2048: 64 + (sg + 1) * 2048]

def run_A(l, xTs, inp):
    w_in = np.asarray(inp["w_in"][l], np.float32)
    wA = np.zeros((1024, NA_PAD), np.float32); wA[:, :len(A_COLS)] = w_in[:, A_COLS]
    gm = np.ascontiguousarray(np.asarray(inp["norm_mix"][l], np.float32).reshape(8, 128).T)
    nc = _prog(('A',), lambda: build_A(NA_PAD))
    res = run_bass_kernel_spmd(nc, [{"xT": xTs[c], "gmix": gm, "wA": wA} for c in range(8)], core_ids=list(range(8)))
    z = np.zeros((2, 8256, 5640), np.float32)
    for c in range(8):
        b, sg = c // 4, c % 4
        zt = res.results[c]["zT"][:len(A_COLS)].T
        if sg == 0:
            z[b][48:64, A_COLS] = zt[:16]
        z[b][64 + sg * 2048: 64 + (sg + 1) * 2048, A_COLS] = zt[16:]
    return z

def run_B(l, z, inp):
    cst = consts_B()
    in_maps = []
    g = lambda k: np.asarray(inp[k][l], np.float32)
    for c in range(8):
        b, j = c // 4, c % 4
        m = inputs_B_hg_rt(z[b], j, np.asarray(inp["hg_lb_logits"], np.float32))
        m.update(inputs_B_gd(z[b], j, g('gdn_conv'), g('gdn_a_log'), g('gdn_dt_bias')))
        m.update(inputs_B_s5(z[b], j, g('ssm_a_re'), g('ssm_a_im'), g('ssm_log_dt'), g('ssm_b_re'), g('ssm_b_im'), g('ssm_c_re'), g('ssm_c_im')))
        m.update(cst)
        in_maps.append(m)
    nc = _prog(('B', l), lambda: build_B(l))
    res = run_bass_kernel_spmd(nc, in_maps, core_ids=list(range(8)))
    mix = np.zeros((2, 4, 8256, 512), np.float32)
    for c in range(8):
        b, j = c // 4, c % 4
        r = res.results[c]
        mix[b, 0][:, 128 * j:128 * j + 128] = r["hg_out"][:8256]
        mix[b, 1][:, 128 * j:128 * j + 128] = r["rt_out"][:8256]
        mix[b, 2][:, 128 * j:128 * j + 128] = r["s5_out"].T[:8256]
        mix[b, 3][:, 128 * j:128 * j + 128] = r["gd_out"][:8256]
    return mix

def run_C(l, xTs, z, mix, inp, final):
    g = lambda k: np.asarray(inp[k][l], np.float32)
    w_in = g("w_in")
    col = lambda v: np.ascontiguousarray(v.reshape(-1, 128).T)
    wG = np.ascontiguousarray(w_in[:, np.r_[1024:1536, 2560:3072, 5128:5640]])
    wGate = np.ascontiguousarray(w_in[:, 5640:9736])
    gains = np.stack([col(g("hg_norm")), col(g("ret_norm")), col(g("ssm_d")), col(g("gdn_norm"))], 1)
    w_gu = g("w_gu")
    order = np.concatenate([np.r_[hb * 128:(hb + 1) * 128, 2816 + hb * 128: 2816 + (hb + 1) * 128] for hb in range(22)])
    shared = {"gmix": col(g("norm_mix")), "gffn": col(g("norm_ffn")), "gfin": col(np.asarray(inp["norm_final"], np.float32)),
              "wG": wG, "wGate": wGate, "gains": np.ascontiguousarray(gains), "wglu": g("ssm_w_glu"), "wbr": g("w_branch"), "wout": g("w_out"),
              "wgu": np.ascontiguousarray(w_gu[:, order]), "wdn": g("w_down")}
    in_maps = []
    for c in range(8):
        b, sg = c // 4, c % 4
        rows = tok_rows(sg)
        m = dict(shared)
        m["xT"] = xTs[c]
        m["mixT"] = np.ascontiguousarray(mix[b][:, rows, :].transpose(0, 2, 1).reshape(2048, 2064))
        m["uT"] = np.ascontiguousarray(z[b][rows, 3072:3584].T)
        in_maps.append(m)
    nc = _prog(('C', final), lambda: build_C(final))
    res = run_bass_kernel_spmd(nc, in_maps, core_ids=list(range(8)))
    return [res.results[c]["xo"] for c in range(8)]

def initial_xT(inp):
    x = np.asarray(inp["x"], np.float32); meta = np.asarray(inp["meta"], np.float32)
    return [np.ascontiguousarray(np.concatenate([meta, x[c // 4, (c % 4) * 2048:(c % 4 + 1) * 2048]], 0).T) for c in range(8)]


def kernel(**inputs):
    inp = {k: np.asarray(v) for k, v in inputs.items()}
    xTs = initial_xT(inp)
    for l in range(4):
        z = run_A(l, xTs, inp)
        mix = run_B(l, z, inp)
        xTs = run_C(l, xTs, z, mix, inp, final=(l == 3))
    out = np.zeros((2, 8192, 1024), np.float32)
    for c in range(8):
        b, sg = c // 4, c % 4
        out[b, sg * 2048:(sg + 1) * 2048, :] = xTs[c][:, 16:].T
    return out
```

```python
import contextlib
import numpy as np
import concourse.bass as bass
import concourse.mybir as mybir
from concourse.bass_utils import run_bass_kernel_spmd

F32 = mybir.dt.float32
BF16 = mybir.dt.bfloat16
ALU = mybir.AluOpType
AF = mybir.ActivationFunctionType
AX = mybir.AxisListType


class Sched:
    ENG = ['pe', 'dve', 'act', 'pool', 'sp']

    def __init__(self, nc):
        self.nc = nc
        self.ops = {e: [] for e in self.ENG}
        self.cnt = {}
        self.seen = {e: {} for e in self.ENG}
        self.writers = {}
        self.readers = {}
        self.genwar = {}
        self.stack = contextlib.ExitStack()
        self.nt = 0
        self.dma_idx = {}
        import os; nq = int(os.environ.get('NQ', '48')); self.NQ = {'sp': nq, 'pool': max(1, nq // 2), 'act': 8, 'dve': 4, 'pe': 4}

    def sbuf(self, shape, dtype, name=None):
        self.nt += 1
        name = name or f"t{self.nt}"
        return self.stack.enter_context(self.nc.sbuf_tensor(name, list(shape), dtype))

    def psum(self, shape, dtype, name=None):
        self.nt += 1
        name = name or f"p{self.nt}"
        return self.stack.enter_context(self.nc.psum_tensor(name, list(shape), dtype))

    def op(self, eng, fn, reads=(), writes=(), dma=False, pwrites=()):
        if dma:
            idx = self.dma_idx.get(eng, 0)
            self.dma_idx[eng] = idx + 1
            sem = f"q_{eng}_{idx % self.NQ[eng]}"
        else:
            sem = eng
        deps = []
        if dma and self.cnt.get(sem, 0) > 0:
            deps.append((sem, self.cnt[sem]))
        same = (lambda s: (s == sem and not dma))
        for b in reads:
            deps.extend(self.writers.get(b, {}).items())
        for b in writes:
            for s_, v_ in self.writers.get(b, {}).items():
                if not same(s_):
                    deps.append((s_, v_))
            for r in self.readers.get(b, ()):
                if not same(r[0]):
                    deps.append(r)
        for b in pwrites:
            if self.readers.get(b):
                self.genwar[b] = self.readers[b]
                self.readers[b] = []
                self.writers[b] = {}
            for r in self.genwar.get(b, ()):
                if not same(r[0]):
                    deps.append(r)
        waits = {}
        for (s, v) in deps:
            if s == 'pe' and sem == 'pe':
                continue
            if v > waits.get(s, 0):
                waits[s] = v
        seen = self.seen[eng]
        wl = []
        for s, v in waits.items():
            if v > seen.get(s, 0):
                seen[s] = v
                wl.append((s, v))
        amt = 16 if dma else 1
        self.cnt[sem] = self.cnt.get(sem, 0) + amt
        val = self.cnt[sem]
        for b in writes:
            self.writers[b] = {sem: val}
            self.readers[b] = []
            self.genwar[b] = []
        for b in pwrites:
            self.writers.setdefault(b, {})[sem] = val
        for b in reads:
            self.readers.setdefault(b, []).append((sem, val))
        self.ops[eng].append((fn, wl, sem, amt))

    def dma(self, eng, out, in_, reads=(), writes=(), pwrites=(), **kw):
        self.op(eng, lambda e: e.dma_start(out=out, in_=in_, **kw), reads, writes, dma=True, pwrites=pwrites)

    def emit(self):
        nc = self.nc
        names = sorted(self.cnt.keys())
        sems = {n: self.stack.enter_context(nc.semaphore(n)) for n in names}
        final = dict(self.cnt)
        with nc.Block() as block:
            def mk(engname):
                def body(engine):
                    for (fn, wl, sem, amt) in self.ops[engname]:
                        for (s, v) in wl:
                            engine.wait_ge(sems[s], v)
                        ins = fn(engine)
                        ins.then_inc(sems[sem], amt)
                    if engname == 'sp':
                        for s, v in final.items():
                            engine.wait_ge(sems[s], v)
                return body
            block.tensor(mk('pe'))
            block.vector(mk('dve'))
            block.scalar(mk('act'))
            block.gpsimd(mk('pool'))
            block.sync(mk('sp'))
        self.stack.close()


NT = 2064
TW = 344
NTT = 6
EPS = 1e-6

def load_consts_dense(S, nc):
    ones = S.sbuf([128, 128], F32, "ones")
    S.op('pool', lambda e: e.memset(ones[:], 1.0), writes=['ones'])
    return ones

def stage_rmsnorm(S, x_sb, g_sb, h_sb, ones, ps_pool, tmp, xkey='x', hkey='h', gkey='g'):
    sq, rstd = tmp
    for tt in range(NTT):
        sl = slice(tt * TW, (tt + 1) * TW)
        S.op('act', lambda e, sl=sl: e.activation(out=sq[:, :, :], in_=x_sb[:, :, sl], func=AF.Square), reads=[xkey], writes=['sq'])
        ps, pk = ps_pool()
        for k in range(8):
            S.op('pe', lambda e, k=k, ps=ps: e.matmul(ps[:, 0:TW], lhsT=ones[:, :], rhs=sq[:, k, :], start=(k == 0), stop=(k == 7)),
                 reads=['sq', 'ones'], writes=[pk])
        S.op('act', lambda e, ps=ps: e.activation(out=rstd[:, :], in_=ps[:, 0:TW], func=AF.Sqrt, scale=1.0 / 1024, bias=EPSB[0][:, 0:1]), reads=[pk, 'epsb'], writes=['rstd'])
        S.op('dve', lambda e: e.reciprocal(out=rstd[:, :], in_=rstd[:, :]), reads=['rstd'], writes=['rstd'])
        for k in range(8):
            S.op('dve', lambda e, k=k, sl=sl: e.scalar_tensor_tensor(out=h_sb[:, k, sl], in0=x_sb[:, k, sl], scalar=g_sb[:, k:k + 1], in1=rstd[:, :], op0=ALU.mult, op1=ALU.mult),
                 reads=[xkey, gkey, 'rstd'], writes=[hkey])

EPSB = [None]
def make_epsb(S):
    t = S.sbuf([128, 1], F32, "epsb")
    S.op('pool', lambda e: e.memset(t[:], EPS), writes=['epsb'])
    EPSB[0] = t

def stage_proj(S, h_sb, w_dram, ncols, wbufs, ps_pool, evac, hkey='h', kchunks=8, wname='w'):
    nblk = (ncols + 511) // 512
    for cb in range(nblk):
        c0 = cb * 512
        w = min(512, ncols - c0)
        wt = wbufs[cb % len(wbufs)]
        wk = f'{wname}{cb % len(wbufs)}'
        S.dma('pool', wt[:, 0:kchunks, 0:w], w_dram[:, c0:c0 + w].rearrange("(k p) c -> p k c", p=128), writes=[wk])
        for sub in range(w // 128):
            for tt in range(NTT):
                sl = slice(tt * TW, (tt + 1) * TW)
                ps, pk = ps_pool()
                for k in range(kchunks):
                    S.op('pe', lambda e, k=k, ps=ps, wt=wt, sub=sub, sl=sl: e.matmul(ps[:, 0:TW], lhsT=wt[:, k, sub * 128:(sub + 1) * 128], rhs=h_sb[:, k, sl], start=(k == 0), stop=(k == kchunks - 1)),
                         reads=[wk, hkey], writes=[pk])
                evac(cb * 4 + sub, tt, ps, pk)

class PsPool:
    def __init__(self, S, n, shape=(128, 512), dtype=F32, prefix='ps'):
        self.tiles = [S.psum(list(shape), dtype, f"{prefix}{i}") for i in range(n)]
        self.prefix = prefix
        self.i = 0
    def __call__(self):
        t = self.tiles[self.i % len(self.tiles)]
        k = f"{self.prefix}{self.i % len(self.tiles)}"
        self.i += 1
        return t, k

def build_A(NA):
    nc = bass.Bass("TRN2", target_bir_lowering=False)
    xT = nc.dram_tensor("xT", [1024, NT], F32, kind="ExternalInput").ap()
    gm = nc.dram_tensor("gmix", [128, 8], F32, kind="ExternalInput").ap()
    wA = nc.dram_tensor("wA", [1024, NA], F32, kind="ExternalInput").ap()
    zT = nc.dram_tensor("zT", [NA, NT], F32, kind="ExternalOutput").ap()
    S = Sched(nc)
    x_sb = S.sbuf([128, 8, NT], F32, "x_sb")
    h_sb = S.sbuf([128, 8, NT], BF16, "h_sb")
    g_sb = S.sbuf([128, 8], F32, "g_sb")
    sq = S.sbuf([128, 8, TW], F32, "sq")
    rstd = S.sbuf([128, TW], F32, "rstd")
    wbufs = [S.sbuf([128, 8, 512], BF16, f"wb{i}") for i in range(3)]
    stg = [S.sbuf([128, NT], F32, f"stg{i}") for i in range(2)]
    ones = load_consts_dense(S, nc)
    make_epsb(S)
    pp = PsPool(S, 6)
    S.dma('sp', g_sb[:, :], gm[:, :], writes=['g'])
    for k in range(8):
        S.dma('sp', x_sb[:, k, :], xT[k * 128:(k + 1) * 128, :], pwrites=['x'])
    stage_rmsnorm(S, x_sb, g_sb, h_sb, ones, pp, (sq, rstd))
    cnt = [0]
    def evac(cb, tt, ps, pk):
        st = stg[cb % 2]; sk = f'stg{cb % 2}'
        sl = slice(tt * TW, (tt + 1) * TW)
        eng = 'act' if (cnt[0] % 2 == 0) else 'dve'
        cnt[0] += 1
        if eng == 'act':
            S.op('act', lambda e: e.activation(out=st[:, sl], in_=ps[:, 0:TW], func=AF.Copy), reads=[pk], writes=[sk])
        else:
            S.op('dve', lambda e: e.tensor_copy(out=st[:, sl], in_=ps[:, 0:TW]), reads=[pk], writes=[sk])
        if tt == NTT - 1:
            S.dma('sp', zT[cb * 128:(cb + 1) * 128, :], st[:, :], reads=[sk])
    stage_proj(S, h_sb, wA, NA, wbufs, pp, evac)
    S.emit()
    return nc


T = 8704
NBLK = 17
BW = 512
CH = 64
NCH = 8

def TT(S, eng, out, in0, in1, op, r, w):
    S.op(eng, lambda e: e.tensor_tensor(out=out, in0=in0, in1=in1, op=op), reads=r, writes=w)
def TS(S, eng, out, in0, s1, s2, op0, op1, r, w):
    S.op(eng, lambda e: e.tensor_scalar(out=out, in0=in0, scalar1=s1, scalar2=s2, op0=op0, op1=op1), reads=r, writes=w)
def STT(S, eng, out, in0, sc, in1, op0, op1, r, w):
    S.op(eng, lambda e: e.scalar_tensor_tensor(out=out, in0=in0, scalar=sc, in1=in1, op0=op0, op1=op1), reads=r, writes=w)
def ACTF(S, out, in_, func, r, w, scale=1.0, bias=None):
    if bias is None:
        S.op('act', lambda e: e.activation(out=out, in_=in_, func=func, scale=scale), reads=r, writes=w)
    else:
        S.op('act', lambda e: e.activation(out=out, in_=in_, func=func, scale=scale, bias=bias), reads=r, writes=w)
_PE_MODE = [None]
def _ru(n):
    return 32 if n <= 32 else (64 if n <= 64 else 128)
def _pe_mode(S, ap, tr):
    import os
    if os.environ.get('PE_DRAIN') is None:
        return
    m = (_ru(ap.shape[0]), _ru(ap.shape[1]), str(ap.dtype) == str(F32))
    if _PE_MODE[0] is not None and _PE_MODE[0] != m:
        S.op('pe', lambda e: e.drain(), reads=(), writes=())
    _PE_MODE[0] = m
def TTp(S, eng, out, in0, in1, op, r, wkey):
    S.op(eng, lambda e: e.tensor_tensor(out=out, in0=in0, in1=in1, op=op), reads=r, writes=[wkey])
def MM(S, out, lhsT, rhs, r, w, start=True, stop=True):
    _pe_mode(S, lhsT, False)
    S.op('pe', lambda e: e.matmul(out, lhsT=lhsT, rhs=rhs, start=start, stop=stop), reads=r, writes=w)
def TR(S, out, in_, ident, r, w):
    _pe_mode(S, in_, True)
    S.op('pe', lambda e: e.transpose(out, in_, ident), reads=r, writes=w)
def COPY(S, eng, out, in_, r, w, pw=()):
    if eng == 'act':
        S.op('act', lambda e: e.activation(out=out, in_=in_, func=AF.Copy), reads=r, writes=w, pwrites=pw)
    else:
        S.op(eng, lambda e: e.tensor_copy(out=out, in_=in_), reads=r, writes=w, pwrites=pw)
def RED(S, eng, out, in_, r, w):
    S.op(eng, lambda e: e.tensor_reduce(out=out, in_=in_, axis=AX.X, op=ALU.add), reads=r, writes=w)


class Ctx:
    pass


def norm_out(S, C, pfx, o_ps, ok, half, blk, out_dram, center):
    ost = C.ost[(blk * 2 + half) % 2]
    osk = f'ost{(blk * 2 + half) % 2}'
    o_ps = o_ps[0:64, 0:512]
    o3 = o_ps.rearrange("p (c e) -> p c e", e=128)
    sq3 = C.osq[0:64, :].rearrange("p (c e) -> p c e", e=128)
    ACTF(S, C.osq[0:64, :], o_ps, AF.Square, [ok], ['osq'])
    RED(S, 'dve', C.oss[0:64, 0:4], sq3, ['osq'], ['oss'])
    if center:
        RED(S, 'dve', C.oss[0:64, 4:8], o3, [ok], ['oss'])
        TS(S, 'dve', C.oss[0:64, 4:8], C.oss[0:64, 4:8], 1.0 / 128, None, ALU.mult, ALU.bypass, ['oss'], ['oss'])
        TT(S, 'dve', C.oss[0:64, 8:12], C.oss[0:64, 4:8], C.oss[0:64, 4:8], ALU.mult, ['oss'], ['oss'])
        STT(S, 'dve', C.oss[0:64, 0:4], C.oss[0:64, 0:4], 1.0 / 128, C.oss[0:64, 8:12], ALU.mult, ALU.subtract, ['oss'], ['oss'])
        ACTF(S, C.oss[0:64, 0:4], C.oss[0:64, 0:4], AF.Sqrt, ['oss', 'epsb'], ['oss'], scale=1.0, bias=EPSB[0][0:64, 0:1])
    else:
        ACTF(S, C.oss[0:64, 0:4], C.oss[0:64, 0:4], AF.Sqrt, ['oss', 'epsb'], ['oss'], scale=1.0 / 128, bias=EPSB[0][0:64, 0:1])
    S.op('dve', lambda e: e.reciprocal(out=C.oss[0:64, 0:4], in_=C.oss[0:64, 0:4]), reads=['oss'], writes=['oss'])
    for c in range(4):
        if center:
            TS(S, 'dve', ost[0:64, c, :], o3[:, c, :], C.oss[0:64, 4 + c:5 + c], C.oss[0:64, c:c + 1], ALU.subtract, ALU.mult, [ok, 'oss'], [osk])
        else:
            TS(S, 'dve', ost[0:64, c, :], o3[:, c, :], C.oss[0:64, c:c + 1], None, ALU.mult, ALU.bypass, [ok, 'oss'], [osk])
    t0 = blk * BW + half * 256
    S.dma('sp', out_dram[t0:t0 + 256, :].rearrange("(c p) e -> p c e", p=64), ost[0:64, :, :], reads=[osk])


class Hgrn:
    def __init__(self, S, nc, C, layer):
        self.S, self.C, self.layer = S, C, layer
        d = lambda n, s: nc.dram_tensor(n, s, F32, kind="ExternalInput").ap()
        self.qT = d("hg_qT", [64, T]); self.fT = d("hg_fT", [64, T]); self.v = d("hg_v", [T, 128])
        self.lg = d("hg_lg", [64, 4])
        self.out = nc.dram_tensor("hg_out", [T, 128], F32, kind="ExternalOutput").ap()
        sb = S.sbuf
        self.zq = [sb([64, BW], F32, f"hg_zq{i}") for i in range(1)]
        self.zf = [sb([64, BW], F32, f"hg_zf{i}") for i in range(1)]
        self.vb = [sb([64, NCH, 128], BF16, f"hg_vb{i}") for i in range(2)]
        self.lgs = sb([64, 4], F32, "hg_lgs"); self.lb = sb([64, 4], F32, "hg_lb")
        self.f = sb([64, BW], F32, "hg_f"); self.kT = sb([64, BW], F32, "hg_kT"); self.b = sb([64, BW], F32, "hg_b")
        self.bm = sb([64, BW], F32, "hg_bm"); self.e1 = sb([64, BW], F32, "hg_e1"); self.e2 = sb([64, BW], F32, "hg_e2")
        self.sq = sb([64, BW], F32, "hg_sq")
        self.qt = sb([64, BW], BF16, "hg_qt"); self.kt = sb([64, BW], BF16, "hg_kt"); self.ke = sb([64, BW], F32, "hg_ke")
        self.dec = sb([64, 16], F32, "hg_dec")
        self.kes = [sb([64, 64], BF16, f"hg_kes{i}") for i in range(2)]
        self.att = [sb([64, 64], BF16, f"hg_att{i}") for i in range(2)]
        self.Sst = sb([64, 128], F32, "hg_S"); self.Sb = [sb([64, 128], BF16, f"hg_Sb{i}") for i in range(2)]
        S.op('pool', lambda e: e.memset(self.Sst[:], 0.0), writes=['hg_S'])
        S.dma('sp', self.lgs[:, :], self.lg[:, :], writes=['hg_lgs'])
        S.op('dve', lambda e: e.tensor_reduce(out=self.lb[:, 0:1], in_=self.lgs[:, :], axis=AX.X, op=ALU.max), reads=['hg_lgs'], writes=['hg_lb'])
        TS(S, 'dve', self.lgs[:, :], self.lgs[:, :], self.lb[:, 0:1], None, ALU.subtract, ALU.bypass, ['hg_lgs', 'hg_lb'], ['hg_lgs'])
        ACTF(S, self.lgs[:, :], self.lgs[:, :], AF.Exp, ['hg_lgs'], ['hg_lgs'])
        RED(S, 'dve', self.lb[:, 1:2], self.lgs[:, :], ['hg_lgs'], ['hg_lb'])
        S.op('dve', lambda e: e.reciprocal(out=self.lb[:, 1:2], in_=self.lb[:, 1:2]), reads=['hg_lb'], writes=['hg_lb'])
        if layer == 0:
            S.op('dve', lambda e: e.memset(self.lb[:, 2:3], 0.0), reads=['hg_lb'], writes=['hg_lb'])
        else:
            RED(S, 'dve', self.lb[:, 2:3], self.lgs[:, 1:layer + 1], ['hg_lgs', 'hg_lb'], ['hg_lb'])
            TT(S, 'dve', self.lb[:, 2:3], self.lb[:, 2:3], self.lb[:, 1:2], ALU.mult, ['hg_lb'], ['hg_lb'])
        TS(S, 'dve', self.lb[:, 3:4], self.lb[:, 2:3], -1.0, 1.0, ALU.mult, ALU.add, ['hg_lb'], ['hg_lb'])

    def block(self, blk):
        S, C = self.S, self.C
        par = blk % 2
        zq, zf, vb = self.zq[0], self.zf[0], self.vb[par]
        kq, kf, kv_ = 'hg_zq0', 'hg_zf0', f'hg_vb{par}'
        sl = slice(blk * BW, (blk + 1) * BW)
        S.dma('sp', zq[:, :], self.qT[:, sl], writes=[kq])
        S.dma('sp', zf[:, :], self.fT[:, sl], writes=[kf])
        S.dma('pool', vb[:, :, :], self.v[sl, :].rearrange("(c p) e -> p c e", p=64), writes=[kv_])
        f, kT, b, bm, e1, e2, sq, qt, kt, ke, dec = self.f, self.kT, self.b, self.bm, self.e1, self.e2, self.sq, self.qt, self.kt, self.ke, self.dec
        ACTF(S, f[:, :], zf[:, :], AF.Sigmoid, [kf], ['hg_f'])
        TS(S, 'dve', f[:, :], f[:, :], self.lb[:, 3:4], self.lb[:, 2:3], ALU.mult, ALU.add, ['hg_f', 'hg_lb'], ['hg_f'])
        ACTF(S, b[:, :], f[:, :], AF.Ln, ['hg_f'], ['hg_b'])
        TS(S, 'pool', kT[:, :], f[:, :], -1.0, 1.0, ALU.mult, ALU.add, ['hg_f'], ['hg_kT'])
        S.op('dve', lambda e: e.tensor_tensor_scan(out=b[:, :], data0=C.rmask[0:64, :], data1=b[:, :], initial=0.0, op0=ALU.mult, op1=ALU.add), reads=['hg_b', 'rmask'], writes=['hg_b'])
        b3 = b[:, :].rearrange("p (c t) -> p c t", t=CH)
        v3 = lambda t: t[:, :].rearrange("p (c t) -> p c t", t=CH)
        TT(S, 'dve', v3(bm), b3, b3[:, :, 31:32].broadcast_to([64, NCH, CH]), ALU.subtract, ['hg_b'], ['hg_bm'])
        ACTF(S, e1[:, :], bm[:, :], AF.Exp, ['hg_bm'], ['hg_e1'])
        ACTF(S, e2[:, :], bm[:, :], AF.Exp, ['hg_bm'], ['hg_e2'], scale=-1.0)
        TT(S, 'dve', v3(bm), b3[:, :, 63:64].broadcast_to([64, NCH, CH]), b3, ALU.subtract, ['hg_b'], ['hg_bm'])
        ACTF(S, ke[:, :], bm[:, :], AF.Exp, ['hg_bm'], ['hg_ke'])
        ACTF(S, dec[:, 0:8], b3[:, :, 63], AF.Exp, ['hg_b'], ['hg_dec'])
        ACTF(S, dec[:, 8:16], b3[:, :, 31], AF.Exp, ['hg_b'], ['hg_dec'])
        ACTF(S, sq[:, :], zq[:, :], AF.Silu, [kq], ['hg_sq'])
        STT(S, 'dve', qt[:, :], sq[:, :], 0.125, e1[:, :], ALU.mult, ALU.mult, ['hg_sq', 'hg_e1'], ['hg_qt'])
        TT(S, 'pool', kt[:, :], kT[:, :], e2[:, :], ALU.mult, ['hg_kT', 'hg_e2'], ['hg_kt'])
        TT(S, 'pool', ke[:, :], kT[:, :], ke[:, :], ALU.mult, ['hg_kT', 'hg_ke'], ['hg_ke'])
        yield
        for c in range(NCH):
            cs = slice(c * CH, (c + 1) * CH)
            i2 = c % 2
            if c % 4 == 0:
                o_ps, ok = C.ob_hg()
            pt, ptk = C.pp()
            TR(S, pt[0:64, 0:64], ke[:, cs], C.ident[0:64, 0:64], ['hg_ke', 'ident'], [ptk])
            COPY(S, 'act', self.kes[i2][:, :], pt[0:64, 0:64], [ptk], [f'hg_kes{i2}'])
            pa, pak = C.pp()
            MM(S, pa[0:64, 0:64], kt[:, cs], qt[:, cs], ['hg_kt', 'hg_qt'], [pak])
            TT(S, 'dve', self.att[i2][:, :], pa[0:64, 0:64], C.maskU[0:64, :], ALU.mult, [pak, 'maskU'], [f'hg_att{i2}'])
            pk, pkk = C.pp()
            MM(S, pk[0:64, 0:128], self.kes[i2][:, :], vb[:, c, :], [f'hg_kes{i2}', kv_], [pkk])
            TS(S, 'pool', self.Sb[i2][:, :], self.Sst[:, :], dec[:, 8 + c:9 + c], None, ALU.mult, ALU.bypass, ['hg_S', 'hg_dec'], [f'hg_Sb{i2}'])
            oc = o_ps[0:64, (c % 4) * 128:(c % 4 + 1) * 128]
            MM(S, oc, self.att[i2][:, :], vb[:, c, :], [f'hg_att{i2}', kv_], [ok], start=True, stop=False)
            MM(S, oc, qt[:, cs], self.Sb[i2][:, :], ['hg_qt', f'hg_Sb{i2}'], [ok], start=False, stop=True)
            STT(S, 'dve', self.Sst[:, :], self.Sst[:, :], dec[:, c:c + 1], pk[0:64, 0:128], ALU.mult, ALU.add, ['hg_S', 'hg_dec', pkk], ['hg_S'])
            if c % 4 == 3:
                norm_out(S, C, 'hg_', o_ps, ok, c // 4, blk, self.out, center=False)
            yield


class Ret:
    def __init__(self, S, nc, C):
        self.S, self.C = S, C
        d = lambda n, s: nc.dram_tensor(n, s, F32, kind="ExternalInput").ap()
        self.qT = d("rt_qT", [64, T]); self.qpT = d("rt_qpT", [64, T]); self.kT = d("rt_kT", [64, T]); self.kpT = d("rt_kpT", [64, T])
        self.v = d("rt_v", [T, 128]); self.cos = d("rt_cos", [64, T]); self.sin = d("rt_sin", [64, T])
        self.tab = d("rt_tab", [64, 64 + 512 + 512 + 1])
        self.out = nc.dram_tensor("rt_out", [T, 128], F32, kind="ExternalOutput").ap()
        sb = S.sbuf
        self.inb = [[sb([64, BW], F32, f"rt_in{j}_{i}") for j in range(6)] for i in range(1)]
        self.vb = [sb([64, NCH, 128], BF16, f"rt_vb{i}") for i in range(2)]
        self.tabs = sb([64, 64 + 512 + 512 + 1], F32, "rt_tabs")
        self.t1 = sb([64, BW], F32, "rt_t1"); self.t2 = sb([64, BW], F32, "rt_t2")
        self.qr = sb([64, BW], BF16, "rt_qr"); self.qx = sb([64, BW], BF16, "rt_qx"); self.kr = sb([64, BW], BF16, "rt_kr"); self.kz = sb([64, BW], F32, "rt_kz")
        self.kzs = [sb([64, 64], BF16, f"rt_kzs{i}") for i in range(2)]
        self.att = [sb([64, 64], BF16, f"rt_att{i}") for i in range(2)]
        self.R = sb([64, 128], F32, "rt_R"); self.Rb = [sb([64, 128], BF16, f"rt_Rb{i}") for i in range(2)]
        S.op('pool', lambda e: e.memset(self.R[:], 0.0), writes=['rt_R'])
        S.dma('sp', self.tabs[:, :], self.tab[:, :], writes=['rt_tabs'])

    def block(self, blk):
        S, C = self.S, self.C
        par = blk % 2
        sl = slice(blk * BW, (blk + 1) * BW)
        ib = self.inb[0]; vb = self.vb[par]
        ik = [f'rt_in{j}_0' for j in range(6)]
        kv_ = f'rt_vb{par}'
        for j, src in enumerate([self.qT, self.qpT, self.kT, self.kpT, self.cos, self.sin]):
            S.dma('sp', ib[j][:, :], src[:, sl], writes=[ik[j]])
        S.dma('pool', vb[:, :, :], self.v[sl, :].rearrange("(c p) e -> p c e", p=64), writes=[kv_])
        decT = self.tabs[:, 0:64]; xi = self.tabs[:, 64:576]; zeta = self.tabs[:, 576:1088]; g64 = self.tabs[:, 1088:1089]
        t1, t2, qr, qx, kr, kz = self.t1, self.t2, self.qr, self.qx, self.kr, self.kz
        TT(S, 'dve', t1[:, :], ib[0][:, :], ib[4][:, :], ALU.mult, [ik[0], ik[4]], ['rt_t1'])
        TT(S, 'pool', t2[:, :], ib[1][:, :], ib[5][:, :], ALU.mult, [ik[1], ik[5]], ['rt_t2'])
        TT(S, 'dve', t1[:, :], t1[:, :], t2[:, :], ALU.add, ['rt_t1', 'rt_t2'], ['rt_t1'])
        COPY(S, 'act', qr[:, :], t1[:, :], ['rt_t1'], ['rt_qr'])
        TT(S, 'dve', qx[:, :], t1[:, :], xi, ALU.mult, ['rt_t1', 'rt_tabs'], ['rt_qx'])
        TT(S, 'dve', t1[:, :], ib[2][:, :], ib[4][:, :], ALU.mult, [ik[2], ik[4], 'rt_qr', 'rt_qx'], ['rt_t1'])
        TT(S, 'pool', t2[:, :], ib[3][:, :], ib[5][:, :], ALU.mult, [ik[3], ik[5]], ['rt_t2'])
        TT(S, 'dve', t1[:, :], t1[:, :], t2[:, :], ALU.add, ['rt_t1', 'rt_t2'], ['rt_t1'])
        ACTF(S, kr[:, :], t1[:, :], AF.Copy, ['rt_t1'], ['rt_kr'], scale=0.125)
        STT(S, 'dve', kz[:, :], t1[:, :], 0.125, zeta, ALU.mult, ALU.mult, ['rt_t1', 'rt_tabs'], ['rt_kz'])
        yield
        for c in range(NCH):
            cs = slice(c * CH, (c + 1) * CH)
            i2 = c % 2
            if c % 4 == 0:
                o_ps, ok = C.ob_rt()
            pt, ptk = C.pp()
            TR(S, pt[0:64, 0:64], kz[:, cs], C.ident[0:64, 0:64], ['rt_kz', 'ident'], [ptk])
            COPY(S, 'act', self.kzs[i2][:, :], pt[0:64, 0:64], [ptk], [f'rt_kzs{i2}'])
            pa, pak = C.pp()
            MM(S, pa[0:64, 0:64], kr[:, cs], qr[:, cs], ['rt_kr', 'rt_qr'], [pak])
            TT(S, 'dve', self.att[i2][:, :], pa[0:64, 0:64], decT, ALU.mult, [pak, 'rt_tabs'], [f'rt_att{i2}'])
            pk, pkk = C.pp()
            MM(S, pk[0:64, 0:128], self.kzs[i2][:, :], vb[:, c, :], [f'rt_kzs{i2}', kv_], [pkk])
            COPY(S, 'pool', self.Rb[i2][:, :], self.R[:, :], ['rt_R'], [f'rt_Rb{i2}'])
            oc = o_ps[0:64, (c % 4) * 128:(c % 4 + 1) * 128]
            MM(S, oc, self.att[i2][:, :], vb[:, c, :], [f'rt_att{i2}', kv_], [ok], start=True, stop=False)
            MM(S, oc, qx[:, cs], self.Rb[i2][:, :], ['rt_qx', f'rt_Rb{i2}'], [ok], start=False, stop=True)
            STT(S, 'dve', self.R[:, :], self.R[:, :], g64, pk[0:64, 0:128], ALU.mult, ALU.add, ['rt_R', 'rt_tabs', pkk], ['rt_R'])
            if c % 4 == 3:
                norm_out(S, C, 'rt_', o_ps, ok, c // 4, blk, self.out, center=True)
            yield


def make_ctx(S, nc, n_general=6):
    C = Ctx()
    d = lambda n, s: nc.dram_tensor(n, s, F32, kind="ExternalInput").ap()
    C.maskU_d = d("maskU", [64, 64]); C.ident_d = d("ident", [128, 128]); C.rmask_d = d("rmask", [128, 512])
    C.maskU = S.sbuf([64, 64], F32, "maskU_s"); C.ident = S.sbuf([128, 128], F32, "ident_s"); C.rmask = S.sbuf([128, 512], F32, "rmask_s")
    S.dma('sp', C.maskU[:, :], C.maskU_d[:, :], writes=['maskU'])
    S.dma('sp', C.ident[:, :], C.ident_d[:, :], writes=['ident'])
    S.dma('sp', C.rmask[:, :], C.rmask_d[:, :], writes=['rmask'])
    make_epsb(S)
    C.ones128 = S.sbuf([128, 128], F32, "ones128")
    S.op('pool', lambda e: e.memset(C.ones128[:], 1.0), writes=['ones128'])
    C.maskLs = S.sbuf([64, 64], F32, "maskLs")
    TS(S, 'dve', C.maskLs[:, :], C.maskU[:, :], -1.0, 1.0, ALU.mult, ALU.add, ['maskU'], ['maskLs'])
    C.pp = PsPool(S, 5)
    C.ob_hg = PsPool(S, 1, prefix='pohg'); C.ob_rt = PsPool(S, 1, prefix='port'); C.ob_s5 = PsPool(S, 1, prefix='pos5')
    C.ost = [S.sbuf([128, 4, 128], F32, f"ost{i}") for i in range(2)]
    C.osq = S.sbuf([128, 512], F32, "osq")
    C.oss = S.sbuf([128, 16], F32, "oss")
    return C


def build_B(layer, mixers=('hg', 'rt', 's5', 'gd'), nblk=NBLK):
    nc = bass.Bass("TRN2", target_bir_lowering=False)
    S = Sched(nc)
    C = make_ctx(S, nc)
    ms = []
    if 'hg' in mixers: ms.append(Hgrn(S, nc, C, layer))
    if 'rt' in mixers: ms.append(Ret(S, nc, C))
    if 'gd' in mixers: ms.append(Gdn(S, nc, C))
    if 's5' in mixers: ms.append(S5(S, nc, C))
    for blk in range(nblk):
        active = [m.block(blk) for m in ms]
        while active:
            for g in list(active):
                try:
                    next(g)
                except StopIteration:
                    active.remove(g)
    S.emit()
    return nc


class Rot:
    def __init__(self, S, name, shape, dtype, n=2):
        self.t = [S.sbuf(shape, dtype, f"{name}{i}") for i in range(n)]
        self.name = name; self.i = -1
    def next(self):
        self.i += 1
        return self.cur()
    def cur(self):
        j = self.i % len(self.t)
        return self.t[j], f"{self.name}{j}"


import os
PDT = BF16 if os.environ.get('GD_BF') else F32

class Gdn:
    def __init__(self, S, nc, C):
        self.S, self.C = S, C
        d = lambda n, s: nc.dram_tensor(n, s, F32, kind="ExternalInput").ap()
        self.x = d("gd_x", [384, T + 3]); self.cw = d("gd_cw", [128, 12]); self.ba = d("gd_ba", [64, 2, T // 64]); self.par = d("gd_par", [64, 2])
        self.out = nc.dram_tensor("gd_out", [T, 128], F32, kind="ExternalOutput").ap()
        sb = S.sbuf
        self.xin = Rot(S, "gd_xin", [128, 3, BW + 3], F32, 1)
        self.cws = sb([128, 12], F32, "gd_cws"); self.bas = sb([64, 2, T // 64], F32, "gd_bas"); self.pars = sb([64, 4], F32, "gd_pars")
        self.y = sb([128, 3, BW], F32, "gd_y"); self.sq = sb([128, 2, BW], F32, "gd_sq"); self.rs = self.sq
        self.qT = sb([128, BW], BF16, "gd_qT"); self.kT = sb([128, BW], BF16, "gd_kT"); self.kf = sb([128, BW], F32, "gd_kf")
        self.sc = sb([128, 64], F32, "gd_sc")
        self.pc = []
        for c in range(NCH):
            X = {}
            for n in ('gl', 'GL', 'GT'):
                X[n] = sb([64, 64], F32, f"gdc{c}_{n}")
            X['P'] = [sb([64, 64], F32, f"gdc{c}_P{i}") for i in range(2)]
            X['PT'] = [sb([64, 64], F32, f"gdc{c}_PT{i}") for i in range(2)]
            X['TT'] = [sb([64, 64], F32, f"gdc{c}_TT{i}") for i in range(2)]
            X['TTb'] = sb([64, 64], BF16, f"gdc{c}_TTb"); X['att'] = sb([64, 64], BF16, f"gdc{c}_att")
            for n in ('kbe', 'kd', 'bv'):
                X[n] = sb([64, 128], BF16, f"gdc{c}_{n}")
            X['WqT'] = sb([128, 64], BF16, f"gdc{c}_WqT"); X['wtok'] = sb([64, 128], BF16, f"gdc{c}_wtok"); X['u'] = sb([64, 128], BF16, f"gdc{c}_u")
            X['MT'] = sb([128, 128], F32, f"gdc{c}_MT"); X['Nn'] = sb([128, 128], F32, f"gdc{c}_Nn")
            self.pc.append(X)
        self.o1 = Rot(S, "gd_o1", [64, 128], F32)
        self.Stt = Rot(S, "gd_S", [128, 128], F32); self.Sb = Rot(S, "gd_Sb", [128, 128], BF16)
        self.ost = Rot(S, "gd_ost", [64, 4, 128], F32)
        self.ones = sb([64, 128], F32, "gd_ones"); self.ctmp = sb([64, 16], F32, "gd_ctmp")
        S.op('pool', lambda e: e.memset(self.ones[:], 1.0), writes=['gd_ones'])
        st0, st0k = self.Stt.next()
        S.op('pool', lambda e: e.memset(st0[:], 0.0), writes=[st0k])
        sbt, sbk = self.Sb.next()
        S.op('pool', lambda e: e.memset(sbt[:], 0.0), writes=[sbk])
        S.dma('sp', self.cws[:, :], self.cw[:, :], writes=['gd_cws'])
        S.dma('sp', self.bas[:, :, :], self.ba[:, :, :], writes=['gd_bas'])
        S.dma('sp', self.pars[:, 0:2], self.par[:, :], writes=['gd_pars'])
        ACTF(S, self.pars[:, 2:3], self.pars[:, 0:1], AF.Exp, ['gd_pars'], ['gd_pars'])
        TS(S, 'dve', self.pars[:, 2:3], self.pars[:, 2:3], -1.0, None, ALU.mult, ALU.bypass, ['gd_pars'], ['gd_pars'])

    def block(self, blk):
        S, C = self.S, self.C
        xin, xk = self.xin.next()
        for m in range(3):
            S.dma('sp', xin[:, m, :], self.x[m * 128:(m + 1) * 128, blk * BW: blk * BW + BW + 3], pwrites=[xk])
        y, sq, rs, qT, kT, kf, sc = self.y, self.sq, self.rs, self.qT, self.kT, self.kf, self.sc
        for m in range(3):
            eng = 'dve'
            TS(S, eng, y[:, m, :], xin[:, m, 0:BW], self.cws[:, 4 * m:4 * m + 1], None, ALU.mult, ALU.bypass, [xk, 'gd_cws'], [f'gd_y{m}'])
            for j in range(1, 4):
                STT(S, eng, y[:, m, :], xin[:, m, j:j + BW], self.cws[:, 4 * m + j:4 * m + j + 1], y[:, m, :], ALU.mult, ALU.add, [xk, 'gd_cws', f'gd_y{m}'], [f'gd_y{m}'])
            ACTF(S, y[:, m, :], y[:, m, :], AF.Silu, [f'gd_y{m}'], [f'gd_y{m}'])
        ACTF(S, sq[:, :, :], y[:, 0:2, :], AF.Square, ['gd_y0', 'gd_y1'], ['gd_sq'])
        for m in range(2):
            ps, pk = C.pp()
            MM(S, ps[:, 0:BW], C.ones128[:, :], sq[:, m, :], ['ones128', 'gd_sq'], [pk])
            ACTF(S, rs[:, m, :], ps[:, 0:BW], AF.Sqrt, [pk, 'epsb', 'gd_sq'], ['gd_sq'], bias=EPSB[0][:, 0:1])
        S.op('dve', lambda e: e.reciprocal(out=rs[:, :, :], in_=rs[:, :, :]), reads=['gd_sq'], writes=['gd_sq'])
        STT(S, 'dve', qT[:, :], y[:, 0, :], 128 ** -0.5, rs[:, 0, :], ALU.mult, ALU.mult, ['gd_y0', 'gd_sq'], ['gd_qT'])
        TT(S, 'dve', kf[:, :], y[:, 1, :], rs[:, 1, :], ALU.mult, ['gd_y1', 'gd_sq'], ['gd_kf'])
        COPY(S, 'act', kT[:, :], kf[:, :], ['gd_kf'], ['gd_kT'])
        c0 = blk * NCH
        ACTF(S, sc[0:64, 0:8], self.bas[:, 0, c0:c0 + 8], AF.Sigmoid, ['gd_bas'], ['gd_sc'])
        ACTF(S, sc[0:64, 8:16], self.bas[:, 1, c0:c0 + 8], AF.Exp, ['gd_bas', 'gd_pars'], ['gd_sc'], bias=self.pars[:, 1:2])
        ACTF(S, sc[0:64, 8:16], sc[0:64, 8:16], AF.Ln, ['gd_sc'], ['gd_sc'], bias=1.0)
        TS(S, 'dve', sc[0:64, 8:16], sc[0:64, 8:16], self.pars[:, 2:3], None, ALU.mult, ALU.bypass, ['gd_sc', 'gd_pars'], ['gd_sc'])
        TS(S, 'dve', sc[0:64, 16:24], sc[0:64, 8:16], -1.0, None, ALU.mult, ALU.bypass, ['gd_sc'], ['gd_sc'])
        TS(S, 'dve', sc[0:64, 56:64], sc[0:64, 0:8], -1.0, None, ALU.mult, ALU.bypass, ['gd_sc'], ['gd_sc'])
        pc, pck = C.pp()
        MM(S, pc[0:64, 0:8], C.maskU[0:64, :], sc[0:64, 8:16], ['maskU', 'gd_sc'], [pck])
        MM(S, pc[:, 8:16], self.ones[:, :], sc[0:64, 8:16], ['gd_ones', 'gd_sc'], [pck])
        ACTF(S, sc[0:64, 24:32], pc[0:64, 0:8], AF.Exp, [pck], ['gd_sc'])
        ACTF(S, sc[:, 48:56], pc[:, 8:16], AF.Exp, [pck], ['gd_sc'])
        COPY(S, 'act', self.ctmp[0:64, 0:16], pc[0:64, 0:16], [pck], ['gd_ctmp'])
        TT(S, 'dve', sc[0:64, 32:40], self.ctmp[0:64, 8:16], self.ctmp[0:64, 0:8], ALU.subtract, ['gd_ctmp'], ['gd_sc'])
        ACTF(S, sc[0:64, 32:40], sc[0:64, 32:40], AF.Exp, ['gd_sc'], ['gd_sc'])
        TT(S, 'dve', sc[0:64, 40:48], sc[0:64, 0:8], sc[0:64, 24:32], ALU.mult, ['gd_sc'], ['gd_sc'])
        import os
        yield
        preps = [self.prep(c) for c in range(NCH)]
        alive = list(preps)
        while alive:
            for g in list(alive):
                try:
                    next(g)
                except StopIteration:
                    alive.remove(g)
            yield
        for c in range(NCH):
            self.recur(blk, c)
            yield

    def prep(self, c):
        S, C = self.S, self.C
        y, qT, kT, kf, sc = self.y, self.qT, self.kT, self.kf, self.sc
        cs = slice(c * CH, (c + 1) * CH)
        X = self.pc[c]
        kx = lambda n: f'gdc{c}_{n}'
        pt, ptk = C.pp()
        TR(S, pt[0:64, 0:128], kf[:, cs], C.ident[:, :], ['gd_kf', 'ident'], [ptk])
        TS(S, 'dve', X['kbe'][:, :], pt[0:64, 0:128], sc[0:64, 40 + c:41 + c], None, ALU.mult, ALU.bypass, [ptk, 'gd_sc'], [kx('kbe')])
        TS(S, 'dve', X['kd'][:, :], pt[0:64, 0:128], sc[0:64, 32 + c:33 + c], None, ALU.mult, ALU.bypass, [ptk, 'gd_sc'], [kx('kd')])
        yield
        pv, pvk = C.pp()
        TR(S, pv[0:64, 0:128], y[:, 2, cs], C.ident[:, :], ['gd_y2', 'ident'], [pvk])
        TS(S, 'dve', X['bv'][:, :], pv[0:64, 0:128], sc[0:64, c:c + 1], None, ALU.mult, ALU.bypass, [pvk, 'gd_sc'], [kx('bv')])
        yield
        gl, GL, GT = X['gl'], X['GL'], X['GT']
        TS(S, 'pool', gl[:, :], C.maskU[0:64, :], sc[0:64, 16 + c:17 + c], sc[0:64, 8 + c:9 + c], ALU.mult, ALU.add, ['maskU', 'gd_sc'], [kx('gl')])
        pd, pdk = C.pp()
        MM(S, pd[0:64, 0:64], C.maskU[0:64, :], gl[:, :], ['maskU', kx('gl')], [pdk])
        MM(S, pd[0:64, 64:128], gl[:, :], C.maskU[0:64, :], ['maskU', kx('gl')], [pdk])
        ACTF(S, GL[:, :], pd[0:64, 0:64], AF.Exp, [pdk], [kx('GL')])
        ACTF(S, GT[:, :], pd[0:64, 64:128], AF.Exp, [pdk], [kx('GT')])
        yield
        TT(S, 'pool', GL[:, :], GL[:, :], C.maskLs[0:64, :], ALU.mult, [kx('GL'), 'maskLs'], [kx('GL')])
        TT(S, 'pool', GT[:, :], GT[:, :], C.maskU[0:64, :], ALU.mult, [kx('GT'), 'maskU'], [kx('GT')])
        pg, pgk = C.pp()
        MM(S, pg[0:64, 0:64], kT[:, cs], kT[:, cs], ['gd_kT'], [pgk])
        MM(S, pg[0:64, 64:128], kT[:, cs], qT[:, cs], ['gd_kT', 'gd_qT'], [pgk])
        TT(S, 'dve', X['att'][:, :], pg[0:64, 64:128], GT[:, :], ALU.mult, [pgk, kx('GT')], [kx('att')])
        P, PT, TTf = X['P'], X['PT'], X['TT']
        i = 0
        STT(S, 'dve', P[i][:, :], pg[0:64, 0:64], sc[0:64, 56 + c:57 + c], GL[:, :], ALU.mult, ALU.mult, [pgk, 'gd_sc', kx('GL')], [kx('P0')])
        yield
        pp_, ppk = C.pp()
        TR(S, pp_[0:64, 0:64], P[i][:, :], C.ident[0:64, 0:64], [kx('P0'), 'ident'], [ppk])
        COPY(S, 'act', PT[i][:, :], pp_[0:64, 0:64], [ppk], [kx('PT0')])
        TT(S, 'dve', TTf[i][:, :], PT[i][:, :], C.ident[0:64, 0:64], ALU.add, [kx('PT0'), 'ident'], [kx('TT0')])
        yield
        for it in range(5):
            j = 1 - i
            pq, pqk = C.pp()
            MM(S, pq[0:64, 0:64], PT[i][:, :], P[i][:, :], [kx(f'PT{i}'), kx(f'P{i}')], [pqk])
            COPY(S, 'act', P[j][:, :], pq[0:64, 0:64], [pqk], [kx(f'P{j}')])
            if it < 4:
                pq2, pq2k = C.pp()
                MM(S, pq2[0:64, 0:64], P[i][:, :], PT[i][:, :], [kx(f'PT{i}'), kx(f'P{i}')], [pq2k])
                COPY(S, 'dve', PT[j][:, :], pq2[0:64, 0:64], [pq2k], [kx(f'PT{j}')])
            pu, puk = C.pp()
            MM(S, pu[0:64, 0:64], P[j][:, :], TTf[i][:, :], [kx(f'P{j}'), kx(f'TT{i}')], [puk])
            TT(S, 'dve', TTf[j][:, :], pu[0:64, 0:64], TTf[i][:, :], ALU.add, [puk, kx(f'TT{i}')], [kx(f'TT{j}')])
            i = j
            yield
        COPY(S, 'act', X['TTb'][:, :], TTf[i][:, :], [kx(f'TT{i}')], [kx('TTb')])
        pw, pwk = C.pp()
        MM(S, pw[0:64, 0:128], X['TTb'][:, :], X['kbe'][:, :], [kx('kbe'), kx('TTb')], [pwk])
        MM(S, pw[0:64, 128:256], X['TTb'][:, :], X['bv'][:, :], [kx('bv'), kx('TTb')], [pwk])
        COPY(S, 'act', X['wtok'][:, :], pw[0:64, 0:128], [pwk], [kx('wtok')])
        COPY(S, 'act', X['u'][:, :], pw[0:64, 128:256], [pwk], [kx('u')])
        yield
        pm, pmk = C.pp()
        MM(S, pm[:, 0:128], X['wtok'][:, :], X['kd'][:, :], [kx('wtok'), kx('kd')], [pmk])
        STT(S, 'dve', X['MT'][:, :], C.ident[:, :], sc[:, 48 + c:49 + c], pm[:, 0:128], ALU.mult, ALU.subtract, ['ident', 'gd_sc', pmk], [kx('MT')])
        pn_, pn_k = C.pp()
        MM(S, pn_[:, 0:128], X['kd'][:, :], X['u'][:, :], [kx('kd'), kx('u')], [pn_k])
        COPY(S, 'act', X['Nn'][:, :], pn_[:, 0:128], [pn_k], [kx('Nn')])
        yield
        pq_, pq_k = C.pp()
        MM(S, pq_[:, 0:64], X['wtok'][:, :], X['att'][:, :], [kx('wtok'), kx('att')], [pq_k])
        ACTF(S, X['WqT'][:, :], pq_[:, 0:64], AF.Copy, [pq_k], [kx('WqT')], scale=-1.0)

    def recur(self, blk, c):
        S, C = self.S, self.C
        qT, sc = self.qT, self.sc
        cs = slice(c * CH, (c + 1) * CH)
        X = self.pc[c]
        kx = lambda n: f'gdc{c}_{n}'
        if c % 4 == 0:
            self.ost_cur = self.ost.next()
        ost, osk = self.ost_cur
        Sb, Sbk = self.Sb.cur()
        St, Stk = self.Stt.cur()
        pS, pSk = C.pp()
        MM(S, pS[:, 0:128], X['MT'][:, :], St[:, :], [kx('MT'), Stk], [pSk])
        St2, St2k = self.Stt.next()
        TT(S, 'dve', St2[:, :], pS[:, 0:128], X['Nn'][:, :], ALU.add, [pSk, kx('Nn')], [St2k])
        po, pok = C.pp()
        MM(S, po[0:64, 0:128], qT[:, cs], Sb[:, :], ['gd_qT', Sbk], [pok])
        o1, o1k = self.o1.next()
        TS(S, 'dve', o1[:, :], po[0:64, 0:128], sc[0:64, 24 + c:25 + c], None, ALU.mult, ALU.bypass, [pok, 'gd_sc'], [o1k])
        po2, po2k = C.pp()
        MM(S, po2[0:64, 0:128], X['WqT'][:, :], Sb[:, :], [kx('WqT'), Sbk], [po2k], start=True, stop=False)
        MM(S, po2[0:64, 0:128], X['att'][:, :], X['u'][:, :], [kx('att'), kx('u')], [po2k], start=False, stop=True)
        TT(S, 'dve', ost[:, c % 4, :], po2[0:64, 0:128], o1[:, :], ALU.add, [po2k, o1k], [osk])
        Sb2, Sb2k = self.Sb.next()
        COPY(S, 'act', Sb2[:, :], St2[:, :], [St2k], [Sb2k])
        if c % 4 == 3:
            norm_out(S, C, 'gd_', ost[:, :, :].rearrange("p c e -> p (c e)"), osk, c // 4, blk, self.out, center=False)

HALF_PI = 1.5707963267948966


def cplx_unit(S, pfx, theta, c, s, tmp, shape_sl, keys_r, piT, keys=None):
    kc, ks, kt = keys if keys is not None else (pfx + 'c', pfx + 's', pfx + 't')
    ACTF(S, s, theta, AF.Sin, keys_r, [ks], scale=1.0 / 16)
    ACTF(S, c, theta, AF.Sin, keys_r + ['halfpi'], [kc], scale=1.0 / 16, bias=piT)
    for _ in range(4):
        TT(S, 'dve', tmp, c, s, ALU.mult, [kc, ks], [kt])
        TT(S, 'dve', c, c, c, ALU.mult, [kc], [kc])
        TT(S, 'dve', s, s, s, ALU.mult, [ks], [ks])
        TT(S, 'dve', c, c, s, ALU.subtract, [kc, ks], [kc])
        TS(S, 'dve', s, tmp, 2.0, None, ALU.mult, ALU.bypass, [kt], [ks])


class S5:
    def __init__(self, S, nc, C):
        self.S, self.C = S, C
        d = lambda n, s: nc.dram_tensor(n, s, F32, kind="ExternalInput").ap()
        self.uT = d("s5_uT", [128, T])
        self.pP = d("s5_pP", [128, 3, 4])
        self.pF = d("s5_pF", [128, 3, 512])
        self.bF = d("s5_bF", [128, 2, 512])
        self.cP = d("s5_cP", [128, 2, 512])
        self.out = nc.dram_tensor("s5_out", [128, T], F32, kind="ExternalOutput").ap()
        sb = S.sbuf
        self.halfpi = sb([128, 1], F32, "s5_halfpi")
        S.op('pool', lambda e: e.memset(self.halfpi[:], HALF_PI), writes=['halfpi'])
        pP = sb([128, 3, 4], F32, "s5_pPs"); pF = sb([128, 3, 512], F32, "s5_pFs"); bF = sb([128, 2, 512], F32, "s5_bFs"); cP = sb([128, 2, 512], F32, "s5_cPs")
        S.dma('sp', pP[:, :, :], self.pP[:, :, :], writes=['s5_pP']); S.dma('sp', pF[:, :, :], self.pF[:, :, :], writes=['s5_pF'])
        S.dma('sp', bF[:, :, :], self.bF[:, :, :], writes=['s5_bF']); S.dma('sp', cP[:, :, :], self.cP[:, :, :], writes=['s5_cP'])
        W = 512
        self.w = [sb([128, BW], F32, f"s5_w{i}") for i in range(4)]
        self.z = [sb([128, BW], F32, f"s5_z{i}") for i in range(2)]
        self.xf = [sb([128, BW], F32, f"s5_xf{i}") for i in range(2)]
        dtF, magF, thF, cF, sF, tF, t2F = self.w[0], self.w[1], self.w[2], self.w[3], self.z[0], self.z[1], self.xf[0]
        ACTF(S, dtF[:, :], pF[:, 2, :], AF.Exp, ['s5_pF'], ['s5_w0'])
        TT(S, 'dve', thF[:, :], dtF[:, :], pF[:, 1, :], ALU.mult, ['s5_w0', 's5_pF'], ['s5_w2'])
        TT(S, 'dve', magF[:, :], dtF[:, :], pF[:, 0, :], ALU.mult, ['s5_w0', 's5_pF'], ['s5_w1'])
        ACTF(S, magF[:, :], magF[:, :], AF.Exp, ['s5_w1'], ['s5_w1'])
        cplx_unit(S, 's5F_', thF[:, :], cF[:, :], sF[:, :], tF[:, :], None, ['s5_w2'], self.halfpi[:, 0:1], keys=('s5_w3', 's5_z0', 's5_z1'))
        kc, ks = 's5_w3', 's5_z0'
        TT(S, 'dve', cF[:, :], cF[:, :], magF[:, :], ALU.mult, [kc, 's5_w1'], [kc])
        TS(S, 'dve', cF[:, :], cF[:, :], -1.0, None, ALU.add, ALU.bypass, [kc], [kc])
        TT(S, 'dve', sF[:, :], sF[:, :], magF[:, :], ALU.mult, [ks, 's5_w1'], [ks])
        TT(S, 'dve', dtF[:, :], pF[:, 0, :], pF[:, 0, :], ALU.mult, ['s5_pF'], ['s5_w0'])
        TT(S, 'dve', tF[:, :], pF[:, 1, :], pF[:, 1, :], ALU.mult, ['s5_pF'], ['s5_z1'])
        TT(S, 'dve', dtF[:, :], dtF[:, :], tF[:, :], ALU.add, ['s5_w0', 's5_z1'], ['s5_w0'])
        S.op('dve', lambda e: e.reciprocal(out=dtF[:, :], in_=dtF[:, :]), reads=['s5_w0'], writes=['s5_w0'])
        TT(S, 'dve', tF[:, :], cF[:, :], pF[:, 0, :], ALU.mult, [kc, 's5_pF'], ['s5_z1'])
        TT(S, 'dve', t2F[:, :], sF[:, :], pF[:, 1, :], ALU.mult, [ks, 's5_pF'], ['s5_xf0'])
        TT(S, 'dve', thF[:, :], tF[:, :], t2F[:, :], ALU.add, ['s5_z1', 's5_xf0'], ['s5_w2'])
        TT(S, 'dve', thF[:, :], thF[:, :], dtF[:, :], ALU.mult, ['s5_w2', 's5_w0'], ['s5_w2'])
        TT(S, 'dve', tF[:, :], sF[:, :], pF[:, 0, :], ALU.mult, [ks, 's5_pF'], ['s5_z1'])
        TT(S, 'dve', t2F[:, :], cF[:, :], pF[:, 1, :], ALU.mult, [kc, 's5_pF'], ['s5_xf0'])
        TT(S, 'dve', magF[:, :], tF[:, :], t2F[:, :], ALU.subtract, ['s5_z1', 's5_xf0'], ['s5_w1'])
        TT(S, 'dve', magF[:, :], magF[:, :], dtF[:, :], ALU.mult, ['s5_w1', 's5_w0'], ['s5_w1'])
        self.Bre = sb([128, 512], BF16, "s5_Bre"); self.Bim = sb([128, 512], BF16, "s5_Bim")
        TT(S, 'dve', tF[:, :], thF[:, :], bF[:, 0, :], ALU.mult, ['s5_w2', 's5_bF'], ['s5_z1'])
        TT(S, 'dve', t2F[:, :], magF[:, :], bF[:, 1, :], ALU.mult, ['s5_w1', 's5_bF'], ['s5_xf0'])
        TT(S, 'dve', self.Bre[:, :], tF[:, :], t2F[:, :], ALU.subtract, ['s5_z1', 's5_xf0'], ['s5_Bre'])
        TT(S, 'dve', tF[:, :], thF[:, :], bF[:, 1, :], ALU.mult, ['s5_w2', 's5_bF'], ['s5_z1'])
        TT(S, 'dve', t2F[:, :], magF[:, :], bF[:, 0, :], ALU.mult, ['s5_w1', 's5_bF'], ['s5_xf0'])
        TT(S, 'dve', self.Bim[:, :], tF[:, :], t2F[:, :], ALU.add, ['s5_z1', 's5_xf0'], ['s5_Bim'])
        self.Cre = sb([128, 512], BF16, "s5_Cre"); self.Cim = sb([128, 512], BF16, "s5_Cim")
        COPY(S, 'act', self.Cre[:, :], cP[:, 0, :], ['s5_cP'], ['s5_Cre'])
        ACTF(S, self.Cim[:, :], cP[:, 1, :], AF.Copy, ['s5_cP'], ['s5_Cim'], scale=-1.0)
        dtP = sb([128, 4], F32, "s5_dtP"); self.r = sb([128, 4], F32, "s5_r"); thP = sb([128, 4], F32, "s5_thP")
        self.E = sb([128, 3, 4], F32, "s5_E")
        self.c1 = sb([128, 2, 4], F32, "s5_c1")
        tP = sb([128, 4], F32, "s5_tP")
        ACTF(S, dtP[:, :], pP[:, 2, :], AF.Exp, ['s5_pP'], ['s5_dtP'])
        TT(S, 'dve', thP[:, :], dtP[:, :], pP[:, 1, :], ALU.mult, ['s5_dtP', 's5_pP'], ['s5_thP'])
        TT(S, 'dve', self.r[:, :], dtP[:, :], pP[:, 0, :], ALU.mult, ['s5_dtP', 's5_pP'], ['s5_r'])
        ACTF(S, self.r[:, :], self.r[:, :], AF.Exp, ['s5_r'], ['s5_r'])
        cplx_unit(S, 's5P_', thP[:, :], self.E[:, 0, :], self.E[:, 1, :], tP[:, :], None, ['s5_thP'], self.halfpi[:, 0:1])
        COPY(S, 'dve', self.c1[:, 0, :], self.E[:, 0, :], ['s5P_c'], ['s5_c1'])
        COPY(S, 'dve', self.c1[:, 1, :], self.E[:, 1, :], ['s5P_s'], ['s5_c1'])
        SW = 256
        self.SW = SW
        self.cr = sb([128, 4, SW], F32, "s5_cr"); self.ci = sb([128, 4, SW], F32, "s5_ci"); self.rtab = sb([128, 4, SW], F32, "s5_rtab")
        tt = self.xf[1]
        S.op('pool', lambda e: e.memset(self.cr[:, :, 0:1], 1.0), writes=['s5_cr'])
        S.op('pool', lambda e: e.memset(self.ci[:, :, 0:1], 0.0), writes=['s5_ci'])
        S.op('pool', lambda e: e.memset(self.rtab[:, :, :], 1.0), writes=['s5_rtab'])
        for q in range(4):
            TS(S, 'pool', self.rtab[:, q, :], self.rtab[:, q, :], self.r[:, q:q + 1], None, ALU.mult, ALU.bypass, ['s5_rtab', 's5_r'], ['s5_rtab'])
        for k in range(8):
            w = 1 << k
            TS(S, 'dve', self.E[:, 2, :], self.E[:, 1, :], -1.0, None, ALU.mult, ALU.bypass, ['s5P_s'], ['s5_En'])
            for q in range(4):
                Ec, Es, En = self.E[:, 0, q:q + 1], self.E[:, 1, q:q + 1], self.E[:, 2, q:q + 1]
                TS(S, 'dve', tt[:, 0:w], self.cr[:, q, 0:w], Ec, None, ALU.mult, ALU.bypass, ['s5_cr', 's5P_c'], ['s5_xf1'])
                STT(S, 'dve', self.cr[:, q, w:2 * w], self.ci[:, q, 0:w], En, tt[:, 0:w], ALU.mult, ALU.add, ['s5_ci', 's5_En', 's5_xf1', 's5_cr'], ['s5_cr'])
                TS(S, 'dve', tt[:, 0:w], self.ci[:, q, 0:w], Ec, None, ALU.mult, ALU.bypass, ['s5_ci', 's5P_c'], ['s5_xf1'])
                STT(S, 'dve', self.ci[:, q, w:2 * w], self.cr[:, q, 0:w], Es, tt[:, 0:w], ALU.mult, ALU.add, ['s5_cr', 's5P_s', 's5_xf1', 's5_ci'], ['s5_ci'])
            TT(S, 'dve', tP[:, :], self.E[:, 0, :], self.E[:, 1, :], ALU.mult, ['s5P_c', 's5P_s'], ['s5P_t'])
            TT(S, 'dve', self.E[:, 0, :], self.E[:, 0, :], self.E[:, 0, :], ALU.mult, ['s5P_c'], ['s5P_c'])
            TT(S, 'dve', self.E[:, 1, :], self.E[:, 1, :], self.E[:, 1, :], ALU.mult, ['s5P_s'], ['s5P_s'])
            TT(S, 'dve', self.E[:, 0, :], self.E[:, 0, :], self.E[:, 1, :], ALU.subtract, ['s5P_c', 's5P_s'], ['s5P_c'])
            TS(S, 'dve', self.E[:, 1, :], tP[:, :], 2.0, None, ALU.mult, ALU.bypass, ['s5P_t'], ['s5P_s'])
        self.zi = sb([128, 2, 4], F32, "s5_zi")
        S.op('pool', lambda e: e.memset(self.zi[:, :, :], 0.0), writes=['s5_zi'])
        big = self.w + self.z + self.xf
        bigk = ['s5_w0', 's5_w1', 's5_w2', 's5_w3', 's5_z0', 's5_z1', 's5_xf0', 's5_xf1']
        halves = [(big[i][:, h * SW:(h + 1) * SW], bigk[i]) for i in range(8) for h in range(2)]
        extra = [sb([128, SW], F32, f"s5_ex{i}") for i in range(8)]
        views = halves + [(extra[i][:, :], f's5_ex{i}') for i in range(8)]
        self.pt = []
        allk = []
        for q in range(4):
            names = ['bre', 'bim', 'a', 'b', 'c', 'd']
            self.pt.append({n: (views[q * 6 + i][0], f's5q{q}_{n}') for i, n in enumerate(names)})
            allk += [f's5q{q}_{n}' for n in names]
        self.dummy = sb([128, 1], F32, "s5_dummy")
        S.op('pool', lambda e: e.memset(self.dummy[:, :], 0.0), reads=bigk, writes=allk + ['s5_dummy'])
        self.xbq = [[sb([128, SW], BF16, f"s5_xb{q}_{i}") for i in range(2)] for q in range(4)]
        self.ub = Rot(S, "s5_ub", [128, SW], BF16)
        self.yst = Rot(S, "s5_yst", [128, SW], F32, 1)
        self.xe = sb([128, 4, 4], F32, "s5_xe")

    def pair(self, q, ub, ubk, y_ps, yk):
        S, C = self.S, self.C
        SW = self.SW
        P = self.pt[q]
        (bre, brek), (bim, bimk), (a, ak), (b, bk), (c, ck), (d, dk) = P['bre'], P['bim'], P['a'], P['b'], P['c'], P['d']
        qs = slice(q * 128, (q + 1) * 128)
        cr, ci, rt = self.cr[:, q, :], self.ci[:, q, :], self.rtab[:, q, :]
        p1, p1k = C.pp(); p2, p2k = C.pp()
        MM(S, p1[:, 0:SW], self.Bre[:, qs], ub[:, :], ['s5_Bre', ubk], [p1k])
        MM(S, p2[:, 0:SW], self.Bim[:, qs], ub[:, :], ['s5_Bim', ubk], [p2k])
        COPY(S, 'act', bre, p1[:, 0:SW], [p1k], [brek])
        COPY(S, 'act', bim, p2[:, 0:SW], [p2k], [bimk])
        yield
        TTp(S, 'pool', a, bre, cr, ALU.mult, [brek, 's5_cr'], ak)
        TTp(S, 'pool', b, bim, ci, ALU.mult, [bimk, 's5_ci'], bk)
        TTp(S, 'pool', a, a, b, ALU.add, [ak, bk], ak)
        yield
        TTp(S, 'pool', c, bim, cr, ALU.mult, [bimk, 's5_cr'], ck)
        TTp(S, 'pool', b, bre, ci, ALU.mult, [brek, 's5_ci'], bk)
        TTp(S, 'pool', c, c, b, ALU.subtract, [ck, bk], ck)
        yield
        S.op('dve', lambda e: e.tensor_tensor_scan(out=d, data0=rt, data1=a, initial=self.zi[:, 0, q:q + 1], op0=ALU.mult, op1=ALU.add),
             reads=['s5_rtab', ak, 's5_zi'], writes=[dk])
        S.op('dve', lambda e: e.tensor_tensor_scan(out=a, data0=rt, data1=c, initial=self.zi[:, 1, q:q + 1], op0=ALU.mult, op1=ALU.add),
             reads=['s5_rtab', ck, 's5_zi'], writes=[ak])
        yield
        TTp(S, 'pool', b, d, cr, ALU.mult, [dk, 's5_cr'], bk)
        TTp(S, 'pool', c, a, ci, ALU.mult, [ak, 's5_ci'], ck)
        TTp(S, 'pool', b, b, c, ALU.subtract, [bk, ck], bk)
        yield
        TTp(S, 'pool', c, a, cr, ALU.mult, [ak, 's5_cr'], ck)
        TTp(S, 'pool', bre, d, ci, ALU.mult, [dk, 's5_ci'], brek)
        TTp(S, 'pool', c, c, bre, ALU.add, [ck, brek], ck)
        yield
        xb0, xb1 = self.xbq[q]
        COPY(S, 'act', xb0[:, :], b, [bk], [f's5_xb{q}_0'])
        COPY(S, 'act', xb1[:, :], c, [ck], [f's5_xb{q}_1'])
        c1, s1 = self.c1[:, 0, q:q + 1], self.c1[:, 1, q:q + 1]
        xe = self.xe[:, q, :]
        xk = f's5_xe{q}'
        TS(S, 'dve', xe[:, 0:1], b[:, SW - 1:SW], c1, None, ALU.mult, ALU.bypass, [bk, 's5_c1'], [xk])
        TS(S, 'dve', xe[:, 1:2], c[:, SW - 1:SW], s1, None, ALU.mult, ALU.bypass, [ck, 's5_c1'], [xk])
        TT(S, 'dve', self.zi[:, 0, q:q + 1], xe[:, 0:1], xe[:, 1:2], ALU.subtract, [xk], ['s5_zi'])
        TS(S, 'dve', xe[:, 2:3], c[:, SW - 1:SW], c1, None, ALU.mult, ALU.bypass, [ck, 's5_c1'], [xk])
        TS(S, 'dve', xe[:, 3:4], b[:, SW - 1:SW], s1, None, ALU.mult, ALU.bypass, [bk, 's5_c1'], [xk])
        TT(S, 'dve', self.zi[:, 1, q:q + 1], xe[:, 2:3], xe[:, 3:4], ALU.add, [xk], ['s5_zi'])
        MM(S, y_ps[:, 0:SW], self.Cre[:, qs], xb0[:, :], ['s5_Cre', f's5_xb{q}_0'], [yk], start=(q == 0), stop=False)
        MM(S, y_ps[:, 0:SW], self.Cim[:, qs], xb1[:, :], ['s5_Cim', f's5_xb{q}_1'], [yk], start=False, stop=(q == 3))

    def block(self, blk):
        S, C = self.S, self.C
        SW = self.SW
        for sub in range(BW // SW):
            t0 = blk * BW + sub * SW
            sl = slice(t0, t0 + SW)
            ub, ubk = self.ub.next()
            S.dma('pool', ub[:, :], self.uT[:, sl], writes=[ubk])
            y_ps, yk = C.ob_s5()
            gens = [self.pair(q, ub, ubk, y_ps, yk) for q in range(4)]
            alive = list(gens)
            while alive:
                for g in list(alive):
                    try:
                        next(g)
                    except StopIteration:
                        alive.remove(g)
                yield
            yst, ystk = self.yst.next()
            COPY(S, 'act', yst[:, :], y_ps[:, 0:SW], [yk], [ystk])
            S.dma('sp', self.out[:, sl], yst[:, :], reads=[ystk])
            yield


PW = 688
NPASS = 3
NTILE = 2
DFF = 2816
NHB = 22


def rmsnorm_pass(S, x_sb, g_sb, h_sb, ones, pp, sq, rstd, xkey, hkey, gkey):
    for tt in range(NTILE):
        sl = slice(tt * TW, (tt + 1) * TW)
        S.op('act', lambda e, sl=sl: e.activation(out=sq[:, :, :], in_=x_sb[:, :, sl], func=AF.Square), reads=[xkey], writes=['sq'])
        ps, pk = pp()
        for k in range(8):
            MM(S, ps[:, 0:TW], ones[:, :], sq[:, k, :], ['sq', 'ones'], [pk], start=(k == 0), stop=(k == 7))
        ACTF(S, rstd[:, :], ps[:, 0:TW], AF.Sqrt, [pk, 'epsb'], ['rstd'], scale=1.0 / 1024, bias=EPSB[0][:, 0:1])
        S.op('dve', lambda e: e.reciprocal(out=rstd[:, :], in_=rstd[:, :]), reads=['rstd'], writes=['rstd'])
        for k in range(8):
            STT(S, 'dve', h_sb[:, k, sl], x_sb[:, k, sl], g_sb[:, k:k + 1], rstd[:, :], ALU.mult, ALU.mult, [xkey, gkey, 'rstd'], [hkey])


def proj_pass(S, rhs_sb, rkey, w_dram, ncols, kchunks, wbufs, pp, evac, wname, colw=512):
    nblk = (ncols + colw - 1) // colw
    for cbk in range(nblk):
        c0 = cbk * colw
        w = min(colw, ncols - c0)
        i = proj_pass.cnt % len(wbufs)
        proj_pass.cnt += 1
        wt = wbufs[i][:, 0:kchunks * colw].rearrange("p (k c) -> p k c", c=colw)
        wk = f'wb{i}'
        S.dma('pool', wt[:, :, 0:w], w_dram[:, c0:c0 + w].rearrange("(k p) c -> p k c", p=128), writes=[wk])
        for sub in range(w // 128):
            for tt in range(NTILE):
                sl = slice(tt * TW, (tt + 1) * TW)
                ps, pk = pp()
                for k in range(kchunks):
                    MM(S, ps[:, 0:TW], wt[:, k, sub * 128:(sub + 1) * 128], rhs_sb[:, k, sl], [wk, rkey], [pk], start=(k == 0), stop=(k == kchunks - 1))
                evac(cbk * (colw // 128) + sub, tt, ps, pk)
proj_pass.cnt = 0


def build_C(final=False):
    nc = bass.Bass("TRN2", target_bir_lowering=False)
    din = lambda n, s: nc.dram_tensor(n, s, F32, kind="ExternalInput").ap()
    xT = din("xT", [1024, NT]); gm = din("gmix", [128, 8]); gf = din("gffn", [128, 8]); gfin = din("gfin", [128, 8])
    wG = din("wG", [1024, 1536])
    wGate = din("wGate", [1024, 4096])
    mixT = din("mixT", [4 * 512, NT])
    uT = din("uT", [512, NT])
    gains = din("gains", [128, 4, 4])
    wglu = din("wglu", [512, 512]); wbr = din("wbr", [4, 512, 1024]); wout = din("wout", [1024, 1024])
    wgu = din("wgu", [1024, 2 * DFF])
    wdn = din("wdn", [DFF, 1024])
    xo = nc.dram_tensor("xo", [1024, NT], F32, kind="ExternalOutput").ap()
    S = Sched(nc)
    sb = S.sbuf
    ones = load_consts_dense(S, nc)
    make_epsb(S)
    pp = PsPool(S, 8)
    x_sb = sb([128, 8, PW], F32, "x_sb"); h_sb = sb([128, 8, PW], BF16, "h_sb")
    g1 = sb([128, 8], F32, "g1"); g2 = sb([128, 8], F32, "g2"); g3 = sb([128, 8], F32, "g3"); gn = sb([128, 4, 4], F32, "gn")
    sq = sb([128, 8, TW], F32, "sq"); rstd = sb([128, TW], F32, "rstd")
    wbufs = [sb([128, 8 * 512], BF16, f"wbuf{i}") for i in range(3)]
    br = [sb([128, 4, PW], BF16, f"br{m}") for m in range(4)]
    yg = sb([128, 4, PW], F32, "yg"); ygb = sb([128, 4, PW], BF16, "ygb")
    acc = sb([128, 8, PW], F32, "acc"); mg = sb([128, 8, PW], BF16, "mg")
    hid = sb([128, NHB, PW], BF16, "hid")
    stg = [sb([128, PW], F32, f"stg{i}") for i in range(4)]
    tmp = [sb([128, TW], F32, f"tmp{i}") for i in range(4)]
    wbrs = [sb([128, 4, 1024], BF16, f"wbrs{i}") for i in range(1)]
    S.dma('sp', g1[:, :], gm[:, :], writes=['g1']); S.dma('sp', g2[:, :], gf[:, :], writes=['g2']); S.dma('sp', g3[:, :], gfin[:, :], writes=['g3'])
    S.dma('sp', gn[:, :, :], gains[:, :, :], writes=['gn'])
    cnt = [0]
    def stage(i):
        return stg[i % 4], f'stg{i % 4}'
    for ps_ in range(NPASS):
        t0 = ps_ * PW
        tsl = slice(t0, t0 + PW)
        for k in range(8):
            S.dma('sp', x_sb[:, k, :], xT[k * 128:(k + 1) * 128, tsl], pwrites=['x'])
        rmsnorm_pass(S, x_sb, g1, h_sb, ones, pp, sq, rstd, 'x', 'h', 'g1')
        def evac_g(cb, tt, ps, pk):
            m3, kb = cb // 4, cb % 4
            m = (0, 1, 3)[m3]
            sl = slice(tt * TW, (tt + 1) * TW)
            if tt == 0:
                st, sk = stage(cnt[0]); cnt[0] += 1
                evac_g.cur = (st, sk)
                S.dma('sp', st[:, :], mixT[m * 512 + kb * 128: m * 512 + (kb + 1) * 128, tsl], writes=[sk])
            st, sk = evac_g.cur
            tp, tk = tmp[cnt[0] % 2], f'tmp{cnt[0] % 2}'; cnt[0] += 1
            ACTF(S, tp[:, :], ps[:, 0:TW], AF.Silu, [pk], [tk])
            STT(S, 'dve', br[m][:, kb, sl], st[:, sl], gn[:, m, kb:kb + 1], tp[:, :], ALU.mult, ALU.mult, [sk, 'gn', tk], [f'br{m}'])
        proj_pass(S, h_sb, 'h', wG, 1536, 8, wbufs, pp, evac_g, 'w')
        for kb in range(4):
            st, sk = stage(cnt[0]); cnt[0] += 1
            st2, sk2 = stage(cnt[0]); cnt[0] += 1
            S.dma('sp', st[:, :], mixT[2 * 512 + kb * 128: 2 * 512 + (kb + 1) * 128, tsl], writes=[sk])
            S.dma('sp', st2[:, :], uT[kb * 128:(kb + 1) * 128, tsl], writes=[sk2])
            STT(S, 'dve', st[:, :], st2[:, :], gn[:, 2, kb:kb + 1], st[:, :], ALU.mult, ALU.add, [sk, sk2, 'gn'], [sk])
            TT(S, 'pool', st2[:, :], st[:, :], st[:, :], ALU.mult, [sk], [sk2])
            TS(S, 'pool', st2[:, :], st2[:, :], 0.044715, 1.0, ALU.mult, ALU.add, [sk2], [sk2])
            TT(S, 'pool', st2[:, :], st2[:, :], st[:, :], ALU.mult, [sk, sk2], [sk2])
            ACTF(S, st2[:, :], st2[:, :], AF.Sigmoid, [sk2], [sk2], scale=1.5957691216057308)
            TT(S, 'dve', yg[:, kb, :], st[:, :], st2[:, :], ALU.mult, [sk, sk2], ['yg'])
            COPY(S, 'act', ygb[:, kb, :], yg[:, kb, :], ['yg'], ['ygb'])
        def evac_glu(cb, tt, ps, pk):
            sl = slice(tt * TW, (tt + 1) * TW)
            tp, tk = tmp[cnt[0] % 2], f'tmp{cnt[0] % 2}'; cnt[0] += 1
            ACTF(S, tp[:, :], ps[:, 0:TW], AF.Sigmoid, [pk], [tk])
            TT(S, 'dve', br[2][:, cb, sl], yg[:, cb, sl], tp[:, :], ALU.mult, ['yg', tk], ['br2'])
        proj_pass(S, ygb, 'ygb', wglu, 512, 4, wbufs, pp, evac_glu, 'w')
        for m in range(4):
            wb, wbk = wbrs[0], 'wbrs0'
            S.dma('pool', wb[:, :, :], wbr[m].rearrange("(k p) c -> p k c", p=128), writes=[wbk])
            for half in range(2):
                i = proj_pass.cnt % 3; proj_pass.cnt += 1
                wt = wbufs[i][:, :].rearrange("p (k c) -> p k c", c=512); wk = f'wb{i}'
                c0 = m * 1024 + half * 512
                S.dma('pool', wt[:, :, :], wGate[:, c0:c0 + 512].rearrange("(k p) c -> p k c", p=128), writes=[wk])
                for sub in range(4):
                    ob = half * 4 + sub
                    for tt in range(NTILE):
                        sl = slice(tt * TW, (tt + 1) * TW)
                        pg, pgk = pp()
                        for k in range(8):
                            MM(S, pg[:, 0:TW], wt[:, k, sub * 128:(sub + 1) * 128], h_sb[:, k, sl], [wk, 'h'], [pgk], start=(k == 0), stop=(k == 7))
                        pb, pbk = pp()
                        for k in range(4):
                            MM(S, pb[:, 0:TW], wb[:, k, ob * 128:(ob + 1) * 128], br[m][:, k, sl], [wbk, f'br{m}'], [pbk], start=(k == 0), stop=(k == 3))
                        tp, tk = tmp[cnt[0] % 2], f'tmp{cnt[0] % 2}'; cnt[0] += 1
                        ACTF(S, tp[:, :], pg[:, 0:TW], AF.Sigmoid, [pgk], [tk])
                        if m == 0:
                            TT(S, 'dve', acc[:, ob, sl], pb[:, 0:TW], tp[:, :], ALU.mult, [pbk, tk], ['acc'])
                        else:
                            tp2, tk2 = tmp[2 + cnt[0] % 2], f'tmp{2 + cnt[0] % 2}'
                            TT(S, 'dve', tp2[:, :], pb[:, 0:TW], tp[:, :], ALU.mult, [pbk, tk], [tk2])
                            TT(S, 'pool', acc[:, ob, sl], acc[:, ob, sl], tp2[:, :], ALU.add, ['acc', tk2], ['acc'])
        for ob in range(8):
            COPY(S, 'act', mg[:, ob, :], acc[:, ob, :], ['acc'], ['mg'])
        def evac_res(cb, tt, ps, pk):
            sl = slice(tt * TW, (tt + 1) * TW)
            TT(S, 'dve', x_sb[:, cb, sl], ps[:, 0:TW], x_sb[:, cb, sl], ALU.add, [pk, 'x'], ['x'])
        proj_pass(S, mg, 'mg', wout, 1024, 8, wbufs, pp, evac_res, 'w')
        rmsnorm_pass(S, x_sb, g2, h_sb, ones, pp, sq, rstd, 'x', 'h', 'g2')
        def evac_gu(cb, tt, ps, pk):
            hb, isup = cb // 2, cb % 2
            sl = slice(tt * TW, (tt + 1) * TW)
            tp, tk = tmp[tt], f'tmp{tt}'
            if not isup:
                ACTF(S, tp[:, :], ps[:, 0:TW], AF.Silu, [pk], [tk])
            else:
                TT(S, 'dve', hid[:, hb, sl], ps[:, 0:TW], tp[:, :], ALU.mult, [pk, tk], ['hid'])
        proj_pass(S, h_sb, 'h', wgu, 2 * DFF, 8, wbufs, pp, evac_gu, 'w', colw=256)
        proj_pass(S, hid, 'hid', wdn, 1024, NHB, wbufs, pp, evac_res, 'w', colw=128)
        if final:
            rmsnorm_pass(S, x_sb, g3, acc, ones, pp, sq, rstd, 'x', 'acc', 'g3')
            for k in range(8):
                S.dma('sp', xo[k * 128:(k + 1) * 128, tsl], acc[:, k, :], reads=['acc'])
        else:
            for k in range(8):
                S.dma('sp', xo[k * 128:(k + 1) * 128, tsl], x_sb[:, k, :], reads=['x'])
    S.emit()
    return nc


def pad_T(a):
    out = np.zeros((T, a.shape[1]), np.float32); out[:a.shape[0]] = a; return out
def consts_B():
    s = np.arange(64)
    maskU = (s[:, None] <= s[None, :]).astype(np.float32)
    rmask = np.ones((128, 512), np.float32); rmask[:, ::64] = 0.0
    return {"maskU": maskU, "ident": np.eye(128, dtype=np.float32), "rmask": rmask}
def ret_tables(j):
    lg = np.log1p(-np.exp2(np.float32(-5.0 - j))).astype(np.float32)
    s = np.arange(64, dtype=np.float32)
    diff = s[None, :] - s[:, None]
    decT = np.where(diff >= 0, np.exp(lg * np.maximum(diff, 0)), 0).astype(np.float32)
    tm = (np.arange(512) % 64).astype(np.float32)
    xi = np.exp(lg * (tm + 1)).astype(np.float32); zeta = np.exp(lg * (63 - tm)).astype(np.float32)
    g64 = np.exp(lg * 64).astype(np.float32)
    tab = np.concatenate([decT, np.tile(xi[None], (64, 1)), np.tile(zeta[None], (64, 1)), np.full((64, 1), g64, np.float32)], 1)
    return np.ascontiguousarray(tab.astype(np.float32))
def rope_tables():
    pos = (np.arange(T) - 48).astype(np.float32)
    half = 32
    inv = (np.float32(10000.0) ** (-np.arange(half, dtype=np.float32) / half)).astype(np.float32)
    ang = pos[None, :] * inv[:, None]
    c = np.cos(ang).astype(np.float32); s_ = np.sin(ang).astype(np.float32)
    return np.concatenate([c, c], 0), np.concatenate([-s_, s_], 0)
def inputs_B_hg_rt(zb, j, lb_logits):
    m = {}
    m["hg_qT"] = np.ascontiguousarray(pad_T(zb[:, 64*j:64*j+64]).T)
    m["hg_fT"] = np.ascontiguousarray(pad_T(zb[:, 256+64*j:256+64*j+64]).T)
    m["hg_v"] = pad_T(zb[:, 512+128*j:512+128*j+128])
    m["hg_lg"] = np.ascontiguousarray(lb_logits[:, 64*j:64*j+64].T)
    perm = (np.arange(64) + 32) % 64
    q = pad_T(zb[:, 1536+64*j:1536+64*j+64]); k = pad_T(zb[:, 1792+64*j:1792+64*j+64])
    m["rt_qT"] = np.ascontiguousarray(q.T); m["rt_qpT"] = np.ascontiguousarray(q[:, perm].T)
    m["rt_kT"] = np.ascontiguousarray(k.T); m["rt_kpT"] = np.ascontiguousarray(k[:, perm].T)
    m["rt_v"] = pad_T(zb[:, 2048+128*j:2048+128*j+128])
    c, s_ = rope_tables(); m["rt_cos"] = c; m["rt_sin"] = s_
    m["rt_tab"] = ret_tables(j)
    return m

def inputs_B_gd(zb, j, conv_w, a_log, dt_bias):
    m = {}
    x = np.zeros((384, T + 3), np.float32)
    for mm in range(3):
        c0 = 3584 + 512 * mm + 128 * j
        x[mm*128:(mm+1)*128, 3:3+zb.shape[0]] = zb[:, c0:c0+128].T
    m["gd_x"] = x
    cw = np.zeros((128, 12), np.float32)
    for mm in range(3):
        cw[:, mm*4:(mm+1)*4] = conv_w[:, 512*mm + 128*j: 512*mm + 128*j + 128].T
    m["gd_cw"] = cw
    ba = np.zeros((64, 2, T // 64), np.float32)
    ba[:, 0, :] = pad_T(zb[:, 5120+j:5121+j])[:, 0].reshape(T // 64, 64).T
    ba[:, 1, :] = pad_T(zb[:, 5124+j:5125+j])[:, 0].reshape(T // 64, 64).T
    m["gd_ba"] = ba
    m["gd_par"] = np.tile(np.array([[a_log[j], dt_bias[j]]], np.float32), (64, 1))
    return m

def inputs_B_s5(zb, j, a_re, a_im, log_dt, b_re, b_im, c_re, c_im):
    m = {}
    m["s5_uT"] = np.ascontiguousarray(pad_T(zb[:, 3072 + 128 * j: 3072 + 128 * j + 128]).T)
    gs = np.arange(8 * j, 8 * j + 8)
    A = np.stack([a_re[gs], a_im[gs], np.tile(log_dt[gs][:, None], (1, 64))], 0)
    pP = A.reshape(3, 4, 2, 64).transpose(2, 3, 0, 1).reshape(128, 3, 4)
    pF = np.tile(A.reshape(3, 512)[None], (128, 1, 1))
    bF = np.zeros((128, 2, 8, 64), np.float32); cP = np.zeros((2, 64, 2, 4, 128), np.float32)
    for gl in range(8):
        g = gs[gl]
        bF[16 * gl:16 * gl + 16, 0, gl, :] = b_re[g].T
        bF[16 * gl:16 * gl + 16, 1, gl, :] = b_im[g].T
        q, g2 = gl // 2, gl % 2
        cP[g2, :, 0, q, 16 * gl:16 * gl + 16] = c_re[g].T
        cP[g2, :, 1, q, 16 * gl:16 * gl + 16] = c_im[g].T
    m["s5_pP"] = np.ascontiguousarray(pP.astype(np.float32)); m["s5_pF"] = np.ascontiguousarray(pF.astype(np.float32))
    m["s5_bF"] = np.ascontiguousarray(bF.reshape(128, 2, 512)); m["s5_cP"] = np.ascontiguousarray(cP.reshape(128, 2, 512))
    return m


A_COLS = np.r_[0:1024, 1536:2560, 3072:3584, 3584:5128]
NA_PAD = ((len(A_COLS) + 127) // 128) * 128
_CACHE = {}

def _prog(key, fn):
    if key not in _CACHE:
        _CACHE[key] = fn()
    return _CACHE[key]

def tok_rows(sg):
    return np.r_[48:64, 64 + sg * 2048: 64 + (sg + 1) * 2048]

def run_A(l, xTs, inp):
    w_in = np.asarray(inp["w_in"][l], np.float32)
    wA = np.zeros((1024, NA_PAD), np.float32); wA[:, :len(A_COLS)] = w_in[:, A_COLS]
    gm = np.ascontiguousarray(np.asarray(inp["norm_mix"][l], np.float32).reshape(8, 128).T)
    nc = _prog(('A',), lambda: build_A(NA_PAD))
    res = run_bass_kernel_spmd(nc, [{"xT": xTs[c], "gmix": gm, "wA": wA} for c in range(8)], core_ids=list(range(8)))
    z = np.zeros((2, 8256, 5640), np.float32)
    for c in range(8):
        b, sg = c // 4, c % 4
        zt = res.results[c]["zT"][:len(A_COLS)].T
        if sg == 0:
            z[b][48:64, A_COLS] = zt[:16]
        z[b][64 + sg * 2048: 64 + (sg + 1) * 2048, A_COLS] = zt[16:]
    return z

def run_B(l, z, inp):
    cst = consts_B()
    in_maps = []
    g = lambda k: np.asarray(inp[k][l], np.float32)
    for c in range(8):
        b, j = c // 4, c % 4
        m = inputs_B_hg_rt(z[b], j, np.asarray(inp["hg_lb_logits"], np.float32))
        m.update(inputs_B_gd(z[b], j, g('gdn_conv'), g('gdn_a_log'), g('gdn_dt_bias')))
        m.update(inputs_B_s5(z[b], j, g('ssm_a_re'), g('ssm_a_im'), g('ssm_log_dt'), g('ssm_b_re'), g('ssm_b_im'), g('ssm_c_re'), g('ssm_c_im')))
        m.update(cst)
        in_maps.append(m)
    nc = _prog(('B', l), lambda: build_B(l))
    res = run_bass_kernel_spmd(nc, in_maps, core_ids=list(range(8)))
    mix = np.zeros((2, 4, 8256, 512), np.float32)
    for c in range(8):
        b, j = c // 4, c % 4
        r = res.results[c]
        mix[b, 0][:, 128 * j:128 * j + 128] = r["hg_out"][:8256]
        mix[b, 1][:, 128 * j:128 * j + 128] = r["rt_out"][:8256]
        mix[b, 2][:, 128 * j:128 * j + 128] = r["s5_out"].T[:8256]
        mix[b, 3][:, 128 * j:128 * j + 128] = r["gd_out"][:8256]
    return mix

def run_C(l, xTs, z, mix, inp, final):
    g = lambda k: np.asarray(inp[k][l], np.float32)
    w_in = g("w_in")
    col = lambda v: np.ascontiguousarray(v.reshape(-1, 128).T)
    wG = np.ascontiguousarray(w_in[:, np.r_[1024:1536, 2560:3072, 5128:5640]])
    wGate = np.ascontiguousarray(w_in[:, 5640:9736])
    gains = np.stack([col(g("hg_norm")), col(g("ret_norm")), col(g("ssm_d")), col(g("gdn_norm"))], 1)
    w_gu = g("w_gu")
    order = np.concatenate([np.r_[hb * 128:(hb + 1) * 128, 2816 + hb * 128: 2816 + (hb + 1) * 128] for hb in range(22)])
    shared = {"gmix": col(g("norm_mix")), "gffn": col(g("norm_ffn")), "gfin": col(np.asarray(inp["norm_final"], np.float32)),
              "wG": wG, "wGate": wGate, "gains": np.ascontiguousarray(gains), "wglu": g("ssm_w_glu"), "wbr": g("w_branch"), "wout": g("w_out"),
              "wgu": np.ascontiguousarray(w_gu[:, order]), "wdn": g("w_down")}
    in_maps = []
    for c in range(8):
        b, sg = c // 4, c % 4
        rows = tok_rows(sg)
        m = dict(shared)
        m["xT"] = xTs[c]
        m["mixT"] = np.ascontiguousarray(mix[b][:, rows, :].transpose(0, 2, 1).reshape(2048, 2064))
        m["uT"] = np.ascontiguousarray(z[b][rows, 3072:3584].T)
        in_maps.append(m)
    nc = _prog(('C', final), lambda: build_C(final))
    res = run_bass_kernel_spmd(nc, in_maps, core_ids=list(range(8)))
    return [res.results[c]["xo"] for c in range(8)]

def initial_xT(inp):
    x = np.asarray(inp["x"], np.float32); meta = np.asarray(inp["meta"], np.float32)
    return [np.ascontiguousarray(np.concatenate([meta, x[c // 4, (c % 4) * 2048:(c % 4 + 1) * 2048]], 0).T) for c in range(8)]


def kernel(**inputs):
    inp = {k: np.asarray(v) for k, v in inputs.items()}
    xTs = initial_xT(inp)
    for l in range(4):
        z = run_A(l, xTs, inp)
        mix = run_B(l, z, inp)
        xTs = run_C(l, xTs, z, mix, inp, final=(l == 3))
    out = np.zeros((2, 8192, 1024), np.float32)
    for c in range(8):
        b, sg = c // 4, c % 4
        out[b, sg * 2048:(sg + 1) * 2048, :] = xTs[c][:, 16:].T
    return out
```
